# Optimizing a Trainium2 kernel written in Bass

```python
import jax, jax.numpy as jnp
from jax import lax
import numpy as np

D_MODEL = 1024
BATCH = 8
SEQ = 2048
DEPTH = 2

CHUNK = 64
N_EVEN = (DEPTH + 1) // 2
N_ODD = DEPTH // 2
GROUP_WIDTH = D_MODEL // 2
D_FF = 4 * D_MODEL
NORM_EPS = 1e-6

GLA_HEADS = 4
GLA_DV = GROUP_WIDTH // GLA_HEADS
GLA_DK = GLA_DV // 2
GLA_KW = GLA_HEADS * GLA_DK
GLA_RANK = 16
GLA_GATE_TAU = 16.0

FOX_HEAD_DIM = 64
FOX_HEADS = GROUP_WIDTH // FOX_HEAD_DIM
FOX_BLOCK = 128

CA_HEAD_DIM = 64
CA_HEADS = GROUP_WIDTH // CA_HEAD_DIM
CA_LEFT_CHUNKS = 8
CA_BAND = (CA_LEFT_CHUNKS + 1) * CHUNK
REL_CLIP = 128

LRU_WIDTH = GROUP_WIDTH
LRU_BLOCKS = 8
LRU_BLOCK_DIM = LRU_WIDTH // LRU_BLOCKS
CONV_WIDTH = 4
LRU_C = 8.0

EVEN_SIZES = (GLA_KW, GLA_KW, GROUP_WIDTH, GROUP_WIDTH, GLA_RANK,
              GROUP_WIDTH, GROUP_WIDTH, GROUP_WIDTH, FOX_HEADS)
EVEN_IN = sum(EVEN_SIZES)
ODD_SIZES = (GROUP_WIDTH, GROUP_WIDTH, GROUP_WIDTH, LRU_WIDTH, LRU_WIDTH)
ODD_IN = sum(ODD_SIZES)

kernel_name = 'hybrid_gla_fox_chunkattn_rglru_trunk'


def _split(a, sizes):
    return jnp.split(a, [int(s) for s in np.cumsum(sizes)[:-1]], axis=-1)


def rmsnorm(x, w):
    xf = x.astype(jnp.float32)
    y = xf * lax.rsqrt(jnp.mean(xf * xf, axis=-1, keepdims=True) + NORM_EPS)
    return (y * w.astype(jnp.float32)).astype(x.dtype)


def gla_mixer(q, k, v, r, a_low, w_a_up, b_a, norm_w):
    B, T, _ = q.shape
    nc = T // CHUNK
    f32 = jnp.float32
    qc = q.reshape(B, nc, CHUNK, GLA_HEADS, GLA_DK).astype(f32) * (GLA_DK ** -0.5)
    kc = k.reshape(B, nc, CHUNK, GLA_HEADS, GLA_DK).astype(f32)
    vc = v.reshape(B, nc, CHUNK, GLA_HEADS, GLA_DV).astype(f32)
    log_a = jax.nn.log_sigmoid((a_low @ w_a_up + b_a).astype(f32)) / GLA_GATE_TAU
    log_a = log_a.reshape(B, nc, CHUNK, GLA_HEADS, GLA_DK)
    cum = jnp.cumsum(log_a, axis=2)
    total = cum[:, :, -1]
    k_dec = kc * jnp.exp(total[:, :, None] - cum)
    inc = jnp.einsum('bcshk,bcshv->bchkv', k_dec, vc)

    def step(state, inp):
        dec, add = inp
        state = dec[..., None] * state + add
        return state, state

    init = jnp.zeros((B, GLA_HEADS, GLA_DK, GLA_DV), f32)
    _, states = lax.scan(step, init, (jnp.moveaxis(jnp.exp(total), 1, 0), jnp.moveaxis(inc, 1, 0)))
    states = jnp.moveaxis(states, 0, 1)
    o = jnp.einsum('bcthk,bchkv->bcthv', qc, states).reshape(B, T, GLA_HEADS, GLA_DV)
    o = o * lax.rsqrt(jnp.mean(o * o, axis=-1, keepdims=True) + NORM_EPS)
    o = o.reshape(B, T, GROUP_WIDTH) * norm_w.astype(f32)
    return (o * jax.nn.silu(r.astype(f32))).astype(q.dtype)


def fox_mixer(q, k, v, f_logit):
    B, T, _ = q.shape
    f32 = jnp.float32
    qh = q.reshape(B, T, FOX_HEADS, FOX_HEAD_DIM)
    kh = k.reshape(B, T, FOX_HEADS, FOX_HEAD_DIM)
    vh = v.reshape(B, T, FOX_HEADS, FOX_HEAD_DIM)
    cum = jnp.cumsum(jax.nn.log_sigmoid(f_logit.astype(f32)), axis=1).transpose(0, 2, 1)
    scale = FOX_HEAD_DIM ** -0.5
    neg = jnp.finfo(f32).min
    outs = []
    for blk in range(T // FOX_BLOCK):
        q0 = blk * FOX_BLOCK
        q1 = q0 + FOX_BLOCK
        s = jnp.einsum('bqhd,bkhd->bhqk', qh[:, q0:q1], kh[:, :q1]).astype(f32) * scale
        s = s + (cum[:, :, q0:q1, None] - cum[:, :, None, :q1])
        mask = (q0 + jnp.arange(FOX_BLOCK))[:, None] >= jnp.arange(q1)[None, :]
        p = jax.nn.softmax(jnp.where(mask, s, neg), axis=-1)
        outs.append(jnp.einsum('bhqk,bkhd->bqhd', p.astype(v.dtype), vh[:, :q1]))
    return jnp.concatenate(outs, axis=1).reshape(B, T, GROUP_WIDTH)


def chunk_rel_attention(q, k, v, rel_bias):
    B, T, _ = q.shape
    nc = T // CHUNK
    f32 = jnp.float32
    pad = CA_LEFT_CHUNKS * CHUNK
    qc = q.reshape(B, nc, CHUNK, CA_HEADS, CA_HEAD_DIM)
    kp = jnp.pad(k, ((0, 0), (pad, 0), (0, 0))).reshape(B, nc + CA_LEFT_CHUNKS, CHUNK, CA_HEADS, CA_HEAD_DIM)
    vp = jnp.pad(v, ((0, 0), (pad, 0), (0, 0))).reshape(B, nc + CA_LEFT_CHUNKS, CHUNK, CA_HEADS, CA_HEAD_DIM)
    k_band = jnp.concatenate([kp[:, j:j + nc] for j in range(CA_LEFT_CHUNKS + 1)], axis=2)
    v_band = jnp.concatenate([vp[:, j:j + nc] for j in range(CA_LEFT_CHUNKS + 1)], axis=2)
    s = jnp.einsum('bcqhd,bckhd->bchqk', qc, k_band).astype(f32) * (CA_HEAD_DIM ** -0.5)
    qi = jnp.arange(CHUNK)
    kj = jnp.arange(CA_BAND)
    rel = jnp.clip(pad + qi[:, None] - kj[None, :], -REL_CLIP, REL_CLIP) + REL_CLIP
    bias = rel_bias.astype(f32)[:, rel]
    key_pos = jnp.arange(nc)[:, None] * CHUNK - pad + kj[None, :]
    valid = (key_pos >= 0)[None, :, None, None, :]
    s = jnp.where(valid, s + bias[None, None], jnp.finfo(f32).min)
    p = jax.nn.softmax(s, axis=-1)
    o = jnp.einsum('bchqk,bckhd->bcqhd', p.astype(v.dtype), v_band)
    return o.reshape(B, T, GROUP_WIDTH)


def rglru_mixer(gate_in, x_in, conv_w, conv_b, w_a, b_a, w_x, b_x, lam):
    B, T, W = x_in.shape
    f32 = jnp.float32
    xc = lax.conv_general_dilated(x_in, conv_w[:, None, :], window_strides=(1,),
                                  padding=[(CONV_WIDTH - 1, 0)],
                                  dimension_numbers=('NWC', 'WIO', 'NWC'),
                                  feature_group_count=W) + conv_b
    xb = xc.reshape(B, T, LRU_BLOCKS, LRU_BLOCK_DIM)
    r = jax.nn.sigmoid((jnp.einsum('btnd,nde->btne', xb, w_a).reshape(B, T, W) + b_a).astype(f32))
    i = jax.nn.sigmoid((jnp.einsum('btnd,nde->btne', xb, w_x).reshape(B, T, W) + b_x).astype(f32))
    log_a = LRU_C * r * jax.nn.log_sigmoid(lam.astype(f32))
    a = jnp.exp(log_a)
    b = jnp.sqrt(-jnp.expm1(2.0 * log_a)) * (i * xc.astype(f32))

    def combine(left, right):
        a_l, b_l = left
        a_r, b_r = right
        return a_l * a_r, a_r * b_l + b_r

    _, h = lax.associative_scan(combine, (a, b), axis=1)
    return (h * jax.nn.gelu(gate_in.astype(f32))).astype(x_in.dtype)


def sq_relu_mlp(h, w_up, w_down):
    return jnp.square(jax.nn.relu(h @ w_up)) @ w_down


def setup_inputs(seed: int = 0) -> dict:
    key = jax.random.key(seed)
    ks = jax.random.split(key, 24)
    nrm = jax.random.normal
    f32 = jnp.float32
    x = nrm(ks[0], (BATCH, SEQ, D_MODEL), f32)
    norm_w = 1.0 + 0.05 * nrm(ks[1], (DEPTH, 4, D_MODEL), f32)
    w_in_even = nrm(ks[2], (N_EVEN, D_MODEL, EVEN_IN), f32) * D_MODEL ** -0.5
    gla_w_a_up = nrm(ks[3], (N_EVEN, GLA_RANK, GLA_KW), f32) * GLA_RANK ** -0.5
    gla_b_a = jax.random.uniform(ks[4], (N_EVEN, GLA_KW), f32, 0.0, 2.0)
    gla_norm_w = 1.0 + 0.05 * nrm(ks[5], (N_EVEN, GROUP_WIDTH), f32)
    fox_b_f = jax.random.uniform(ks[6], (N_EVEN, FOX_HEADS), f32, 1.0, 4.0)
    w_out_even = nrm(ks[7], (N_EVEN, 2 * GROUP_WIDTH, D_MODEL), f32) * (2 * GROUP_WIDTH) ** -0.5
    w_in_odd = nrm(ks[8], (N_ODD, D_MODEL, ODD_IN), f32) * D_MODEL ** -0.5
    rel_bias = 0.5 * nrm(ks[9], (N_ODD, CA_HEADS, 2 * REL_CLIP + 1), f32)
    conv_w = nrm(ks[10], (N_ODD, CONV_WIDTH, LRU_WIDTH), f32) * CONV_WIDTH ** -0.5
    conv_b = 0.01 * nrm(ks[11], (N_ODD, LRU_WIDTH), f32)
    lru_w_a = nrm(ks[12], (N_ODD, LRU_BLOCKS, LRU_BLOCK_DIM, LRU_BLOCK_DIM), f32) * LRU_BLOCK_DIM ** -0.5
    lru_b_a = 0.1 * nrm(ks[13], (N_ODD, LRU_WIDTH), f32)
    lru_w_x = nrm(ks[14], (N_ODD, LRU_BLOCKS, LRU_BLOCK_DIM, LRU_BLOCK_DIM), f32) * LRU_BLOCK_DIM ** -0.5
    lru_b_x = 0.1 * nrm(ks[15], (N_ODD, LRU_WIDTH), f32)
    a_c = jax.random.uniform(ks[16], (N_ODD, LRU_WIDTH), f32, 0.9, 0.999)
    a_base = a_c ** (1.0 / LRU_C)
    lru_lambda = jnp.log(a_base) - jnp.log1p(-a_base)
    w_out_odd = nrm(ks[17], (N_ODD, 2 * GROUP_WIDTH, D_MODEL), f32) * (2 * GROUP_WIDTH) ** -0.5
    w_mlp_up = nrm(ks[18], (DEPTH, D_MODEL, D_FF), f32) * D_MODEL ** -0.5
    w_mlp_down = nrm(ks[19], (DEPTH, D_FF, D_MODEL), f32) * D_FF ** -0.5
    return {'x': x, 'norm_w': norm_w, 'w_in_even': w_in_even, 'gla_w_a_up': gla_w_a_up,
            'gla_b_a': gla_b_a, 'gla_norm_w': gla_norm_w, 'fox_b_f': fox_b_f,
            'w_out_even': w_out_even, 'w_in_odd': w_in_odd, 'rel_bias': rel_bias,
            'conv_w': conv_w, 'conv_b': conv_b, 'lru_w_a': lru_w_a, 'lru_b_a': lru_b_a,
            'lru_w_x': lru_w_x, 'lru_b_x': lru_b_x, 'lru_lambda': lru_lambda,
            'w_out_odd': w_out_odd, 'w_mlp_up': w_mlp_up, 'w_mlp_down': w_mlp_down}


def reference(x, norm_w, w_in_even, gla_w_a_up, gla_b_a, gla_norm_w, fox_b_f, w_out_even,
              w_in_odd, rel_bias, conv_w, conv_b, lru_w_a, lru_b_a, lru_w_x, lru_b_x,
              lru_lambda, w_out_odd, w_mlp_up, w_mlp_down):
    for layer in range(DEPTH):
        j = layer // 2
        h = rmsnorm(x, norm_w[layer, 0])
        if layer % 2 == 0:
            proj = h @ w_in_even[j]
            g_q, g_k, g_v, g_r, g_a, f_q, f_k, f_v, f_f = _split(proj, EVEN_SIZES)
            out_a = gla_mixer(g_q, g_k, g_v, g_r, g_a, gla_w_a_up[j], gla_b_a[j], gla_norm_w[j])
            out_b = fox_mixer(f_q, f_k, f_v, f_f + fox_b_f[j])
            mix = jnp.concatenate([out_a, out_b], axis=-1) @ w_out_even[j]
        else:
            proj = h @ w_in_odd[j]
            c_q, c_k, c_v, d_gate, d_in = _split(proj, ODD_SIZES)
            out_c = chunk_rel_attention(c_q, c_k, c_v, rel_bias[j])
            out_d = rglru_mixer(d_gate, d_in, conv_w[j], conv_b[j], lru_w_a[j], lru_b_a[j],
                                lru_w_x[j], lru_b_x[j], lru_lambda[j])
            mix = jnp.concatenate([out_c, out_d], axis=-1) @ w_out_odd[j]
        x = x + rmsnorm(mix, norm_w[layer, 1])
        h = rmsnorm(x, norm_w[layer, 2])
        x = x + rmsnorm(sq_relu_mlp(h, w_mlp_up[layer], w_mlp_down[layer]), norm_w[layer, 3])
    return x
```

```python
from contextlib import ExitStack
import numpy as np
import concourse.bass as bass
import concourse.mybir as mybir
from concourse.bass_utils import run_bass_kernel_spmd

F32 = mybir.dt.float32
BF16 = mybir.dt.bfloat16
AF = mybir.ActivationFunctionType
ALU = mybir.AluOpType

T = 2048
D = 1024
NB = 4
NT = 16
EPS = 1e-6
NEG = -30000.0


import types


def _freeze(fn):
    if fn.__closure__ is None:
        return fn
    cells = []
    for c in fn.__closure__:
        try:
            cells.append(types.CellType(c.cell_contents))
        except ValueError:
            cells.append(c)
    return types.FunctionType(fn.__code__, fn.__globals__, fn.__name__, fn.__defaults__, tuple(cells))


class Res:
    __slots__ = ("name", "lw", "rs")

    def __init__(self, name="r"):
        self.name = name
        self.lw = None
        self.rs = []


class Op:
    __slots__ = ("eng", "fn", "deps", "signal", "tick", "dma", "dsem", "dval", "prev_dval")

    def __init__(self, eng, fn, dma):
        self.eng = eng
        self.fn = fn
        self.dma = dma
        self.deps = []
        self.signal = False
        self.tick = 0
        self.dsem = None
        self.dval = 0
        self.prev_dval = 0


class Sched:
    ENGS = ("pe", "act", "dve", "pool", "sp")
    NDSEM = {"sp": 16, "pool": 4, "act": 8}

    def __init__(self, nc):
        self.nc = nc
        self.q = {e: [] for e in self.ENGS}
        self.dcount = {e: 0 for e in self.NDSEM}

    def add(self, eng, fn, reads=(), writes=(), dma=False):
        op = Op(eng, _freeze(fn), dma)
        deps = []
        for r in reads:
            if r.lw is not None:
                deps.append(r.lw)
        for w in writes:
            if w.lw is not None:
                deps.append(w.lw)
            deps.extend(w.rs)
        seen = set()
        for d in deps:
            if d is op or id(d) in seen:
                continue
            seen.add(id(d))
            if (not d.dma) and (not dma) and d.eng == "pe" and eng == "pe":
                continue
            op.deps.append(d)
        for r in reads:
            r.rs.append(op)
        for w in writes:
            w.lw = op
            w.rs = []
        if dma:
            i = self.dcount[eng]
            self.dcount[eng] += 1
            op.dsem = (eng, i % self.NDSEM[eng])
        self.q[eng].append(op)
        return op

    def pe(self, fn, r=(), w=()):
        return self.add("pe", fn, r, w)

    def act(self, fn, r=(), w=()):
        return self.add("act", fn, r, w)

    def dve(self, fn, r=(), w=()):
        return self.add("dve", fn, r, w)

    def pool(self, fn, r=(), w=()):
        return self.add("dve", fn, r, w)

    def dma(self, fn, r=(), w=(), q="sp"):
        return self.add(q, fn, r, w, dma=True)

    def emit(self):
        nc = self.nc
        for e in self.ENGS:
            for op in self.q[e]:
                for d in op.deps:
                    if not d.dma:
                        d.signal = True
        for e in self.ENGS:
            t = 0
            for op in self.q[e]:
                if op.dma:
                    continue
                if op.signal:
                    t += 1
                    op.tick = t
        dvals = {}
        for e in self.NDSEM:
            k = 0
            for op in self.q[e]:
                if op.dma:
                    op.dsem = (e, k % self.NDSEM[e])
                    k += 1
                    v = dvals.get(op.dsem, 0)
                    op.prev_dval = v
                    op.dval = v + 16
                    dvals[op.dsem] = op.dval
        with ExitStack() as st:
            esem = {e: st.enter_context(nc.semaphore("s_" + e)) for e in ("pe", "act", "dve", "pool")}
            dsem = {}
            for e, n in self.NDSEM.items():
                for i in range(min(n, self.dcount[e])):
                    dsem[(e, i)] = st.enter_context(nc.semaphore("d_%s%d" % (e, i)))
            block = st.enter_context(nc.Block())
            q = self.q

            def run(ename, eng):
                waited = {}

                def wait(key, sem, val):
                    if waited.get(key, 0) >= val:
                        return
                    waited[key] = val
                    eng.wait_ge(sem, val)

                for op in q[ename]:
                    need = {}
                    for d in op.deps:
                        if d.dma:
                            k = ("d",) + d.dsem
                            need[k] = max(need.get(k, 0), d.dval)
                        else:
                            k = ("e", d.eng)
                            need[k] = max(need.get(k, 0), d.tick)
                    for k, v in need.items():
                        if k[0] == "d":
                            wait(k, dsem[(k[1], k[2])], v)
                        else:
                            wait(k, esem[k[1]], v)
                    if op.dma:
                        if op.prev_dval:
                            wait(("d",) + op.dsem, dsem[op.dsem], op.prev_dval)
                        op.fn(eng).then_inc(dsem[op.dsem], 16)
                    else:
                        ins = op.fn(eng)
                        if op.signal:
                            ins.then_inc(esem[ename], 1)
                if ename in self.NDSEM:
                    last = {}
                    for op in q[ename]:
                        if op.dma:
                            last[op.dsem] = op.dval
                    for k, v in last.items():
                        wait(("d",) + k, dsem[k], v)

            if q["pe"]:
                block.tensor(lambda e: run("pe", e))
            if q["act"]:
                block.scalar(lambda e: run("act", e))
            if q["dve"]:
                block.vector(lambda e: run("dve", e))
            if q["pool"]:
                block.gpsimd(lambda e: run("pool", e))
            if q["sp"]:
                block.sync(lambda e: run("sp", e))


ARENA_BYTES = 42 * 1024


def build_program(stop_after=99):
    nc = bass.Bass("TRN2", target_bir_lowering=False)
    dt_in = {}

    def din(name, shape):
        dt_in[name] = nc.dram_tensor(name, list(shape), F32, kind="ExternalInput").ap()
        return dt_in[name]

    x_d = din("x", [T, D])
    nw_d = din("nw", [128, 64])
    consts_d = din("consts", [128, 520])
    w_in0_d = din("w_in0", [D, 3096])
    wa_aug_d = din("wa_aug", [17, 256])
    gnw_d = din("gnw", [128, 4])
    fbf_d = din("fbf", [8, 1])
    w_out0_d = din("w_out0", [D, D])
    w_in1_d = din("w_in1", [D, 2560])
    rel34_d = din("rel34", [128, 8 * 2 * 128])
    cvec_d = din("cvec", [128, 8])
    cw_d = din("cw", [128, 16])
    cb_d = din("cb", [128, 4])
    lwa_d = din("lwa", [8, 64, 64])
    lwx_d = din("lwx", [8, 64, 64])
    lba_d = din("lba", [128, 4])
    lbx_d = din("lbx", [128, 4])
    lam_d = din("lam", [128, 4])
    w_out1_d = din("w_out1", [D, D])
    w_up_d = din("w_up", [2, D, 4 * D])
    w_dn_d = din("w_dn", [2, 4 * D, D])
    out_d = nc.dram_tensor("out", [T, D], F32, kind="ExternalOutput").ap()
    wsb = nc.dram_tensor("wsb", [72, 128, 4096], BF16, kind="Internal").ap()

    S = Sched(nc)
    with ExitStack() as st:
        def sb(name, shape, dt):
            return st.enter_context(nc.sbuf_tensor(name, list(shape), dt))

        xT = sb("xT", [128, 8, T], F32)
        hT = sb("hT", [128, 8, T], BF16)
        mixT = sb("mixT", [128, 8, T], BF16)
        wbuf = [sb("wbuf%d" % i, [128, 8, 512], BF16) for i in range(3)]
        nw = sb("nw_sb", [128, 64], F32)
        cf = sb("cf", [128, 128], F32)
        cb16 = sb("cb16", [128, 520], BF16)
        ftmp = [sb("ftmp%d" % i, [128, 512], F32) for i in range(2)]
        dummy = sb("dummyt", [128, 8], F32)
        rstd = [sb("rstd%d" % i, [128, 512], F32) for i in range(2)]
        sqt = [sb("sqt%d" % i, [128, 512], BF16) for i in range(2)]
        ncum_p = sb("ncum_p", [128, NT * 8], F32)
        fbf_p = sb("fbf_p", [8, 1], F32)
        arena = sb("arena", [128, ARENA_BYTES // 2], BF16)
        banks = [st.enter_context(nc.psum_tensor("bank%d" % i, [128, 512], F32)) for i in range(8)]

        ident_f = cf[:, 0:128]
        R_ftmp = [Res(), Res()]
        ident_b = cb16[:, 0:128]
        ones_b = cb16[:, 128:256]
        tri_b = cb16[:, 256:384]
        trimask_b = cb16[:, 384:512]
        ind_b = cb16[:, 512:514]

        R_xT = [[Res() for _ in range(NB)] for _ in range(8)]
        R_hT = [[Res() for _ in range(NB)] for _ in range(8)]
        R_mix = [[Res() for _ in range(NB)] for _ in range(8)]
        R_w = [Res() for _ in range(3)]
        R_nw, R_cf, R_cb = Res(), Res(), Res()
        R_rstd = [Res(), Res()]
        R_sq = [Res(), Res()]
        R_bank = [Res() for _ in range(8)]
        R_dummy = Res()
        wcount = [0]

        arena_res = []

        class Carver:
            def __init__(self):
                self.off = 0

            def take(self, shape_free, dt, parts=128):
                n = 1
                for s in shape_free:
                    n *= s
                nbytes = n * (4 if dt == F32 else 2)
                nbytes = (nbytes + 31) // 32 * 32
                assert self.off + nbytes <= ARENA_BYTES, (self.off, nbytes)
                v = arena[0:parts, self.off // 2:(self.off + nbytes) // 2]
                if dt == F32:
                    v = v.bitcast(F32)
                v = v[:, 0:n]
                if len(shape_free) == 2:
                    v = v.rearrange("p (a b) -> p a b", a=shape_free[0])
                elif len(shape_free) == 3:
                    v = v.rearrange("p (a b c) -> p a b c", a=shape_free[0], b=shape_free[1])
                elif len(shape_free) == 4:
                    v = v.rearrange("p (a b c d) -> p a b c d", a=shape_free[0], b=shape_free[1], c=shape_free[2])
                self.off += nbytes
                return v

        def ares(n=1):
            rs = [Res() for _ in range(n)]
            arena_res.extend(rs)
            return rs if n > 1 else rs[0]

        def phase_switch():
            old = list(arena_res)
            del arena_res[:]
            j = S.pool(lambda e: e.memset(dummy[:, :], 0.0), w=old + [R_dummy])
            return j

        def new_phase():
            j = phase_switch()
            return Carver(), j

        def seed(rs, j):
            for r in (rs if isinstance(rs, (list, tuple)) else [rs]):
                r.lw = j

        scr = {}
        conv_ops = []

        def wconv(src_ap, kc, ncols):
            key = repr(src_ap)
            if key in scr:
                return scr[key]
            idx = len(scr)
            R = Res()
            src = src_ap.rearrange("(k p) n -> p k n", p=128)
            dst = wsb[idx][:, 0:kc * ncols].rearrange("p (k n) -> p k n", k=kc)
            op = S.dma(lambda e: e.dma_start(out=dst, in_=src), w=[R], q="pool")
            S.q["pool"].remove(op)
            conv_ops.append(op)
            scr[key] = (idx, R)
            return scr[key]

        def wload(src_ap, kc, ncols, dst=None, R_dst=None):
            idx, R = wconv(src_ap, kc, ncols)
            slot = None
            if dst is None:
                slot = wcount[0] % 3
                wcount[0] += 1
                dst = wbuf[slot][:, 0:kc, 0:ncols]
                R_dst = R_w[slot]
            src = wsb[idx][:, 0:kc * ncols].rearrange("p (k n) -> p k n", k=kc)
            S.dma(lambda e: e.dma_start(out=dst, in_=src), r=[R], w=[R_dst], q="sp")
            return slot

        def mlp_tile_src(l, idx):
            if idx < 8:
                return w_up_d[l, :, idx * 512:(idx + 1) * 512]
            mg, fg = (idx - 8) // 4, (idx - 8) % 4
            return w_dn_d[l, fg * 1024:(fg + 1) * 1024, mg * 512:(mg + 1) * 512]

        def wload_bf(l, idx):
            return wload(mlp_tile_src(l, idx), 8, 512)

        S.dma(lambda e: e.dma_start(out=nw[:], in_=nw_d[:, :]), w=[R_nw])
        S.dma(lambda e: e.dma_start(out=cf[:], in_=consts_d[:, 0:128]), w=[R_cf])
        S.dma(lambda e: e.dma_start(out=cb16[:], in_=consts_d[:, :]), w=[R_cb], q="pool")

        def nwcol(l, j, c):
            i = (l * 4 + j) * 8 + c
            return nw[:, i:i + 1]

        car, j0 = new_phase()
        NXS = 6
        xs = [car.take([D], F32) for _ in range(NXS)]
        R_xs = ares(NXS)
        seed(R_xs, j0)
        for tt in range(NT):
            s = tt % NXS
            n = tt // 4
            S.dma(lambda e, s=s, tt=tt: e.dma_start(out=xs[s], in_=x_d[tt * 128:(tt + 1) * 128, :]), w=[R_xs[s]],
                  q=("sp" if tt % 2 == 0 else "act"))
            for half in range(2):
                bk = 2 * (tt % 2) + half
                for c4 in range(4):
                    c = half * 4 + c4
                    S.pe(lambda e, bk=bk, c4=c4, c=c, s=s: e.transpose(
                        out=banks[bk][:, c4 * 128:(c4 + 1) * 128], in_=xs[s][:, c * 128:(c + 1) * 128], identity=ident_f),
                        r=[R_xs[s], R_cf], w=[R_bank[bk]])
                dst = xT[:, half * 4:half * 4 + 4, tt * 128:(tt + 1) * 128]
                src = banks[bk][:, :].rearrange("p (a b) -> p a b", a=4)
                wr = [R_xT[half * 4 + c4][n] for c4 in range(4)]
                if half == 0:
                    S.act(lambda e, dst=dst, src=src: e.activation(out=dst, in_=src, func=AF.Copy), r=[R_bank[bk]], w=wr)
                else:
                    S.dve(lambda e, dst=dst, src=src: e.tensor_copy(out=dst, in_=src), r=[R_bank[bk]], w=wr)

        def rs_of(slot):
            if isinstance(slot, int):
                return rstd[slot], R_rstd[slot]
            return slot

        def stats_rstd(src_fn, src_res_fn, n, slot, bank, nchunks, scale, use_pool=False):
            rt, R_rt = rs_of(slot)

            def st_mm(c):
                sq = c % 2
                S.pe(lambda e, sq=sq, c=c: e.matmul(banks[bank][:, :], lhsT=ones_b, rhs=sqt[sq][:],
                                                   start=(c == 0), stop=(c == nchunks - 1)),
                     r=[R_sq[sq], R_cb], w=[R_bank[bank]])

            for c in range(nchunks):
                sq = c % 2
                src = src_fn(c)
                if use_pool and c % 2 == 1:
                    S.add("pool", lambda e, src=src, sq=sq: e.tensor_tensor(out=sqt[sq][:], in0=src, in1=src, op=ALU.mult),
                          [src_res_fn(c)], [R_sq[sq]])
                else:
                    S.act(lambda e, src=src, sq=sq: e.activation(out=sqt[sq][:], in_=src, func=AF.Square),
                          r=[src_res_fn(c)], w=[R_sq[sq]])
                if c >= 1:
                    st_mm(c - 1)
            st_mm(nchunks - 1)
            S.act(lambda e: e.activation(out=rt[:, :], in_=banks[bank][:, :], func=AF.Ln, scale=scale, bias=EPS),
                  r=[R_bank[bank]], w=[R_rt])
            S.act(lambda e: e.activation(out=rt[:, :], in_=rt[:, :], func=AF.Exp, scale=-0.5), r=[R_rt], w=[R_rt])

        def prenorm_block(l, j, n, bank, use_pool=False):
            slot = n % 2
            blk = slice(n * 512, (n + 1) * 512)
            stats_rstd(lambda c: xT[:, c, blk], lambda c: R_xT[c][n], n, slot, bank, 8, 1.0 / D, use_pool=use_pool)
            for c in range(8):
                S.dve(lambda e, c=c: e.scalar_tensor_tensor(out=hT[:, c, blk], in0=xT[:, c, blk], scalar=nwcol(l, j, c),
                                                            in1=rstd[slot][:], op0=ALU.mult, op1=ALU.mult),
                      r=[R_xT[c][n], R_rstd[slot], R_nw], w=[R_hT[c][n]])

        def evac_y(bk, y32, R_y, m, l, j, stats_bank):
            sq = m % 2
            S.act(lambda e, bk=bk, sq=sq: e.activation(out=sqt[sq][:], in_=banks[bk][:, :], func=AF.Square),
                  r=[R_bank[bk]], w=[R_sq[sq]])
            S.act(lambda e, bk=bk, m=m: e.activation(out=y32[:, m, :], in_=banks[bk][:, :], func=AF.Identity, scale=nwcol(l, j, m)),
                  r=[R_bank[bk], R_nw], w=[R_y[m]])

            def stats_mm():
                S.pe(lambda e, sq=sq, m=m: e.matmul(banks[stats_bank][:, :], lhsT=ones_b, rhs=sqt[sq][:],
                                                   start=(m == 0), stop=(m == 7)),
                     r=[R_sq[sq], R_cb], w=[R_bank[stats_bank]])
            return stats_mm

        def post_rstd(bank, slot):
            rt, R_rt = rs_of(slot)
            S.act(lambda e: e.activation(out=rt[:, :], in_=banks[bank][:, :], func=AF.Ln, scale=1.0 / D, bias=EPS),
                  r=[R_bank[bank]], w=[R_rt])
            S.act(lambda e: e.activation(out=rt[:, :], in_=rt[:, :], func=AF.Exp, scale=-0.5), r=[R_rt], w=[R_rt])

        def postnorm_residual(l, j, n, y32, R_y, bank, slot):
            rt, R_rt = rs_of(slot)
            blk = slice(n * 512, (n + 1) * 512)
            for hf in range(2):
                cs = slice(hf * 4, hf * 4 + 4)
                S.dve(lambda e, cs=cs: e.tensor_tensor(out=y32[:, cs, :], in0=y32[:, cs, :],
                                                      in1=rt[:, :].unsqueeze(1).broadcast_to([128, 4, 512]), op=ALU.mult),
                      r=list(R_y[cs]) + [R_rt], w=list(R_y[cs]))
                S.dve(lambda e, cs=cs: e.tensor_tensor(out=xT[:, cs, blk], in0=xT[:, cs, blk], in1=y32[:, cs, :], op=ALU.add),
                      r=list(R_y[cs]) + [R_xT[c][n] for c in range(hf * 4, hf * 4 + 4)], w=[R_xT[c][n] for c in range(hf * 4, hf * 4 + 4)])

        def out_proj_residual(l, w_out_d):
            car, j = new_phase()
            y32s = [car.take([8, 512], F32) for _ in range(2)]
            prs = [car.take([512], F32) for _ in range(2)]
            R_ys = [ares(8), ares(8)]
            R_prs = ares(2)
            seed(R_ys[0] + R_ys[1] + R_prs, j)
            pend = [None]
            for n in range(NB):
                blk = slice(n * 512, (n + 1) * 512)
                y32, R_y = y32s[n % 2], R_ys[n % 2]
                for half in range(2):
                    slot = wload(w_out_d[:, half * 512:(half + 1) * 512], 8, 512)
                    for m4 in range(4):
                        m = half * 4 + m4
                        bk = m % 2
                        for c in range(8):
                            S.pe(lambda e, bk=bk, slot=slot, c=c, m4=m4: e.matmul(
                                banks[bk][:, :], lhsT=wbuf[slot][:, c, m4 * 128:(m4 + 1) * 128], rhs=mixT[:, c, blk],
                                start=(c == 0), stop=(c == 7)),
                                r=[R_w[slot], R_mix[c][n]], w=[R_bank[bk]])
                        if pend[0] is not None:
                            pend[0]()
                        pend[0] = evac_y(bk, y32, R_y, m, l, 1, 2)
                pend[0]()
                pend[0] = None
                post_rstd(2, (prs[n % 2], R_prs[n % 2]))
                if n > 0:
                    postnorm_residual(l, 1, n - 1, y32s[(n - 1) % 2], R_ys[(n - 1) % 2], 2, slot=(prs[(n - 1) % 2], R_prs[(n - 1) % 2]))
                if n == 2:
                    prenorm_block(l, 2, 0, 3)
            postnorm_residual(l, 1, NB - 1, y32s[(NB - 1) % 2], R_ys[(NB - 1) % 2], 2, slot=(prs[(NB - 1) % 2], R_prs[(NB - 1) % 2]))

        TL = {}
        TAIL = 7040

        def alloc_tail(j):
            tailc = Carver()
            tailc.off = ARENA_BYTES - TAIL
            bd = tailc.take([2, 4, 128], BF16)
            cw = tailc.take([16], F32)
            cbias = tailc.take([4], F32)
            lba = tailc.take([4], F32)
            lbx = tailc.take([4], F32)
            lam = tailc.take([4], F32)
            c8 = tailc.take([4], F32)
            c16 = tailc.take([4], F32)
            c8h = tailc.take([4], F32)
            lbah = tailc.take([4], F32)
            lbxh = tailc.take([4], F32)
            LN_HALF = tailc.take([1], F32)
            rel34 = tailc.take([8, 2, 128], BF16)
            cvec = tailc.take([8], F32)
            tail_res = [Res() for _ in range(9)]
            R_bd, R_cw, R_cbias, R_lba, R_lbx, R_lam, R_c8, R_rel34, R_cvec = tail_res
            seed(tail_res, j)
            S.dve(lambda e: e.memset(bd[:, :, :, :], 0.0), w=[R_bd])
            for wi, wd in enumerate((lwa_d, lwx_d)):
                for blk8 in range(8):
                    c = blk8 // 2
                    o = (blk8 % 2) * 64
                    S.dma(lambda e, wi=wi, wd=wd, blk8=blk8, c=c, o=o: e.dma_start(out=bd[o:o + 64, wi, c, o:o + 64], in_=wd[blk8, :, :]),
                          r=[], w=[R_bd], q="pool")
            S.dma(lambda e: e.dma_start(out=rel34[:, :, :, :], in_=rel34_d[:, :].rearrange("p (h a b) -> p h a b", h=8, a=2)),
                  w=[R_rel34], q="pool")
            S.dma(lambda e: e.dma_start(out=cvec, in_=cvec_d[:, :]), w=[R_cvec])
            S.dma(lambda e: e.dma_start(out=cw, in_=cw_d[:, :]), w=[R_cw])
            S.dma(lambda e: e.dma_start(out=cbias, in_=cb_d[:, :]), w=[R_cbias])
            S.dma(lambda e: e.dma_start(out=lba, in_=lba_d[:, :]), w=[R_lba])
            S.dma(lambda e: e.dma_start(out=lbx, in_=lbx_d[:, :]), w=[R_lbx])
            S.dma(lambda e: e.dma_start(out=lam, in_=lam_d[:, :]), w=[R_lam])
            TL.update(dict(bd=bd, cw=cw, cbias=cbias, lba=lba, lbx=lbx, lam=lam, c8=c8, c16=c16, c8h=c8h, lbah=lbah, lbxh=lbxh,
                           LN_HALF=LN_HALF, rel34=rel34, cvec=cvec, tail_res=tail_res))

        def mlp(l):
            car, j = new_phase()
            uT = mixT[:, :, :].rearrange("p c (a b) -> p (c a) b", b=512)
            y32s = [car.take([8, 512], F32) for _ in range(2)]
            prs1 = car.take([512], F32)
            prs = [prs1, prs1]
            rl = [ftmp[0][:, :], ftmp[1][:, :]]
            R_u = [R_mix[f // 4][f % 4] for f in range(32)]
            R_ys = [ares(8), ares(8)]
            R_prs1 = ares()
            R_prs = [R_prs1, R_prs1]
            R_rl = R_ftmp
            seed(R_ys[0] + R_ys[1] + [R_prs1], j)
            assert car.off <= ARENA_BYTES - TAIL, car.off

            def up(n):
                blk = slice(n * 512, (n + 1) * 512)
                for fg in range(8):
                    slot = wload_bf(l, fg)
                    for f4 in range(4):
                        f = fg * 4 + f4
                        bk = f % 2
                        for c in range(8):
                            S.pe(lambda e, bk=bk, slot=slot, c=c, f4=f4: e.matmul(
                                banks[bk][:, :], lhsT=wbuf[slot][:, c, f4 * 128:(f4 + 1) * 128], rhs=hT[:, c, blk],
                                start=(c == 0), stop=(c == 7)),
                                r=[R_w[slot], R_hT[c][n]], w=[R_bank[bk]])
                        S.act(lambda e, bk=bk: e.activation(out=rl[bk], in_=banks[bk][:, :], func=AF.Relu),
                              r=[R_bank[bk]], w=[R_rl[bk]])
                        S.dve(lambda e, bk=bk, f=f: e.tensor_tensor(out=uT[:, f, :], in0=rl[bk], in1=rl[bk], op=ALU.mult),
                              r=[R_rl[bk]], w=[R_u[f]])

            def down(n):
                y32, R_y = y32s[n % 2], R_ys[n % 2]
                pend_dn = None
                for mg in range(2):
                    for fg in range(4):
                        slot = wload_bf(l, 8 + mg * 4 + fg)
                        for m4 in range(4):
                            if mg == 1 and fg == 0 and m4 == 1 and pend_dn is not None:
                                pend_dn()
                                pend_dn = None
                            bk = 4 + m4
                            for f8 in range(8):
                                f = fg * 8 + f8
                                S.pe(lambda e, bk=bk, slot=slot, f8=f8, f=f, m4=m4, fg=fg: e.matmul(
                                    banks[bk][:, :], lhsT=wbuf[slot][:, f8, m4 * 128:(m4 + 1) * 128], rhs=uT[:, f, :],
                                    start=(fg == 0 and f8 == 0), stop=(fg == 3 and f8 == 7)),
                                    r=[R_w[slot], R_u[f]], w=[R_bank[bk]])
                    prev = None
                    for m4 in range(4):
                        m = mg * 4 + m4
                        bk = 4 + m4
                        cur = evac_y(bk, y32, R_y, m, l, 3, 3)
                        if prev is not None:
                            prev()
                        prev = cur
                    if mg == 0:
                        pend_dn = prev
                    else:
                        prev()
                post_rstd(3, (prs[n % 2], R_prs[n % 2]))

            def post(n):
                postnorm_residual(l, 3, n, y32s[n % 2], R_ys[n % 2], 3, slot=(prs[n % 2], R_prs[n % 2]))

            def next_prenorm(n):
                pass

            for n in range(NB):
                up(n)
                if n == 0 and l == 0:
                    alloc_tail(j)
                if n > 0:
                    post(n - 1)
                if n > 1:
                    next_prenorm(n - 2)
                if n + 1 < NB:
                    prenorm_block(l, 2, n + 1, 2)
                down(n)
            post(NB - 1)
            next_prenorm(NB - 2)
            next_prenorm(NB - 1)

        if stop_after >= 0.25:

            car, j = new_phase()
            wk = car.take([8, 256], BF16)
            qg = car.take([2, T], BF16)
            a_aug = car.take([T], BF16, parts=32)
            wa_f = ftmp[0][0:32, 0:256]
            wa_b = car.take([256], BF16, parts=32)
            gnw = car.take([4], F32)
            tmpE = [car.take([256], F32) for _ in range(2)]
            L_bf = [car.take([256], BF16) for _ in range(2)]
            kdec = [car.take([2, 2, 128], BF16) for _ in range(2)]
            v_bf = [car.take([512], BF16) for _ in range(2)]
            dec = car.take([2, 32], F32)
            S32 = car.take([2, 128], F32)
            S_bf = [car.take([2, 128], BF16) for _ in range(2)]
            o32s = [car.take([4, 512], F32) for _ in range(2)]
            gate = ftmp[0][:, :]
            t1 = ftmp[1][:, :]
            R_gate, R_t1 = R_ftmp
            R_wk, R_qg, R_aaug, R_wab, R_gnw, R_S32 = ares(6)
            R_waf = R_ftmp[0]
            R_decs = ares(NT)
            R_tmpE = ares(2); R_L = ares(2); R_kdec = ares(2); R_vbf = ares(2); R_Sbf = ares(2)
            R_o32s = [ares(4), ares(4)]
            seed([R_wk, R_qg, R_aaug, R_wab, R_gnw, R_S32], j)
            seed(R_decs + R_tmpE + R_L + R_kdec + R_vbf + R_Sbf + R_o32s[0] + R_o32s[1], j)

            wload(w_in0_d[:, 256:512], 8, 256, dst=wk[:, :, :], R_dst=R_wk)
            S.dma(lambda e: e.dma_start(out=wa_f[0:17, :], in_=wa_aug_d[:, :]), w=[R_waf])
            S.dma(lambda e: e.dma_start(out=gnw, in_=gnw_d[:, :]), w=[R_gnw])
            S.dve(lambda e: e.tensor_copy(out=wa_b[0:17, :], in_=wa_f[0:17, :]), r=[R_waf], w=[R_wab])
            S.pool(lambda e: e.memset(a_aug[0:32, :], 1.0), w=[R_aaug])
            S.pool(lambda e: e.memset(S32[:, :, :], 0.0), w=[R_S32])
            for _s in range(2):
                S.pool(lambda e, _s=_s: e.memset(kdec[_s][:, :, :, :], 0.0), w=[R_kdec[_s]])
            slot_q = wload(w_in0_d[:, 0:256], 8, 256)
            slot_a = wload(w_in0_d[:, 1536:1552], 8, 16)
            slot_r = wload(w_in0_d[:, 1024:1536], 8, 512)
            fw = car.take([8, 8], BF16)
            R_fw = ares()
            seed(R_fw, j)
            wload(w_in0_d[:, 3088:3096], 8, 8, dst=fw[:, :, :], R_dst=R_fw)
            early_sync = [op for op in S.q["sp"] if op.dma]
            n_early_conv = len(conv_ops)
            R_fbf = Res()
            R_ncum = Res()
            S.dma(lambda e: e.dma_start(out=fbf_p[:, :], in_=fbf_d[:, :]), w=[R_fbf])
            S.dve(lambda e: e.tensor_scalar(out=fbf_p[:, :], in0=fbf_p[:, :], scalar1=-1.0, scalar2=None, op0=ALU.mult), r=[R_fbf], w=[R_fbf])
            ones8 = cb16[0:8, 128:129].broadcast_to([8, 512])
            lf = rstd[1][0:8, :]
            R_lf = R_rstd[1]
            for _n in range(NB):
                S.dve(lambda e, _n=_n: e.memset(mixT[:, 7, _n * 512:(_n + 1) * 512], 0.0), w=[R_mix[7][_n]])
            prenorm_block(0, 0, 0, 5)
            pend_tr = []
            for n in range(NB):
                blk = slice(n * 512, (n + 1) * 512)
                for pr in range(2):
                    bk = pr
                    for c in range(8):
                        S.pe(lambda e, bk=bk, c=c, pr=pr: e.matmul(banks[bk][:, :], lhsT=wbuf[slot_q][:, c, pr * 128:(pr + 1) * 128],
                                                                   rhs=hT[:, c, blk], start=(c == 0), stop=(c == 7)),
                             r=[R_w[slot_q], R_hT[c][n]], w=[R_bank[bk]])
                for c in range(8):
                    S.pe(lambda e, c=c: e.matmul(banks[2][0:16, :], lhsT=wbuf[slot_a][:, c, 0:16], rhs=hT[:, c, blk],
                                                 start=(c == 0), stop=(c == 7)),
                         r=[R_w[slot_a], R_hT[c][n]], w=[R_bank[2]])
                while pend_tr:
                    pend_tr.pop(0)()
                if n + 1 < NB:
                    prenorm_block(0, 0, n + 1, 5 + ((n + 1) % 2))
                for pr in range(2):
                    bk = pr
                    S.act(lambda e, bk=bk, pr=pr: e.activation(out=qg[:, pr, blk], in_=banks[bk][:, :], func=AF.Copy, scale=0.125),
                          r=[R_bank[bk]], w=[R_qg])
                S.dve(lambda e: e.tensor_copy(out=a_aug[0:16, blk], in_=banks[2][0:16, :]), r=[R_bank[2]], w=[R_aaug])
                for hd in range(4):
                    bk = 3 + hd % 2
                    for c in range(8):
                        S.pe(lambda e, c=c, hd=hd, bk=bk: e.matmul(banks[bk][:, :], lhsT=wbuf[slot_r][:, c, hd * 128:(hd + 1) * 128],
                                                                   rhs=hT[:, c, blk], start=(c == 0), stop=(c == 7)),
                             r=[R_w[slot_r], R_hT[c][n]], w=[R_bank[bk]])
                    S.act(lambda e, hd=hd, bk=bk: e.activation(out=mixT[:, hd, blk], in_=banks[bk][:, :], func=AF.Silu),
                          r=[R_bank[bk]], w=[R_mix[hd][n]])
                cumb = ftmp[n % 2][0:8, :]
                for c in range(8):
                    S.pe(lambda e, c=c: e.matmul(banks[2][0:8, :], lhsT=fw[:, c, :], rhs=hT[:, c, blk], start=(c == 0), stop=(c == 7)),
                         r=[R_fw, R_hT[c][n]], w=[R_bank[2]])
                S.act(lambda e: e.activation(out=lf, in_=banks[2][0:8, :], func=AF.Exp, scale=-1.0, bias=fbf_p[:, :]),
                      r=[R_bank[2], R_fbf], w=[R_lf])
                S.act(lambda e: e.activation(out=lf, in_=lf, func=AF.Ln, bias=1.0), r=[R_lf], w=[R_lf])
                S.dve(lambda e: e.tensor_scalar(out=lf, in0=lf, scalar1=-1.0, scalar2=None, op0=ALU.mult), r=[R_lf], w=[R_lf])
                init = 0.0 if n == 0 else ftmp[(n - 1) % 2][0:8, 511:512]
                rinit = [] if n == 0 else [R_ftmp[(n - 1) % 2]]
                S.dve(lambda e: e.tensor_tensor_scan(out=cumb, data0=ones8, data1=lf, initial=init, op0=ALU.mult, op1=ALU.add),
                      r=[R_lf, R_cb] + rinit, w=[R_ftmp[n % 2]])
                S.act(lambda e: e.activation(out=mixT[64:72, 7, blk], in_=cumb, func=AF.Copy, scale=8.0),
                      r=[R_ftmp[n % 2]], w=[R_mix[7][n]])
                def cum_transposes(n=n, cumb=cumb):
                    for t4 in range(4):
                        tt = 4 * n + t4
                        S.pe(lambda e, tt=tt, t4=t4: e.transpose(out=banks[7][:, tt * 8:(tt + 1) * 8], in_=cumb[:, t4 * 128:(t4 + 1) * 128],
                                                                 identity=ident_f[0:8, 0:8]),
                             r=[R_ftmp[n % 2], R_cf], w=[R_bank[7]])
                pend_tr.append(cum_transposes)
            while pend_tr:
                pend_tr.pop(0)()
            S.act(lambda e: e.activation(out=ncum_p[:, :], in_=banks[7][:, 0:128], func=AF.Copy, scale=-1.0),
                  r=[R_bank[7]], w=[R_ncum])
            slot_v = wload(w_in0_d[:, 512:1024], 8, 512)

            def b1_k(tt):
                n = tt // 4
                tok = slice(tt * 128, (tt + 1) * 128)
                for c in range(8):
                    S.pe(lambda e, c=c: e.matmul(banks[0][:, 0:256], lhsT=hT[:, c, tok], rhs=wk[:, c, :],
                                                 start=(c == 0), stop=(c == 7)),
                         r=[R_hT[c][n], R_wk], w=[R_bank[0]])

            def b1_v(tt):
                n = tt // 4
                tok = slice(tt * 128, (tt + 1) * 128)
                for c in range(8):
                    S.pe(lambda e, c=c: e.matmul(banks[1][:, :], lhsT=hT[:, c, tok], rhs=wbuf[slot_v][:, c, :],
                                                 start=(c == 0), stop=(c == 7)),
                         r=[R_hT[c][n], R_w[slot_v]], w=[R_bank[1]])

            def b1_pre(tt):
                s = tt % 2
                tok = slice(tt * 128, (tt + 1) * 128)
                S.pe(lambda e: e.matmul(banks[0][:, 256:512], lhsT=a_aug[0:17, tok], rhs=wa_b[0:17, :], start=True, stop=True),
                     r=[R_aaug, R_wab], w=[R_bank[0]])
                S.act(lambda e: e.activation(out=tmpE[s], in_=banks[0][:, 256:512], func=AF.Exp, scale=-1.0),
                      r=[R_bank[0]], w=[R_tmpE[s]])
                S.act(lambda e: e.activation(out=L_bf[s], in_=tmpE[s], func=AF.Ln, bias=1.0),
                      r=[R_tmpE[s]], w=[R_L[s]])

            def b1_tri(tt):
                s = tt % 2
                S.pe(lambda e: e.matmul(banks[2][:, 0:256], lhsT=tri_b, rhs=L_bf[s], start=True, stop=True),
                     r=[R_L[s], R_cb], w=[R_bank[2]])
                for pr in range(2):
                    S.pe(lambda e, pr=pr: e.matmul(banks[2][:, 256 + 2 * pr:258 + 2 * pr], lhsT=L_bf[s][:, pr * 128:(pr + 1) * 128],
                                                   rhs=ind_b, start=True, stop=True),
                         r=[R_L[s], R_cb], w=[R_bank[2]])
                S.act(lambda e: e.activation(out=tmpE[s], in_=banks[2][:, 0:256], func=AF.Exp),
                      r=[R_bank[2]], w=[R_tmpE[s]])
                S.act(lambda e: e.activation(out=dec[:, :, 2 * tt:2 * tt + 2],
                                             in_=banks[2][:, 256:260].rearrange("p (a b) -> p a b", a=2), func=AF.Exp),
                      r=[R_bank[2]], w=[R_decs[tt]])
                for hh in range(2):
                    S.dve(lambda e, hh=hh: e.tensor_tensor(
                        out=kdec[s][:, :, hh, hh * 64:(hh + 1) * 64],
                        in0=banks[0][:, 0:256].rearrange("p (a b c) -> p a b c", a=2, b=2)[:, :, hh, :],
                        in1=tmpE[s].rearrange("p (a b c) -> p a b c", a=2, b=2)[:, :, hh, :], op=ALU.mult),
                        r=[R_bank[0], R_tmpE[s]], w=[R_kdec[s]])
                S.act(lambda e: e.activation(out=v_bf[s], in_=banks[1][:, :], func=AF.Copy),
                      r=[R_bank[1]], w=[R_vbf[s]])

            def b2_inc(tt, jc):
                s = tt % 2
                rows = slice(jc * 64, (jc + 1) * 64)
                bki = 3 + jc
                for pr in range(2):
                    for hh in range(2):
                        hd = 2 * pr + hh
                        S.pe(lambda e, pr=pr, hh=hh, hd=hd: e.matmul(
                            banks[bki][:, pr * 128:(pr + 1) * 128], lhsT=kdec[s][rows, pr, hh, :],
                            rhs=v_bf[s][rows, hd * 128:(hd + 1) * 128], start=(hh == 0), stop=(hh == 1)),
                            r=[R_kdec[s], R_vbf[s]], w=[R_bank[bki]])

            def b2_chain(tt, jc):
                cg = 2 * tt + jc
                ss = cg % 2
                bki = 3 + jc
                for pr in range(2):
                    S.dve(lambda e, pr=pr: e.scalar_tensor_tensor(
                        out=S32[:, pr, :], in0=S32[:, pr, :], scalar=dec[:, pr, cg:cg + 1],
                        in1=banks[bki][:, pr * 128:(pr + 1) * 128], op0=ALU.mult, op1=ALU.add),
                        r=[R_S32, R_decs[tt], R_bank[bki]], w=[R_S32])
                S.dve(lambda e: e.tensor_copy(out=S_bf[ss], in_=S32), r=[R_S32], w=[R_Sbf[ss]])

            def b2_o(tt, jc):
                cg = 2 * tt + jc
                ss = cg % 2
                for pr in range(2):
                    for hh in range(2):
                        pp = slice(hh * 64, (hh + 1) * 64)
                        S.pe(lambda e, pr=pr, hh=hh, pp=pp: e.matmul(
                            banks[5 + hh][:, pr * 128 + jc * 64:pr * 128 + (jc + 1) * 64], lhsT=S_bf[ss][pp, pr, :],
                            rhs=qg[pp, pr, cg * 64:(cg + 1) * 64], start=True, stop=True),
                            r=[R_Sbf[ss], R_qg], w=[R_bank[5 + hh]])

            def b2_out(tt):
                n = tt // 4
                t4 = tt % 4
                ob = n % 2
                for hh in range(2):
                    for pr in range(2):
                        hd = 2 * pr + hh
                        S.act(lambda e, hh=hh, pr=pr, hd=hd: e.activation(out=o32s[ob][:, hd, t4 * 128:(t4 + 1) * 128],
                                                                          in_=banks[5 + hh][:, pr * 128:(pr + 1) * 128], func=AF.Copy),
                              r=[R_bank[5 + hh]], w=[R_o32s[ob][hd]])

            def gla_fin(n, parts_only=False):
                blk = slice(n * 512, (n + 1) * 512)
                ob = n % 2
                o32, R_o32 = o32s[ob], R_o32s[ob]

                def sq(hd):
                    S.act(lambda e: e.activation(out=sqt[hd % 2][:], in_=o32[:, hd, :], func=AF.Square),
                          r=[R_o32[hd]], w=[R_sq[hd % 2]])

                def mm_rstd(hd):
                    sl = hd % 2
                    S.pe(lambda e: e.matmul(banks[2][:, :], lhsT=ones_b, rhs=sqt[hd % 2][:], start=True, stop=True),
                         r=[R_sq[hd % 2], R_cb], w=[R_bank[2]])
                    S.act(lambda e: e.activation(out=rstd[sl][:, :], in_=banks[2][:, :], func=AF.Ln, scale=1.0 / 128, bias=EPS),
                          r=[R_bank[2]], w=[R_rstd[sl]])
                    S.act(lambda e: e.activation(out=rstd[sl][:, :], in_=rstd[sl][:, :], func=AF.Exp, scale=-0.5),
                          r=[R_rstd[sl]], w=[R_rstd[sl]])

                def apply(hd):
                    sl = hd % 2
                    tq = ftmp[hd % 2]
                    S.dve(lambda e: e.scalar_tensor_tensor(out=tq[:, :], in0=o32[:, hd, :], scalar=gnw[:, hd:hd + 1],
                                                           in1=rstd[sl][:], op0=ALU.mult, op1=ALU.mult),
                          r=[R_o32[hd], R_gnw, R_rstd[sl]], w=[R_ftmp[hd % 2]])
                    S.dve(lambda e: e.tensor_tensor(out=mixT[:, hd, blk], in0=tq[:, :], in1=mixT[:, hd, blk], op=ALU.mult),
                          r=[R_ftmp[hd % 2], R_mix[hd][n]], w=[R_mix[hd][n]])

                if parts_only:
                    return sq, mm_rstd, apply
                sq(0); sq(1)
                mm_rstd(0)
                sq(2)
                mm_rstd(1)
                apply(0)
                sq(3)
                mm_rstd(2)
                apply(1)
                mm_rstd(3)
                apply(2)
                apply(3)

            b1_k(0); b1_v(0); b1_pre(0); b1_tri(0)
            for tt in range(NT):
                nxt = tt + 1 < NT
                fin = gla_fin(tt // 4 - 1, parts_only=True) if tt >= 4 else None
                fh = tt % 4
                b2_inc(tt, 0)
                b2_inc(tt, 1)
                if fin:
                    fin[0](fh)
                if nxt:
                    b1_k(tt + 1)
                    b1_pre(tt + 1)
                if fin:
                    fin[1](fh)
                b2_chain(tt, 0)
                b2_chain(tt, 1)
                if fin:
                    fin[2](fh)
                if nxt:
                    b1_v(tt + 1)
                    b1_tri(tt + 1)
                b2_o(tt, 0)
                b2_o(tt, 1)
                b2_out(tt)
            gla_fin(NB - 1)

        if stop_after >= 0.8:
            car, j = new_phase()
            qa = car.take([2, T], BF16, parts=65)
            ka = car.take([2, T], BF16, parts=65)
            va = car.take([NT, 2, 128], BF16)
            PT = [car.take([512], BF16) for _ in range(3)]
            rden = ftmp[0][0:64, :]
            R_rden = R_ftmp[0]
            ncum = ncum_p[:, :].rearrange("p (a b) -> p a b", a=NT)
            R_PT = ares(3)
            seed(R_PT, j)

            R_qab, R_kab, R_vab = ares(NB), ares(NB), ares(NB)
            seed(R_qab + R_kab + R_vab, j)
            S.pool(lambda e: e.memset(va[:, :, 0, 64:128], 1.0), w=R_vab)
            S.pool(lambda e: e.memset(va[:, :, 1, 0:64], 1.0), w=R_vab)
            S.pool(lambda e: e.memset(ka[64:65, :, :], 1.0), w=R_kab)
            slot_qk = wload(w_in0_d[:, 1552:2064], 8, 512)
            slot_k = wload(w_in0_d[:, 2064:2576], 8, 512)
            slot_v = wload(w_in0_d[:, 2576:3088], 8, 512)

            def fox_inproj(hp, n):
                wc = slice(hp * 128, (hp + 1) * 128)
                blk = slice(n * 512, (n + 1) * 512)
                for (slot, dstt, R_dst, bk) in ((slot_qk, qa, R_qab[n], 0), (slot_k, ka, R_kab[n], 1)):
                    for c in range(8):
                        S.pe(lambda e, c=c, slot=slot, bk=bk: e.matmul(banks[bk][:, :], lhsT=wbuf[slot][:, c, wc], rhs=hT[:, c, blk],
                                                                      start=(c == 0), stop=(c == 7)),
                             r=[R_w[slot], R_hT[c][n]], w=[R_bank[bk]])
                    S.dve(lambda e, dstt=dstt, bk=bk: e.tensor_copy(out=dstt[0:64, 0, blk], in_=banks[bk][0:64, :]),
                          r=[R_bank[bk]], w=[R_dst])
                    S.dve(lambda e, dstt=dstt, bk=bk: e.tensor_copy(out=dstt[0:64, 1, blk], in_=banks[bk][64:128, :]),
                          r=[R_bank[bk]], w=[R_dst])
                for t4 in range(4):
                    tt = 4 * n + t4
                    tok = slice(tt * 128, (tt + 1) * 128)
                    for c in range(8):
                        S.pe(lambda e, c=c, t4=t4: e.matmul(banks[2][:, t4 * 128:(t4 + 1) * 128], lhsT=hT[:, c, tok], rhs=wbuf[slot_v][:, c, wc],
                                                            start=(c == 0), stop=(c == 7)),
                             r=[R_w[slot_v], R_hT[c][n]], w=[R_bank[2]])
                S.dve(lambda e: e.tensor_copy(
                    out=va[:, 4 * n:4 * n + 4, 0, 0:64],
                    in_=banks[2][:, :].rearrange("p (a b c) -> p a b c", a=4, b=2)[:, :, 0, :]),
                    r=[R_bank[2]], w=[R_vab[n]])
                S.dve(lambda e: e.tensor_copy(
                    out=va[:, 4 * n:4 * n + 4, 1, 64:128],
                    in_=banks[2][:, :].rearrange("p (a b c) -> p a b c", a=4, b=2)[:, :, 1, :]),
                    r=[R_bank[2]], w=[R_vab[n]])
                for hl in range(2):
                    hd = 2 * hp + hl
                    S.pe(lambda e, hd=hd, hl=hl: e.matmul(banks[hl][0:65, :], lhsT=ident_b[:, hd:hd + 65], rhs=mixT[:, 7, blk], start=True, stop=True),
                         r=[R_cb, R_mix[7][n]], w=[R_bank[hl]])
                    S.dve(lambda e, hl=hl: e.tensor_copy(out=qa[64:65, hl, blk], in_=banks[hl][64:65, :]),
                          r=[R_bank[hl]], w=[R_qab[n]])

            def fox_attn(hp, hl, qb):
                hd = 2 * hp + hl
                qs = qb * 512
                obk = 6 + hl
                nkt = 4 * (qb + 1)

                def qk_step(kt):
                    jd = kt - 4 * qb
                    c0 = 128 * jd if jd > 0 else 0
                    sb_i = kt % 3
                    sbk = 3 + sb_i
                    diag = jd >= 0
                    S.pe(lambda e: e.matmul(
                        banks[sbk][:, c0:512], lhsT=ka[0:65, hl, kt * 128:(kt + 1) * 128],
                        rhs=qa[0:65, hl, qs + c0:qs + 512], start=True, stop=(not diag)),
                        r=[R_kab[kt // 4], R_qab[qb]], w=[R_bank[sbk]])
                    if diag:
                        S.pe(lambda e: e.matmul(banks[sbk][:, c0:c0 + 128], lhsT=ident_b, rhs=trimask_b, start=False, stop=True),
                             r=[R_cb], w=[R_bank[sbk]])
                    S.act(lambda e: e.activation(
                        out=PT[sb_i][:, c0:512], in_=banks[sbk][:, c0:512], func=AF.Exp, scale=0.125,
                        bias=ncum[:, kt, hd:hd + 1]),
                        r=[R_bank[sbk], R_ncum], w=[R_PT[sb_i]])

                def pv_step(kt):
                    jd = kt - 4 * qb
                    c0 = 128 * jd if jd > 0 else 0
                    sb_i = kt % 3
                    S.pe(lambda e: e.matmul(
                        banks[obk][:, c0:512], lhsT=va[:, kt, hl, :], rhs=PT[sb_i][:, c0:512],
                        start=(kt == 0), stop=(kt == nkt - 1)),
                        r=[R_vab[kt // 4], R_PT[sb_i]], w=[R_bank[obk]])

                LA = 2
                for i in range(nkt + LA):
                    if i < nkt:
                        qk_step(i)
                    if i >= LA:
                        pv_step(i - LA)
                po = slice(hl * 64, (hl + 1) * 64)
                pd = slice((1 - hl) * 64, (2 - hl) * 64)
                S.act(lambda e: e.activation(out=ftmp[hl][po, :], in_=banks[obk][pd, :], func=AF.Ln), r=[R_bank[obk]], w=[R_ftmp[hl]])
                S.act(lambda e: e.activation(out=ftmp[hl][po, :], in_=ftmp[hl][po, :], func=AF.Exp, scale=-1.0), r=[R_ftmp[hl]], w=[R_ftmp[hl]])
                S.dve(lambda e: e.tensor_tensor(
                    out=mixT[po, 4 + hp, qs:qs + 512], in0=banks[obk][po, :], in1=ftmp[hl][po, :], op=ALU.mult),
                    r=[R_bank[obk], R_ftmp[hl]], w=[R_mix[4 + hp][qb]])

            for hp in range(4):
                fox_inproj(hp, 0)
                for qb in range(NB):
                    if qb + 1 < NB:
                        fox_inproj(hp, qb + 1)
                    for hl in range(2):
                        fox_attn(hp, hl, qb)

        if stop_after >= 1:
            out_proj_residual(0, w_out0_d)
        if stop_after >= 2:
            mlp(0)

        if stop_after >= 3:
            car, j = new_phase()
            qT2 = car.take([2, T], BF16)
            kT2 = car.take([T], BF16)
            va = car.take([NT, 2, 128], BF16)
            cstA = car.take([8, 128], BF16)
            cst0 = car.take([8, 128], BF16)
            PT5 = [car.take([640], BF16) for _ in range(2)]
            R_cst = ares()
            R_PT5 = ares(2)
            seed([R_cst] + R_PT5, j)
            assert car.off <= ARENA_BYTES - TAIL, car.off
            for n in range(NB):
                prenorm_block(1, 0, n, 4 + (n % 2), use_pool=True)
            bd, cw, cbias, lba, lbx, lam, c8, c16, c8h, lbah, lbxh, LN_HALF, rel34, cvec, tail_res = [TL[k] for k in (
                "bd", "cw", "cbias", "lba", "lbx", "lam", "c8", "c16", "c8h", "lbah", "lbxh", "LN_HALF", "rel34", "cvec", "tail_res")]
            R_bd, R_cw, R_cbias, R_lba, R_lbx, R_lam, R_c8, R_rel34, R_cvec = tail_res
            S.dve(lambda e: e.memset(rel34[64:128, :, 1, 0:64], NEG), r=[R_rel34], w=[R_rel34])
            S.act(lambda e: e.activation(out=c8, in_=lam, func=AF.Exp, scale=-1.0), r=[R_lam], w=[R_c8])
            S.act(lambda e: e.activation(out=c8, in_=c8, func=AF.Ln, bias=1.0), r=[R_c8], w=[R_c8])
            S.dve(lambda e: e.tensor_scalar(out=c16, in0=c8, scalar1=-16.0, scalar2=None, op0=ALU.mult), r=[R_c8], w=[R_c8])
            S.dve(lambda e: e.tensor_scalar(out=c8h, in0=c8, scalar1=-4.0, scalar2=None, op0=ALU.mult), r=[R_c8], w=[R_c8])
            S.dve(lambda e: e.tensor_scalar(out=c8, in0=c8, scalar1=-8.0, scalar2=None, op0=ALU.mult), r=[R_c8], w=[R_c8])
            S.dve(lambda e: e.memset(LN_HALF, -0.6931471805599453), w=[R_c8])
            S.dve(lambda e: e.tensor_scalar(out=lbah, in0=lba, scalar1=0.5, scalar2=None, op0=ALU.mult), r=[R_lba], w=[R_lba])
            S.dve(lambda e: e.tensor_scalar(out=lbxh, in0=lbx, scalar1=0.5, scalar2=None, op0=ALU.mult), r=[R_lbx], w=[R_lbx])
            S.dve(lambda e: e.memset(cst0[:, 0, :], 0.0), w=[R_cst])
            S.dve(lambda e: e.memset(cst0[0:64, 0, 64:128], NEG), r=[R_cst], w=[R_cst])
            for hd in range(8):
                S.dve(lambda e, hd=hd: e.tensor_scalar(out=rel34[:, hd, 0, :], in0=rel34[:, hd, 0, :], scalar1=cvec[:, hd:hd + 1],
                                                       scalar2=None, op0=ALU.subtract),
                      r=[R_rel34, R_cvec], w=[R_rel34])
            R_q2b, R_k2b, R_v2b = ares(NB), ares(NB), ares(NB)
            seed(R_q2b + R_k2b + R_v2b, j)
            S.add("pool", lambda e: e.memset(va[:, :, 0, 64:128], 1.0), (), R_v2b)
            S.add("pool", lambda e: e.memset(va[:, :, 1, 0:64], 1.0), (), R_v2b)
            S.add("pool", lambda e: e.memset(qT2[:, :, :], 0.0), (), R_q2b)
            slot_q = wload(w_in1_d[:, 0:512], 8, 512)
            slot_k = wload(w_in1_d[:, 512:1024], 8, 512)
            slot_v = wload(w_in1_d[:, 1024:1536], 8, 512)

            def ca_inproj(hp, n):
                wc = slice(hp * 128, (hp + 1) * 128)
                blk = slice(n * 512, (n + 1) * 512)
                for c in range(8):
                    S.pe(lambda e, c=c: e.matmul(banks[0][:, :], lhsT=wbuf[slot_q][:, c, wc], rhs=hT[:, c, blk], start=(c == 0), stop=(c == 7)),
                         r=[R_w[slot_q], R_hT[c][n]], w=[R_bank[0]])
                S.dve(lambda e: e.tensor_scalar(out=qT2[0:64, 0, blk], in0=banks[0][0:64, :], scalar1=0.125, scalar2=None, op0=ALU.mult),
                      r=[R_bank[0]], w=[R_q2b[n]])
                S.dve(lambda e: e.tensor_scalar(out=qT2[64:128, 1, blk], in0=banks[0][64:128, :], scalar1=0.125, scalar2=None, op0=ALU.mult),
                      r=[R_bank[0]], w=[R_q2b[n]])
                for c in range(8):
                    S.pe(lambda e, c=c: e.matmul(banks[1][:, :], lhsT=wbuf[slot_k][:, c, wc], rhs=hT[:, c, blk], start=(c == 0), stop=(c == 7)),
                         r=[R_w[slot_k], R_hT[c][n]], w=[R_bank[1]])
                S.dve(lambda e: e.tensor_copy(out=kT2[:, blk], in_=banks[1][:, :]), r=[R_bank[1]], w=[R_k2b[n]])
                for t4 in range(4):
                    tt = 4 * n + t4
                    tok = slice(tt * 128, (tt + 1) * 128)
                    for c in range(8):
                        S.pe(lambda e, c=c, t4=t4: e.matmul(banks[0][:, t4 * 128:(t4 + 1) * 128], lhsT=hT[:, c, tok], rhs=wbuf[slot_v][:, c, wc],
                                                            start=(c == 0), stop=(c == 7)),
                             r=[R_w[slot_v], R_hT[c][n]], w=[R_bank[0]])
                S.dve(lambda e: e.tensor_copy(
                    out=va[:, 4 * n:4 * n + 4, 0, 0:64],
                    in_=banks[0][:, :].rearrange("p (a b c) -> p a b c", a=4, b=2)[:, :, 0, :]),
                    r=[R_bank[0]], w=[R_v2b[n]])
                S.dve(lambda e: e.tensor_copy(
                    out=va[:, 4 * n:4 * n + 4, 1, 64:128],
                    in_=banks[0][:, :].rearrange("p (a b c) -> p a b c", a=4, b=2)[:, :, 1, :]),
                    r=[R_bank[0]], w=[R_v2b[n]])

            for hp in range(4):
                steps = [(hl, jq) for jq in range(NT) for hl in range(2)]

                def ca_qk(it):
                    hl, jq = steps[it]
                    hd = 2 * hp + hl
                    qsl = slice(jq * 128, (jq + 1) * 128)
                    par = it % 2
                    bA = 3 if par == 0 else 5
                    bB = 4 if par == 0 else 6
                    idxs = [i for i in range(5) if jq - 4 + i >= 0]
                    for idx in idxs:
                        kt = jq - 4 + idx
                        bk, col = (bA, idx * 128) if idx < 4 else (bB, 0)
                        nob = idx in (1, 2)
                        S.pe(lambda e, bk=bk, col=col, kt=kt, nob=nob: e.matmul(
                            banks[bk][:, col:col + 128], lhsT=kT2[:, kt * 128:(kt + 1) * 128], rhs=qT2[:, hl, qsl],
                            start=True, stop=nob),
                            r=[R_k2b[kt // 4], R_q2b[jq // 4]], w=[R_bank[bk]])
                        if not nob:
                            brhs = (cst0[:, 0, :], None, None, rel34[:, hd, 0, :], rel34[:, hd, 1, :])[idx]
                            S.pe(lambda e, bk=bk, col=col, brhs=brhs: e.matmul(
                                banks[bk][:, col:col + 128], lhsT=ident_b, rhs=brhs, start=False, stop=True),
                                r=[R_cb, R_cst, R_rel34], w=[R_bank[bk]])
                    i0 = idxs[0]
                    if i0 < 4:
                        S.act(lambda e: e.activation(out=PT5[par][:, i0 * 128:512], in_=banks[bA][:, i0 * 128:512], func=AF.Exp,
                                                     bias=cvec[:, hd:hd + 1]),
                              r=[R_bank[bA], R_cvec], w=[R_PT5[par]])
                    S.act(lambda e: e.activation(out=PT5[par][:, 512:640], in_=banks[bB][:, 0:128], func=AF.Exp),
                          r=[R_bank[bB]], w=[R_PT5[par]])

                def ca_pv(it):
                    hl, jq = steps[it]
                    par = it % 2
                    jq4 = jq % 4
                    obk = 7 if hl == 0 else 2
                    idxs = [i for i in range(5) if jq - 4 + i >= 0]
                    for ii, idx in enumerate(idxs):
                        kt = jq - 4 + idx
                        S.pe(lambda e, kt=kt, idx=idx, ii=ii, last=(idx == 4): e.matmul(
                            banks[obk][:, jq4 * 128:(jq4 + 1) * 128], lhsT=va[:, kt, hl, :], rhs=PT5[par][:, idx * 128:(idx + 1) * 128],
                            start=(ii == 0), stop=last),
                            r=[R_v2b[kt // 4], R_PT5[par]], w=[R_bank[obk]])
                    if jq4 == 3:
                        nq = jq // 4
                        po = slice(hl * 64, (hl + 1) * 64)
                        pd = slice((1 - hl) * 64, (2 - hl) * 64)
                        S.act(lambda e: e.activation(out=ftmp[hl][po, :], in_=banks[obk][pd, :], func=AF.Ln), r=[R_bank[obk]], w=[R_ftmp[hl]])
                        S.act(lambda e: e.activation(out=ftmp[hl][po, :], in_=ftmp[hl][po, :], func=AF.Exp, scale=-1.0), r=[R_ftmp[hl]], w=[R_ftmp[hl]])
                        S.dve(lambda e: e.tensor_tensor(
                            out=mixT[po, hp, nq * 512:(nq + 1) * 512], in0=banks[obk][po, :], in1=ftmp[hl][po, :], op=ALU.mult),
                            r=[R_bank[obk], R_ftmp[hl]], w=[R_mix[hp][nq]])

                ca_inproj(hp, 0)
                ca_qk(0)
                for it in range(len(steps)):
                    if it % 8 == 0 and it // 8 + 1 < NB:
                        ca_inproj(hp, it // 8 + 1)
                    if it + 1 < len(steps):
                        ca_qk(it + 1)
                    ca_pv(it)

            car, j = new_phase()
            arena_res.extend(tail_res)
            XH = [car.take([3 + 512], F32) for _ in range(3)]
            XC_ = [car.take([512], F32) for _ in range(2)]
            XCb_ = [car.take([512], BF16) for _ in range(2)]
            RR_ = [car.take([512], F32) for _ in range(2)]
            II_ = [car.take([512], F32) for _ in range(2)]
            AA_ = [car.take([512], F32) for _ in range(2)]
            MM_ = [car.take([512], F32) for _ in range(2)]
            HH = [car.take([512], F32) for _ in range(2)]
            R_XC_, R_XCb_, R_RR_, R_II_, R_AA_, R_MM_ = ares(2), ares(2), ares(2), ares(2), ares(2), ares(2)
            R_XH = ares(3); R_HH = ares(2)
            seed(R_XC_ + R_XCb_ + R_RR_ + R_II_ + R_AA_ + R_MM_ + R_XH + R_HH, j)
            assert car.off <= ARENA_BYTES - TAIL, car.off
            slot_g = wload(w_in1_d[:, 1536:2048], 8, 512)
            slot_x = wload(w_in1_d[:, 2048:2560], 8, 512)
            lsteps = [(c, n) for c in range(4) for n in range(NB)]

            def lru_sets(it):
                s = it % 2
                return (s, XC_[s], XCb_[s], RR_[s], II_[s], AA_[s], MM_[s],
                        R_XC_[s], R_XCb_[s], R_RR_[s], R_II_[s], R_AA_[s], R_MM_[s],
                        (0, 1, 2, 3) if s == 0 else (4, 5, 6, 7))

            def lru_front_a(it):
                c, n = lsteps[it]
                blk = slice(n * 512, (n + 1) * 512)
                s3 = it % 3
                p3 = (it - 1) % 3
                b0, b1 = (0, 1) if it % 2 == 0 else (4, 5)
                for kc in range(8):
                    S.pe(lambda e, kc=kc: e.matmul(banks[b0][:, :], lhsT=wbuf[slot_g][:, kc, c * 128:(c + 1) * 128], rhs=hT[:, kc, blk],
                                                   start=(kc == 0), stop=(kc == 7)),
                         r=[R_w[slot_g], R_hT[kc][n]], w=[R_bank[b0]])
                for kc in range(8):
                    S.pe(lambda e, kc=kc: e.matmul(banks[b1][:, :], lhsT=wbuf[slot_x][:, kc, c * 128:(c + 1) * 128], rhs=hT[:, kc, blk],
                                                   start=(kc == 0), stop=(kc == 7)),
                         r=[R_w[slot_x], R_hT[kc][n]], w=[R_bank[b1]])
                S.act(lambda e: e.activation(out=mixT[:, 4 + c, blk], in_=banks[b0][:, :], func=AF.Gelu_apprx_tanh),
                      r=[R_bank[b0]], w=[R_mix[4 + c][n]])
                if n == 0:
                    S.dve(lambda e: e.memset(XH[s3][:, 0:3], 0.0), w=[R_XH[s3]])
                else:
                    S.dve(lambda e: e.tensor_copy(out=XH[s3][:, 0:3], in_=XH[p3][:, 512:515]), r=[R_XH[p3]], w=[R_XH[s3]])
                S.act(lambda e: e.activation(out=XH[s3][:, 3:515], in_=banks[b1][:, :], func=AF.Copy), r=[R_bank[b1]], w=[R_XH[s3]])

            def lru_front_b(it):
                c, n = lsteps[it]
                s3 = it % 3
                s, XC, XCb, RR, II, AA, MM, R_XC, R_XCb, R_RR, R_II, R_AA, R_MM, (b0, b1, b2, b3) = lru_sets(it)
                S.dve(lambda e: e.tensor_scalar(out=XC, in0=XH[s3][:, 3:515], scalar1=cw[:, c * 4 + 3:c * 4 + 4],
                                                scalar2=cbias[:, c:c + 1], op0=ALU.mult, op1=ALU.add),
                      r=[R_XH[s3], R_cw, R_cbias], w=[R_XC])
                for jt in range(3):
                    S.dve(lambda e, jt=jt: e.scalar_tensor_tensor(out=XC, in0=XH[s3][:, jt:jt + 512], scalar=cw[:, c * 4 + jt:c * 4 + jt + 1],
                                                                  in1=XC, op0=ALU.mult, op1=ALU.add),
                          r=[R_XH[s3], R_cw, R_XC], w=[R_XC])
                S.dve(lambda e: e.tensor_copy(out=XCb, in_=XC), r=[R_XC], w=[R_XCb])
                S.pe(lambda e: e.matmul(banks[b2][:, :], lhsT=bd[:, 0, c, :], rhs=XCb, start=True, stop=True), r=[R_bd, R_XCb], w=[R_bank[b2]])
                S.pe(lambda e: e.matmul(banks[b3][:, :], lhsT=bd[:, 1, c, :], rhs=XCb, start=True, stop=True), r=[R_bd, R_XCb], w=[R_bank[b3]])

            def lru_back(it):
                c, n = lsteps[it]
                blk = slice(n * 512, (n + 1) * 512)
                s3 = it % 3
                s, XC, XCb, RR, II, AA, MM, R_XC, R_XCb, R_RR, R_II, R_AA, R_MM, (b0, b1, b2, b3) = lru_sets(it)
                S.act(lambda e: e.activation(out=RR, in_=banks[b2][:, :], func=AF.Tanh, scale=0.5, bias=lbah[:, c:c + 1]), r=[R_bank[b2], R_lba], w=[R_RR])
                S.act(lambda e: e.activation(out=II, in_=banks[b3][:, :], func=AF.Tanh, scale=0.5, bias=lbxh[:, c:c + 1]), r=[R_bank[b3], R_lbx], w=[R_II])
                S.act(lambda e: e.activation(out=AA, in_=RR, func=AF.Exp, scale=c8h[:, c:c + 1], bias=c8h[:, c:c + 1]), r=[R_RR, R_c8], w=[R_AA])
                S.act(lambda e: e.activation(out=MM, in_=RR, func=AF.Exp, scale=c8[:, c:c + 1], bias=c8[:, c:c + 1]), r=[R_RR, R_c8], w=[R_MM])
                S.act(lambda e: e.activation(out=MM, in_=MM, func=AF.Ln, scale=-1.0, bias=1.0), r=[R_MM], w=[R_MM])
                S.act(lambda e: e.activation(out=MM, in_=MM, func=AF.Exp, scale=0.5, bias=LN_HALF[:, :]), r=[R_MM, R_c8], w=[R_MM])
                S.dve(lambda e: e.scalar_tensor_tensor(out=II, in0=II, scalar=1.0, in1=XC, op0=ALU.add, op1=ALU.mult), r=[R_II, R_XC], w=[R_II])
                S.dve(lambda e: e.tensor_tensor(out=MM, in0=MM, in1=II, op=ALU.mult), r=[R_II, R_MM], w=[R_MM])
                init = 0.0 if n == 0 else HH[1 - s][:, 511:512]
                rinit = [] if n == 0 else [R_HH[1 - s]]
                S.dve(lambda e: e.tensor_tensor_scan(out=HH[s], data0=AA, data1=MM, initial=init, op0=ALU.mult, op1=ALU.add),
                      r=[R_AA, R_MM] + rinit, w=[R_HH[s]])
                S.dve(lambda e: e.tensor_tensor(out=mixT[:, 4 + c, blk], in0=HH[s], in1=mixT[:, 4 + c, blk], op=ALU.mult),
                      r=[R_HH[s], R_mix[4 + c][n]], w=[R_mix[4 + c][n]])

            NL = len(lsteps)
            lru_front_a(0)
            lru_front_a(1)
            lru_front_b(0)
            for it in range(NL):
                if it + 2 < NL:
                    lru_front_a(it + 2)
                if it + 1 < NL:
                    lru_front_b(it + 1)
                lru_back(it)

            out_proj_residual(1, w_out1_d)
        if stop_after >= 4:
            mlp(1)

        if abs(stop_after - 0.9) < 1e-6 or abs(stop_after - 2.9) < 1e-6:
            for c in range(8):
                for n in range(NB):
                    blk = slice(n * 512, (n + 1) * 512)
                    S.dve(lambda e, c=c, blk=blk: e.tensor_copy(out=xT[:, c, blk], in_=mixT[:, c, blk]), r=[R_mix[c][n]], w=[R_xT[c][n]])
        car, j = new_phase()
        NYS = 6
        ys = [car.take([D], F32) for _ in range(NYS)]
        R_ys = ares(NYS)
        seed(R_ys, j)
        for tt in range(NT):
            s = tt % NYS
            n = tt // 4
            for half in range(2):
                bk = 2 * (tt % 2) + half
                for c4 in range(4):
                    c = half * 4 + c4
                    S.pe(lambda e, bk=bk, c4=c4, c=c, tt=tt: e.transpose(
                        out=banks[bk][:, c4 * 128:(c4 + 1) * 128], in_=xT[:, c, tt * 128:(tt + 1) * 128], identity=ident_f),
                        r=[R_xT[c][n], R_cf], w=[R_bank[bk]])
                if half == 0:
                    S.act(lambda e, bk=bk, s=s: e.activation(out=ys[s][:, 0:512], in_=banks[bk][:, :], func=AF.Copy),
                          r=[R_bank[bk]], w=[R_ys[s]])
                else:
                    S.dve(lambda e, bk=bk, s=s: e.tensor_copy(out=ys[s][:, 512:1024], in_=banks[bk][:, :]),
                          r=[R_bank[bk]], w=[R_ys[s]])
            S.dma(lambda e, s=s, tt=tt: e.dma_start(out=out_d[tt * 128:(tt + 1) * 128, :], in_=ys[s]), r=[R_ys[s]])

        for op in conv_ops[n_early_conv:]:
            op.deps.extend(early_sync)
        npre = 1
        S.q["pool"] = S.q["pool"][:npre] + conv_ops + S.q["pool"][npre:]
        S.emit()
    return nc


def _consts():
    c = np.zeros((128, 520), np.float32)
    c[:, 0:128] = np.eye(128, dtype=np.float32)
    c[:, 128:256] = 1.0
    s = np.arange(128)[:, None]
    t = np.arange(128)[None, :]
    c[:, 256:384] = np.where((s // 64 == t // 64) & (s > t), -1.0 / 16.0, 0.0)
    c[:, 384:512] = np.where(s > t, NEG, 0.0)
    c[:, 512:514] = np.where(s // 64 == np.arange(2)[None, :], -1.0 / 16.0, 0.0)
    return c


def _pc(v, nchunk):
    return np.ascontiguousarray(np.asarray(v, np.float32).reshape(nchunk, 128).T)


def _layout_inputs(inp):
    f = lambda a: np.ascontiguousarray(np.asarray(a, np.float32))
    nw = f(inp["norm_w"]).reshape(2, 4, 8, 128).transpose(3, 0, 1, 2).reshape(128, 64)
    k = np.arange(128)[:, None, None]
    idx = np.arange(5)[None, :, None]
    q = np.arange(128)[None, None, :]
    d = 512 - 128 * idx + q - k
    gidx = np.clip(d, -128, 128) + 128
    rb = f(inp["rel_bias"])[0]
    relT = rb[:, gidx]
    rel34 = np.ascontiguousarray(relT[:, :, 3:5, :].transpose(1, 0, 2, 3)).reshape(128, 8 * 2 * 128)
    cvec = np.ascontiguousarray(np.broadcast_to(rb[:, 256][None, :], (128, 8)))
    cw = f(inp["conv_w"])[0]
    cwl = np.ascontiguousarray(cw.reshape(4, 4, 128).transpose(2, 1, 0)).reshape(128, 16)
    shared = {
        "nw": np.ascontiguousarray(nw),
        "consts": _consts(),
        "w_in0": f(inp["w_in_even"])[0],
        "wa_aug": np.ascontiguousarray(np.concatenate([f(inp["gla_w_a_up"])[0], f(inp["gla_b_a"])[0][None, :]], axis=0)),
        "gnw": _pc(f(inp["gla_norm_w"])[0], 4),
        "fbf": f(inp["fox_b_f"])[0].reshape(8, 1),
        "w_out0": f(inp["w_out_even"])[0],
        "w_in1": f(inp["w_in_odd"])[0],
        "rel34": rel34,
        "cvec": cvec,
        "cw": cwl,
        "cb": _pc(f(inp["conv_b"])[0], 4),
        "lwa": f(inp["lru_w_a"])[0],
        "lwx": f(inp["lru_w_x"])[0],
        "lba": _pc(f(inp["lru_b_a"])[0], 4),
        "lbx": _pc(f(inp["lru_b_x"])[0], 4),
        "lam": _pc(f(inp["lru_lambda"])[0], 4),
        "w_out1": f(inp["w_out_odd"])[0],
        "w_up": f(inp["w_mlp_up"]),
        "w_dn": f(inp["w_mlp_down"]),
    }
    x = f(inp["x"])
    return [dict(shared, x=np.ascontiguousarray(x[b])) for b in range(x.shape[0])]


_NC_CACHE = {}


def kernel(**inputs):
    stop_after = float(inputs.pop("_stop_after", 99))
    ncores = int(inputs.pop("_ncores", 8))
    if stop_after not in _NC_CACHE:
        _NC_CACHE[stop_after] = build_program(stop_after)
    nc = _NC_CACHE[stop_after]
    in_maps = _layout_inputs(inputs)[:ncores]
    res = run_bass_kernel_spmd(nc, in_maps, core_ids=list(range(ncores)))
    return np.stack([np.asarray(r["out"], np.float32) for r in res.results], axis=0)
```

```python
from contextlib import ExitStack
import numpy as np
import concourse.bass as bass
import concourse.mybir as mybir
from concourse.bass_utils import run_bass_kernel_spmd

F32 = mybir.dt.float32
BF16 = mybir.dt.bfloat16
AF = mybir.ActivationFunctionType
ALU = mybir.AluOpType

T = 2048
D = 1024
NB = 4
NT = 16
EPS = 1e-6
NEG = -30000.0


import types


def _freeze(fn):
    if fn.__closure__ is None:
        return fn
    cells = []
    for c in fn.__closure__:
        try:
            cells.append(types.CellType(c.cell_contents))
        except ValueError:
            cells.append(c)
    return types.FunctionType(fn.__code__, fn.__globals__, fn.__name__, fn.__defaults__, tuple(cells))


class Res:
    __slots__ = ("name", "lw", "rs")

    def __init__(self, name="r"):
        self.name = name
        self.lw = None
        self.rs = []


class Op:
    __slots__ = ("eng", "fn", "deps", "signal", "tick", "dma", "dsem", "dval", "prev_dval")

    def __init__(self, eng, fn, dma):
        self.eng = eng
        self.fn = fn
        self.dma = dma
        self.deps = []
        self.signal = False
        self.tick = 0
        self.dsem = None
        self.dval = 0
        self.prev_dval = 0


class Sched:
    ENGS = ("pe", "act", "dve", "pool", "sp")
    NDSEM = {"sp": 16, "pool": 4}

    def __init__(self, nc):
        self.nc = nc
        self.q = {e: [] for e in self.ENGS}
        self.dcount = {e: 0 for e in self.NDSEM}

    def add(self, eng, fn, reads=(), writes=(), dma=False):
        op = Op(eng, _freeze(fn), dma)
        deps = []
        for r in reads:
            if r.lw is not None:
                deps.append(r.lw)
        for w in writes:
            if w.lw is not None:
                deps.append(w.lw)
            deps.extend(w.rs)
        seen = set()
        for d in deps:
            if d is op or id(d) in seen:
                continue
            seen.add(id(d))
            if (not d.dma) and (not dma) and d.eng == "pe" and eng == "pe":
                continue
            op.deps.append(d)
        for r in reads:
            r.rs.append(op)
        for w in writes:
            w.lw = op
            w.rs = []
        if dma:
            i = self.dcount[eng]
            self.dcount[eng] += 1
            op.dsem = (eng, i % self.NDSEM[eng])
        self.q[eng].append(op)
        return op

    def pe(self, fn, r=(), w=()):
        return self.add("pe", fn, r, w)

    def act(self, fn, r=(), w=()):
        return self.add("act", fn, r, w)

    def dve(self, fn, r=(), w=()):
        return self.add("dve", fn, r, w)

    def pool(self, fn, r=(), w=()):
        return self.add("dve", fn, r, w)

    def dma(self, fn, r=(), w=(), q="sp"):
        return self.add(q, fn, r, w, dma=True)

    def emit(self):
        nc = self.nc
        for e in self.ENGS:
            for op in self.q[e]:
                for d in op.deps:
                    if not d.dma:
                        d.signal = True
        for e in self.ENGS:
            t = 0
            for op in self.q[e]:
                if op.dma:
                    continue
                if op.signal:
                    t += 1
                    op.tick = t
        dvals = {}
        for e in self.NDSEM:
            k = 0
            for op in self.q[e]:
                if op.dma:
                    op.dsem = (e, k % self.NDSEM[e])
                    k += 1
                    v = dvals.get(op.dsem, 0)
                    op.prev_dval = v
                    op.dval = v + 16
                    dvals[op.dsem] = op.dval
        with ExitStack() as st:
            esem = {e: st.enter_context(nc.semaphore("s_" + e)) for e in ("pe", "act", "dve", "pool")}
            dsem = {}
            for e, n in self.NDSEM.items():
                for i in range(min(n, self.dcount[e])):
                    dsem[(e, i)] = st.enter_context(nc.semaphore("d_%s%d" % (e, i)))
            block = st.enter_context(nc.Block())
            q = self.q

            def run(ename, eng):
                waited = {}

                def wait(key, sem, val):
                    if waited.get(key, 0) >= val:
                        return
                    waited[key] = val
                    eng.wait_ge(sem, val)

                for op in q[ename]:
                    need = {}
                    for d in op.deps:
                        if d.dma:
                            k = ("d",) + d.dsem
                            need[k] = max(need.get(k, 0), d.dval)
                        else:
                            k = ("e", d.eng)
                            need[k] = max(need.get(k, 0), d.tick)
                    for k, v in need.items():
                        if k[0] == "d":
                            wait(k, dsem[(k[1], k[2])], v)
                        else:
                            wait(k, esem[k[1]], v)
                    if op.dma:
                        if op.prev_dval:
                            wait(("d",) + op.dsem, dsem[op.dsem], op.prev_dval)
                        op.fn(eng).then_inc(dsem[op.dsem], 16)
                    else:
                        ins = op.fn(eng)
                        if op.signal:
                            ins.then_inc(esem[ename], 1)
                if ename in self.NDSEM:
                    last = {}
                    for op in q[ename]:
                        if op.dma:
                            last[op.dsem] = op.dval
                    for k, v in last.items():
                        wait(("d",) + k, dsem[k], v)

            if q["pe"]:
                block.tensor(lambda e: run("pe", e))
            if q["act"]:
                block.scalar(lambda e: run("act", e))
            if q["dve"]:
                block.vector(lambda e: run("dve", e))
            if q["pool"]:
                block.gpsimd(lambda e: run("pool", e))
            if q["sp"]:
                block.sync(lambda e: run("sp", e))


ARENA_BYTES = 42 * 1024


def build_program(stop_after=99):
    nc = bass.Bass("TRN2", target_bir_lowering=False)
    dt_in = {}

    def din(name, shape):
        dt_in[name] = nc.dram_tensor(name, list(shape), F32, kind="ExternalInput").ap()
        return dt_in[name]

    x_d = din("x", [T, D])
    nw_d = din("nw", [128, 64])
    consts_d = din("consts", [128, 520])
    w_in0_d = din("w_in0", [D, 3096])
    wa_aug_d = din("wa_aug", [17, 256])
    gnw_d = din("gnw", [128, 4])
    fbf_d = din("fbf", [8, 1])
    w_out0_d = din("w_out0", [D, D])
    w_in1_d = din("w_in1", [D, 2560])
    rel34_d = din("rel34", [128, 8 * 2 * 128])
    cvec_d = din("cvec", [128, 8])
    cw_d = din("cw", [128, 16])
    cb_d = din("cb", [128, 4])
    lwa_d = din("lwa", [8, 64, 64])
    lwx_d = din("lwx", [8, 64, 64])
    lba_d = din("lba", [128, 4])
    lbx_d = din("lbx", [128, 4])
    lam_d = din("lam", [128, 4])
    w_out1_d = din("w_out1", [D, D])
    w_up_d = din("w_up", [2, D, 4 * D])
    w_dn_d = din("w_dn", [2, 4 * D, D])
    out_d = nc.dram_tensor("out", [T, D], F32, kind="ExternalOutput").ap()
    wsb = nc.dram_tensor("wsb", [72, 128, 4096], BF16, kind="Internal").ap()

    S = Sched(nc)
    with ExitStack() as st:
        def sb(name, shape, dt):
            return st.enter_context(nc.sbuf_tensor(name, list(shape), dt))

        xT = sb("xT", [128, 8, T], F32)
        hT = sb("hT", [128, 8, T], BF16)
        mixT = sb("mixT", [128, 8, T], BF16)
        wbuf = [sb("wbuf%d" % i, [128, 8, 512], BF16) for i in range(3)]
        nw = sb("nw_sb", [128, 64], F32)
        cf = sb("cf", [128, 128], F32)
        cb16 = sb("cb16", [128, 520], BF16)
        ftmp = [sb("ftmp%d" % i, [128, 512], F32) for i in range(2)]
        dummy = sb("dummyt", [128, 8], F32)
        rstd = [sb("rstd%d" % i, [128, 512], F32) for i in range(2)]
        sqt = [sb("sqt%d" % i, [128, 512], BF16) for i in range(2)]
        ncum_p = sb("ncum_p", [128, NT * 8], F32)
        fbf_p = sb("fbf_p", [8, 1], F32)
        arena = sb("arena", [128, ARENA_BYTES // 2], BF16)
        banks = [st.enter_context(nc.psum_tensor("bank%d" % i, [128, 512], F32)) for i in range(8)]

        ident_f = cf[:, 0:128]
        R_ftmp = [Res(), Res()]
        ident_b = cb16[:, 0:128]
        ones_b = cb16[:, 128:256]
        tri_b = cb16[:, 256:384]
        trimask_b = cb16[:, 384:512]
        ind_b = cb16[:, 512:514]

        R_xT = [[Res() for _ in range(NB)] for _ in range(8)]
        R_hT = [[Res() for _ in range(NB)] for _ in range(8)]
        R_mix = [[Res() for _ in range(NB)] for _ in range(8)]
        R_w = [Res() for _ in range(3)]
        R_nw, R_cf, R_cb = Res(), Res(), Res()
        R_rstd = [Res(), Res()]
        R_sq = [Res(), Res()]
        R_bank = [Res() for _ in range(8)]
        R_dummy = Res()
        wcount = [0]

        arena_res = []

        class Carver:
            def __init__(self):
                self.off = 0

            def take(self, shape_free, dt, parts=128):
                n = 1
                for s in shape_free:
                    n *= s
                nbytes = n * (4 if dt == F32 else 2)
                nbytes = (nbytes + 31) // 32 * 32
                assert self.off + nbytes <= ARENA_BYTES, (self.off, nbytes)
                v = arena[0:parts, self.off // 2:(self.off + nbytes) // 2]
                if dt == F32:
                    v = v.bitcast(F32)
                v = v[:, 0:n]
                if len(shape_free) == 2:
                    v = v.rearrange("p (a b) -> p a b", a=shape_free[0])
                elif len(shape_free) == 3:
                    v = v.rearrange("p (a b c) -> p a b c", a=shape_free[0], b=shape_free[1])
                elif len(shape_free) == 4:
                    v = v.rearrange("p (a b c d) -> p a b c d", a=shape_free[0], b=shape_free[1], c=shape_free[2])
                self.off += nbytes
                return v

        def ares(n=1):
            rs = [Res() for _ in range(n)]
            arena_res.extend(rs)
            return rs if n > 1 else rs[0]

        def phase_switch():
            old = list(arena_res)
            del arena_res[:]
            j = S.pool(lambda e: e.memset(dummy[:, :], 0.0), w=old + [R_dummy])
            return j

        def new_phase():
            j = phase_switch()
            return Carver(), j

        def seed(rs, j):
            for r in (rs if isinstance(rs, (list, tuple)) else [rs]):
                r.lw = j

        scr = {}
        conv_ops = []

        def wconv(src_ap, kc, ncols):
            key = repr(src_ap)
            if key in scr:
                return scr[key]
            idx = len(scr)
            R = Res()
            src = src_ap.rearrange("(k p) n -> p k n", p=128)
            dst = wsb[idx][:, 0:kc * ncols].rearrange("p (k n) -> p k n", k=kc)
            op = S.dma(lambda e: e.dma_start(out=dst, in_=src), w=[R], q="pool")
            S.q["pool"].remove(op)
            conv_ops.append(op)
            scr[key] = (idx, R)
            return scr[key]

        def wload(src_ap, kc, ncols, dst=None, R_dst=None):
            idx, R = wconv(src_ap, kc, ncols)
            slot = None
            if dst is None:
                slot = wcount[0] % 3
                wcount[0] += 1
                dst = wbuf[slot][:, 0:kc, 0:ncols]
                R_dst = R_w[slot]
            src = wsb[idx][:, 0:kc * ncols].rearrange("p (k n) -> p k n", k=kc)
            S.dma(lambda e: e.dma_start(out=dst, in_=src), r=[R], w=[R_dst], q="sp")
            return slot

        def mlp_tile_src(l, idx):
            if idx < 8:
                return w_up_d[l, :, idx * 512:(idx + 1) * 512]
            mg, fg = (idx - 8) // 4, (idx - 8) % 4
            return w_dn_d[l, fg * 1024:(fg + 1) * 1024, mg * 512:(mg + 1) * 512]

        def wload_bf(l, idx):
            return wload(mlp_tile_src(l, idx), 8, 512)

        S.dma(lambda e: e.dma_start(out=nw[:], in_=nw_d[:, :]), w=[R_nw])
        S.dma(lambda e: e.dma_start(out=cf[:], in_=consts_d[:, 0:128]), w=[R_cf])
        S.dma(lambda e: e.dma_start(out=cb16[:], in_=consts_d[:, :]), w=[R_cb], q="pool")

        def nwcol(l, j, c):
            i = (l * 4 + j) * 8 + c
            return nw[:, i:i + 1]

        car, j0 = new_phase()
        NXS = 8
        xs = [car.take([D], F32) for _ in range(NXS)]
        R_xs = ares(NXS)
        seed(R_xs, j0)
        for tt in range(NT):
            s = tt % NXS
            n = tt // 4
            S.dma(lambda e, s=s, tt=tt: e.dma_start(out=xs[s], in_=x_d[tt * 128:(tt + 1) * 128, :]), w=[R_xs[s]])
            for half in range(2):
                bk = 2 * (tt % 2) + half
                for c4 in range(4):
                    c = half * 4 + c4
                    S.pe(lambda e, bk=bk, c4=c4, c=c, s=s: e.transpose(
                        out=banks[bk][:, c4 * 128:(c4 + 1) * 128], in_=xs[s][:, c * 128:(c + 1) * 128], identity=ident_f),
                        r=[R_xs[s], R_cf], w=[R_bank[bk]])
                dst = xT[:, half * 4:half * 4 + 4, tt * 128:(tt + 1) * 128]
                src = banks[bk][:, :].rearrange("p (a b) -> p a b", a=4)
                wr = [R_xT[half * 4 + c4][n] for c4 in range(4)]
                if half == 0:
                    S.act(lambda e, dst=dst, src=src: e.activation(out=dst, in_=src, func=AF.Copy), r=[R_bank[bk]], w=wr)
                else:
                    S.dve(lambda e, dst=dst, src=src: e.tensor_copy(out=dst, in_=src), r=[R_bank[bk]], w=wr)

        def rs_of(slot):
            if isinstance(slot, int):
                return rstd[slot], R_rstd[slot]
            return slot

        def stats_rstd(src_fn, src_res_fn, n, slot, bank, nchunks, scale, use_pool=False):
            rt, R_rt = rs_of(slot)

            def st_mm(c):
                sq = c % 2
                S.pe(lambda e, sq=sq, c=c: e.matmul(banks[bank][:, :], lhsT=ones_b, rhs=sqt[sq][:],
                                                   start=(c == 0), stop=(c == nchunks - 1)),
                     r=[R_sq[sq], R_cb], w=[R_bank[bank]])

            for c in range(nchunks):
                sq = c % 2
                src = src_fn(c)
                if use_pool and c % 2 == 1:
                    S.add("pool", lambda e, src=src, sq=sq: e.tensor_tensor(out=sqt[sq][:], in0=src, in1=src, op=ALU.mult),
                          [src_res_fn(c)], [R_sq[sq]])
                else:
                    S.act(lambda e, src=src, sq=sq: e.activation(out=sqt[sq][:], in_=src, func=AF.Square),
                          r=[src_res_fn(c)], w=[R_sq[sq]])
                if c >= 1:
                    st_mm(c - 1)
            st_mm(nchunks - 1)
            S.act(lambda e: e.activation(out=rt[:, :], in_=banks[bank][:, :], func=AF.Ln, scale=scale, bias=EPS),
                  r=[R_bank[bank]], w=[R_rt])
            S.act(lambda e: e.activation(out=rt[:, :], in_=rt[:, :], func=AF.Exp, scale=-0.5), r=[R_rt], w=[R_rt])

        def prenorm_block(l, j, n, bank, use_pool=False):
            slot = n % 2
            blk = slice(n * 512, (n + 1) * 512)
            stats_rstd(lambda c: xT[:, c, blk], lambda c: R_xT[c][n], n, slot, bank, 8, 1.0 / D, use_pool=use_pool)
            for c in range(8):
                S.dve(lambda e, c=c: e.scalar_tensor_tensor(out=hT[:, c, blk], in0=xT[:, c, blk], scalar=nwcol(l, j, c),
                                                            in1=rstd[slot][:], op0=ALU.mult, op1=ALU.mult),
                      r=[R_xT[c][n], R_rstd[slot], R_nw], w=[R_hT[c][n]])

        def evac_y(bk, y32, R_y, m, l, j, stats_bank):
            sq = m % 2
            S.act(lambda e, bk=bk, sq=sq: e.activation(out=sqt[sq][:], in_=banks[bk][:, :], func=AF.Square),
                  r=[R_bank[bk]], w=[R_sq[sq]])
            S.act(lambda e, bk=bk, m=m: e.activation(out=y32[:, m, :], in_=banks[bk][:, :], func=AF.Identity, scale=nwcol(l, j, m)),
                  r=[R_bank[bk], R_nw], w=[R_y[m]])

            def stats_mm():
                S.pe(lambda e, sq=sq, m=m: e.matmul(banks[stats_bank][:, :], lhsT=ones_b, rhs=sqt[sq][:],
                                                   start=(m == 0), stop=(m == 7)),
                     r=[R_sq[sq], R_cb], w=[R_bank[stats_bank]])
            return stats_mm

        def post_rstd(bank, slot):
            rt, R_rt = rs_of(slot)
            S.act(lambda e: e.activation(out=rt[:, :], in_=banks[bank][:, :], func=AF.Ln, scale=1.0 / D, bias=EPS),
                  r=[R_bank[bank]], w=[R_rt])
            S.act(lambda e: e.activation(out=rt[:, :], in_=rt[:, :], func=AF.Exp, scale=-0.5), r=[R_rt], w=[R_rt])

        def postnorm_residual(l, j, n, y32, R_y, bank, slot):
            rt, R_rt = rs_of(slot)
            blk = slice(n * 512, (n + 1) * 512)
            for hf in range(2):
                cs = slice(hf * 4, hf * 4 + 4)
                S.dve(lambda e, cs=cs: e.tensor_tensor(out=y32[:, cs, :], in0=y32[:, cs, :],
                                                      in1=rt[:, :].unsqueeze(1).broadcast_to([128, 4, 512]), op=ALU.mult),
                      r=list(R_y[cs]) + [R_rt], w=list(R_y[cs]))
                S.dve(lambda e, cs=cs: e.tensor_tensor(out=xT[:, cs, blk], in0=xT[:, cs, blk], in1=y32[:, cs, :], op=ALU.add),
                      r=list(R_y[cs]) + [R_xT[c][n] for c in range(hf * 4, hf * 4 + 4)], w=[R_xT[c][n] for c in range(hf * 4, hf * 4 + 4)])

        def out_proj_residual(l, w_out_d):
            car, j = new_phase()
            y32s = [car.take([8, 512], F32) for _ in range(2)]
            prs = [car.take([512], F32) for _ in range(2)]
            R_ys = [ares(8), ares(8)]
            R_prs = ares(2)
            seed(R_ys[0] + R_ys[1] + R_prs, j)
            pend = [None]
            for n in range(NB):
                blk = slice(n * 512, (n + 1) * 512)
                y32, R_y = y32s[n % 2], R_ys[n % 2]
                for half in range(2):
                    slot = wload(w_out_d[:, half * 512:(half + 1) * 512], 8, 512)
                    for m4 in range(4):
                        m = half * 4 + m4
                        bk = m % 2
                        for c in range(8):
                            S.pe(lambda e, bk=bk, slot=slot, c=c, m4=m4: e.matmul(
                                banks[bk][:, :], lhsT=wbuf[slot][:, c, m4 * 128:(m4 + 1) * 128], rhs=mixT[:, c, blk],
                                start=(c == 0), stop=(c == 7)),
                                r=[R_w[slot], R_mix[c][n]], w=[R_bank[bk]])
                        if pend[0] is not None:
                            pend[0]()
                        pend[0] = evac_y(bk, y32, R_y, m, l, 1, 2)
                pend[0]()
                pend[0] = None
                post_rstd(2, (prs[n % 2], R_prs[n % 2]))
                if n > 0:
                    postnorm_residual(l, 1, n - 1, y32s[(n - 1) % 2], R_ys[(n - 1) % 2], 2, slot=(prs[(n - 1) % 2], R_prs[(n - 1) % 2]))
                if n == 2:
                    prenorm_block(l, 2, 0, 3)
            postnorm_residual(l, 1, NB - 1, y32s[(NB - 1) % 2], R_ys[(NB - 1) % 2], 2, slot=(prs[(NB - 1) % 2], R_prs[(NB - 1) % 2]))

        TL = {}
        TAIL = 7040

        def alloc_tail(j):
            tailc = Carver()
            tailc.off = ARENA_BYTES - TAIL
            bd = tailc.take([2, 4, 128], BF16)
            cw = tailc.take([16], F32)
            cbias = tailc.take([4], F32)
            lba = tailc.take([4], F32)
            lbx = tailc.take([4], F32)
            lam = tailc.take([4], F32)
            c8 = tailc.take([4], F32)
            c16 = tailc.take([4], F32)
            c8h = tailc.take([4], F32)
            lbah = tailc.take([4], F32)
            lbxh = tailc.take([4], F32)
            LN_HALF = tailc.take([1], F32)
            rel34 = tailc.take([8, 2, 128], BF16)
            cvec = tailc.take([8], F32)
            tail_res = [Res() for _ in range(9)]
            R_bd, R_cw, R_cbias, R_lba, R_lbx, R_lam, R_c8, R_rel34, R_cvec = tail_res
            seed(tail_res, j)
            S.dve(lambda e: e.memset(bd[:, :, :, :], 0.0), w=[R_bd])
            for wi, wd in enumerate((lwa_d, lwx_d)):
                for blk8 in range(8):
                    c = blk8 // 2
                    o = (blk8 % 2) * 64
                    S.dma(lambda e, wi=wi, wd=wd, blk8=blk8, c=c, o=o: e.dma_start(out=bd[o:o + 64, wi, c, o:o + 64], in_=wd[blk8, :, :]),
                          r=[], w=[R_bd], q="pool")
            S.dma(lambda e: e.dma_start(out=rel34[:, :, :, :], in_=rel34_d[:, :].rearrange("p (h a b) -> p h a b", h=8, a=2)),
                  w=[R_rel34], q="pool")
            S.dma(lambda e: e.dma_start(out=cvec, in_=cvec_d[:, :]), w=[R_cvec])
            S.dma(lambda e: e.dma_start(out=cw, in_=cw_d[:, :]), w=[R_cw])
            S.dma(lambda e: e.dma_start(out=cbias, in_=cb_d[:, :]), w=[R_cbias])
            S.dma(lambda e: e.dma_start(out=lba, in_=lba_d[:, :]), w=[R_lba])
            S.dma(lambda e: e.dma_start(out=lbx, in_=lbx_d[:, :]), w=[R_lbx])
            S.dma(lambda e: e.dma_start(out=lam, in_=lam_d[:, :]), w=[R_lam])
            TL.update(dict(bd=bd, cw=cw, cbias=cbias, lba=lba, lbx=lbx, lam=lam, c8=c8, c16=c16, c8h=c8h, lbah=lbah, lbxh=lbxh,
                           LN_HALF=LN_HALF, rel34=rel34, cvec=cvec, tail_res=tail_res))

        def mlp(l):
            car, j = new_phase()
            uT = mixT[:, :, :].rearrange("p c (a b) -> p (c a) b", b=512)
            y32s = [car.take([8, 512], F32) for _ in range(2)]
            prs1 = car.take([512], F32)
            prs = [prs1, prs1]
            rl = [ftmp[0][:, :], ftmp[1][:, :]]
            R_u = [R_mix[f // 4][f % 4] for f in range(32)]
            R_ys = [ares(8), ares(8)]
            R_prs1 = ares()
            R_prs = [R_prs1, R_prs1]
            R_rl = R_ftmp
            seed(R_ys[0] + R_ys[1] + [R_prs1], j)
            assert car.off <= ARENA_BYTES - TAIL, car.off

            def up(n):
                blk = slice(n * 512, (n + 1) * 512)
                for fg in range(8):
                    slot = wload_bf(l, fg)
                    for f4 in range(4):
                        f = fg * 4 + f4
                        bk = f % 2
                        for c in range(8):
                            S.pe(lambda e, bk=bk, slot=slot, c=c, f4=f4: e.matmul(
                                banks[bk][:, :], lhsT=wbuf[slot][:, c, f4 * 128:(f4 + 1) * 128], rhs=hT[:, c, blk],
                                start=(c == 0), stop=(c == 7)),
                                r=[R_w[slot], R_hT[c][n]], w=[R_bank[bk]])
                        S.act(lambda e, bk=bk: e.activation(out=rl[bk], in_=banks[bk][:, :], func=AF.Relu),
                              r=[R_bank[bk]], w=[R_rl[bk]])
                        S.dve(lambda e, bk=bk, f=f: e.tensor_tensor(out=uT[:, f, :], in0=rl[bk], in1=rl[bk], op=ALU.mult),
                              r=[R_rl[bk]], w=[R_u[f]])

            def down(n):
                y32, R_y = y32s[n % 2], R_ys[n % 2]
                pend_dn = None
                for mg in range(2):
                    for fg in range(4):
                        slot = wload_bf(l, 8 + mg * 4 + fg)
                        for m4 in range(4):
                            if mg == 1 and fg == 0 and m4 == 1 and pend_dn is not None:
                                pend_dn()
                                pend_dn = None
                            bk = 4 + m4
                            for f8 in range(8):
                                f = fg * 8 + f8
                                S.pe(lambda e, bk=bk, slot=slot, f8=f8, f=f, m4=m4, fg=fg: e.matmul(
                                    banks[bk][:, :], lhsT=wbuf[slot][:, f8, m4 * 128:(m4 + 1) * 128], rhs=uT[:, f, :],
                                    start=(fg == 0 and f8 == 0), stop=(fg == 3 and f8 == 7)),
                                    r=[R_w[slot], R_u[f]], w=[R_bank[bk]])
                    prev = None
                    for m4 in range(4):
                        m = mg * 4 + m4
                        bk = 4 + m4
                        cur = evac_y(bk, y32, R_y, m, l, 3, 3)
                        if prev is not None:
                            prev()
                        prev = cur
                    if mg == 0:
                        pend_dn = prev
                    else:
                        prev()
                post_rstd(3, (prs[n % 2], R_prs[n % 2]))

            def post(n):
                postnorm_residual(l, 3, n, y32s[n % 2], R_ys[n % 2], 3, slot=(prs[n % 2], R_prs[n % 2]))

            def next_prenorm(n):
                pass

            for n in range(NB):
                up(n)
                if n == 0 and l == 0:
                    alloc_tail(j)
                if n > 0:
                    post(n - 1)
                if n > 1:
                    next_prenorm(n - 2)
                if n + 1 < NB:
                    prenorm_block(l, 2, n + 1, 2)
                down(n)
            post(NB - 1)
            next_prenorm(NB - 2)
            next_prenorm(NB - 1)

        if stop_after >= 0.25:

            car, j = new_phase()
            wk = car.take([8, 256], BF16)
            qg = car.take([2, T], BF16)
            a_aug = car.take([T], BF16, parts=32)
            wa_f = ftmp[0][0:32, 0:256]
            wa_b = car.take([256], BF16, parts=32)
            gnw = car.take([4], F32)
            tmpE = [car.take([256], F32) for _ in range(2)]
            L_bf = [car.take([256], BF16) for _ in range(2)]
            kdec = [car.take([2, 2, 128], BF16) for _ in range(2)]
            v_bf = [car.take([512], BF16) for _ in range(2)]
            dec = car.take([2, 32], F32)
            S32 = car.take([2, 128], F32)
            S_bf = [car.take([2, 128], BF16) for _ in range(2)]
            o32s = [car.take([4, 512], F32) for _ in range(2)]
            gate = ftmp[0][:, :]
            t1 = ftmp[1][:, :]
            R_gate, R_t1 = R_ftmp
            R_wk, R_qg, R_aaug, R_wab, R_gnw, R_S32 = ares(6)
            R_waf = R_ftmp[0]
            R_decs = ares(NT)
            R_tmpE = ares(2); R_L = ares(2); R_kdec = ares(2); R_vbf = ares(2); R_Sbf = ares(2)
            R_o32s = [ares(4), ares(4)]
            seed([R_wk, R_qg, R_aaug, R_wab, R_gnw, R_S32], j)
            seed(R_decs + R_tmpE + R_L + R_kdec + R_vbf + R_Sbf + R_o32s[0] + R_o32s[1], j)

            wload(w_in0_d[:, 256:512], 8, 256, dst=wk[:, :, :], R_dst=R_wk)
            S.dma(lambda e: e.dma_start(out=wa_f[0:17, :], in_=wa_aug_d[:, :]), w=[R_waf])
            S.dma(lambda e: e.dma_start(out=gnw, in_=gnw_d[:, :]), w=[R_gnw])
            S.dve(lambda e: e.tensor_copy(out=wa_b[0:17, :], in_=wa_f[0:17, :]), r=[R_waf], w=[R_wab])
            S.pool(lambda e: e.memset(a_aug[0:32, :], 1.0), w=[R_aaug])
            S.pool(lambda e: e.memset(S32[:, :, :], 0.0), w=[R_S32])
            for _s in range(2):
                S.pool(lambda e, _s=_s: e.memset(kdec[_s][:, :, :, :], 0.0), w=[R_kdec[_s]])
            slot_q = wload(w_in0_d[:, 0:256], 8, 256)
            slot_a = wload(w_in0_d[:, 1536:1552], 8, 16)
            slot_r = wload(w_in0_d[:, 1024:1536], 8, 512)
            fw = car.take([8, 8], BF16)
            R_fw = ares()
            seed(R_fw, j)
            wload(w_in0_d[:, 3088:3096], 8, 8, dst=fw[:, :, :], R_dst=R_fw)
            early_sync = [op for op in S.q["sp"] if op.dma]
            n_early_conv = len(conv_ops)
            R_fbf = Res()
            R_ncum = Res()
            S.dma(lambda e: e.dma_start(out=fbf_p[:, :], in_=fbf_d[:, :]), w=[R_fbf])
            S.dve(lambda e: e.tensor_scalar(out=fbf_p[:, :], in0=fbf_p[:, :], scalar1=-1.0, scalar2=None, op0=ALU.mult), r=[R_fbf], w=[R_fbf])
            ones8 = cb16[0:8, 128:129].broadcast_to([8, 512])
            lf = rstd[1][0:8, :]
            R_lf = R_rstd[1]
            for _n in range(NB):
                S.dve(lambda e, _n=_n: e.memset(mixT[:, 7, _n * 512:(_n + 1) * 512], 0.0), w=[R_mix[7][_n]])
            prenorm_block(0, 0, 0, 5)
            pend_tr = []
            for n in range(NB):
                blk = slice(n * 512, (n + 1) * 512)
                for pr in range(2):
                    bk = pr
                    for c in range(8):
                        S.pe(lambda e, bk=bk, c=c, pr=pr: e.matmul(banks[bk][:, :], lhsT=wbuf[slot_q][:, c, pr * 128:(pr + 1) * 128],
                                                                   rhs=hT[:, c, blk], start=(c == 0), stop=(c == 7)),
                             r=[R_w[slot_q], R_hT[c][n]], w=[R_bank[bk]])
                for c in range(8):
                    S.pe(lambda e, c=c: e.matmul(banks[2][0:16, :], lhsT=wbuf[slot_a][:, c, 0:16], rhs=hT[:, c, blk],
                                                 start=(c == 0), stop=(c == 7)),
                         r=[R_w[slot_a], R_hT[c][n]], w=[R_bank[2]])
                while pend_tr:
                    pend_tr.pop(0)()
                if n + 1 < NB:
                    prenorm_block(0, 0, n + 1, 5 + ((n + 1) % 2))
                for pr in range(2):
                    bk = pr
                    S.act(lambda e, bk=bk, pr=pr: e.activation(out=qg[:, pr, blk], in_=banks[bk][:, :], func=AF.Copy, scale=0.125),
                          r=[R_bank[bk]], w=[R_qg])
                S.dve(lambda e: e.tensor_copy(out=a_aug[0:16, blk], in_=banks[2][0:16, :]), r=[R_bank[2]], w=[R_aaug])
                for hd in range(4):
                    bk = 3 + hd % 2
                    for c in range(8):
                        S.pe(lambda e, c=c, hd=hd, bk=bk: e.matmul(banks[bk][:, :], lhsT=wbuf[slot_r][:, c, hd * 128:(hd + 1) * 128],
                                                                   rhs=hT[:, c, blk], start=(c == 0), stop=(c == 7)),
                             r=[R_w[slot_r], R_hT[c][n]], w=[R_bank[bk]])
                    S.act(lambda e, hd=hd, bk=bk: e.activation(out=mixT[:, hd, blk], in_=banks[bk][:, :], func=AF.Silu),
                          r=[R_bank[bk]], w=[R_mix[hd][n]])
                cumb = ftmp[n % 2][0:8, :]
                for c in range(8):
                    S.pe(lambda e, c=c: e.matmul(banks[2][0:8, :], lhsT=fw[:, c, :], rhs=hT[:, c, blk], start=(c == 0), stop=(c == 7)),
                         r=[R_fw, R_hT[c][n]], w=[R_bank[2]])
                S.act(lambda e: e.activation(out=lf, in_=banks[2][0:8, :], func=AF.Exp, scale=-1.0, bias=fbf_p[:, :]),
                      r=[R_bank[2], R_fbf], w=[R_lf])
                S.act(lambda e: e.activation(out=lf, in_=lf, func=AF.Ln, bias=1.0), r=[R_lf], w=[R_lf])
                S.dve(lambda e: e.tensor_scalar(out=lf, in0=lf, scalar1=-1.0, scalar2=None, op0=ALU.mult), r=[R_lf], w=[R_lf])
                init = 0.0 if n == 0 else ftmp[(n - 1) % 2][0:8, 511:512]
                rinit = [] if n == 0 else [R_ftmp[(n - 1) % 2]]
                S.dve(lambda e: e.tensor_tensor_scan(out=cumb, data0=ones8, data1=lf, initial=init, op0=ALU.mult, op1=ALU.add),
                      r=[R_lf, R_cb] + rinit, w=[R_ftmp[n % 2]])
                S.act(lambda e: e.activation(out=mixT[64:72, 7, blk], in_=cumb, func=AF.Copy, scale=8.0),
                      r=[R_ftmp[n % 2]], w=[R_mix[7][n]])
                def cum_transposes(n=n, cumb=cumb):
                    for t4 in range(4):
                        tt = 4 * n + t4
                        S.pe(lambda e, tt=tt, t4=t4: e.transpose(out=banks[7][:, tt * 8:(tt + 1) * 8], in_=cumb[:, t4 * 128:(t4 + 1) * 128],
                                                                 identity=ident_f[0:8, 0:8]),
                             r=[R_ftmp[n % 2], R_cf], w=[R_bank[7]])
                pend_tr.append(cum_transposes)
            while pend_tr:
                pend_tr.pop(0)()
            S.act(lambda e: e.activation(out=ncum_p[:, :], in_=banks[7][:, 0:128], func=AF.Copy, scale=-1.0),
                  r=[R_bank[7]], w=[R_ncum])
            slot_v = wload(w_in0_d[:, 512:1024], 8, 512)

            def b1_k(tt):
                n = tt // 4
                tok = slice(tt * 128, (tt + 1) * 128)
                for c in range(8):
                    S.pe(lambda e, c=c: e.matmul(banks[0][:, 0:256], lhsT=hT[:, c, tok], rhs=wk[:, c, :],
                                                 start=(c == 0), stop=(c == 7)),
                         r=[R_hT[c][n], R_wk], w=[R_bank[0]])

            def b1_v(tt):
                n = tt // 4
                tok = slice(tt * 128, (tt + 1) * 128)
                for c in range(8):
                    S.pe(lambda e, c=c: e.matmul(banks[1][:, :], lhsT=hT[:, c, tok], rhs=wbuf[slot_v][:, c, :],
                                                 start=(c == 0), stop=(c == 7)),
                         r=[R_hT[c][n], R_w[slot_v]], w=[R_bank[1]])

            def b1_pre(tt):
                s = tt % 2
                tok = slice(tt * 128, (tt + 1) * 128)
                S.pe(lambda e: e.matmul(banks[0][:, 256:512], lhsT=a_aug[0:17, tok], rhs=wa_b[0:17, :], start=True, stop=True),
                     r=[R_aaug, R_wab], w=[R_bank[0]])
                S.act(lambda e: e.activation(out=tmpE[s], in_=banks[0][:, 256:512], func=AF.Exp, scale=-1.0),
                      r=[R_bank[0]], w=[R_tmpE[s]])
                S.act(lambda e: e.activation(out=L_bf[s], in_=tmpE[s], func=AF.Ln, bias=1.0),
                      r=[R_tmpE[s]], w=[R_L[s]])

            def b1_tri(tt):
                s = tt % 2
                S.pe(lambda e: e.matmul(banks[2][:, 0:256], lhsT=tri_b, rhs=L_bf[s], start=True, stop=True),
                     r=[R_L[s], R_cb], w=[R_bank[2]])
                for pr in range(2):
                    S.pe(lambda e, pr=pr: e.matmul(banks[2][:, 256 + 2 * pr:258 + 2 * pr], lhsT=L_bf[s][:, pr * 128:(pr + 1) * 128],
                                                   rhs=ind_b, start=True, stop=True),
                         r=[R_L[s], R_cb], w=[R_bank[2]])
                S.act(lambda e: e.activation(out=tmpE[s], in_=banks[2][:, 0:256], func=AF.Exp),
                      r=[R_bank[2]], w=[R_tmpE[s]])
                S.act(lambda e: e.activation(out=dec[:, :, 2 * tt:2 * tt + 2],
                                             in_=banks[2][:, 256:260].rearrange("p (a b) -> p a b", a=2), func=AF.Exp),
                      r=[R_bank[2]], w=[R_decs[tt]])
                for hh in range(2):
                    S.dve(lambda e, hh=hh: e.tensor_tensor(
                        out=kdec[s][:, :, hh, hh * 64:(hh + 1) * 64],
                        in0=banks[0][:, 0:256].rearrange("p (a b c) -> p a b c", a=2, b=2)[:, :, hh, :],
                        in1=tmpE[s].rearrange("p (a b c) -> p a b c", a=2, b=2)[:, :, hh, :], op=ALU.mult),
                        r=[R_bank[0], R_tmpE[s]], w=[R_kdec[s]])
                S.act(lambda e: e.activation(out=v_bf[s], in_=banks[1][:, :], func=AF.Copy),
                      r=[R_bank[1]], w=[R_vbf[s]])

            def b2_inc(tt, jc):
                s = tt % 2
                rows = slice(jc * 64, (jc + 1) * 64)
                bki = 3 + jc
                for pr in range(2):
                    for hh in range(2):
                        hd = 2 * pr + hh
                        S.pe(lambda e, pr=pr, hh=hh, hd=hd: e.matmul(
                            banks[bki][:, pr * 128:(pr + 1) * 128], lhsT=kdec[s][rows, pr, hh, :],
                            rhs=v_bf[s][rows, hd * 128:(hd + 1) * 128], start=(hh == 0), stop=(hh == 1)),
                            r=[R_kdec[s], R_vbf[s]], w=[R_bank[bki]])

            def b2_chain(tt, jc):
                cg = 2 * tt + jc
                ss = cg % 2
                bki = 3 + jc
                for pr in range(2):
                    S.dve(lambda e, pr=pr: e.scalar_tensor_tensor(
                        out=S32[:, pr, :], in0=S32[:, pr, :], scalar=dec[:, pr, cg:cg + 1],
                        in1=banks[bki][:, pr * 128:(pr + 1) * 128], op0=ALU.mult, op1=ALU.add),
                        r=[R_S32, R_decs[tt], R_bank[bki]], w=[R_S32])
                S.dve(lambda e: e.tensor_copy(out=S_bf[ss], in_=S32), r=[R_S32], w=[R_Sbf[ss]])

            def b2_o(tt, jc):
                cg = 2 * tt + jc
                ss = cg % 2
                for pr in range(2):
                    for hh in range(2):
                        pp = slice(hh * 64, (hh + 1) * 64)
                        S.pe(lambda e, pr=pr, hh=hh, pp=pp: e.matmul(
                            banks[5 + hh][:, pr * 128 + jc * 64:pr * 128 + (jc + 1) * 64], lhsT=S_bf[ss][pp, pr, :],
                            rhs=qg[pp, pr, cg * 64:(cg + 1) * 64], start=True, stop=True),
                            r=[R_Sbf[ss], R_qg], w=[R_bank[5 + hh]])

            def b2_out(tt):
                n = tt // 4
                t4 = tt % 4
                ob = n % 2
                for hh in range(2):
                    for pr in range(2):
                        hd = 2 * pr + hh
                        S.act(lambda e, hh=hh, pr=pr, hd=hd: e.activation(out=o32s[ob][:, hd, t4 * 128:(t4 + 1) * 128],
                                                                          in_=banks[5 + hh][:, pr * 128:(pr + 1) * 128], func=AF.Copy),
                              r=[R_bank[5 + hh]], w=[R_o32s[ob][hd]])

            def gla_fin(n, parts_only=False):
                blk = slice(n * 512, (n + 1) * 512)
                ob = n % 2
                o32, R_o32 = o32s[ob], R_o32s[ob]

                def sq(hd):
                    S.act(lambda e: e.activation(out=sqt[hd % 2][:], in_=o32[:, hd, :], func=AF.Square),
                          r=[R_o32[hd]], w=[R_sq[hd % 2]])

                def mm_rstd(hd):
                    sl = hd % 2
                    S.pe(lambda e: e.matmul(banks[2][:, :], lhsT=ones_b, rhs=sqt[hd % 2][:], start=True, stop=True),
                         r=[R_sq[hd % 2], R_cb], w=[R_bank[2]])
                    S.act(lambda e: e.activation(out=rstd[sl][:, :], in_=banks[2][:, :], func=AF.Ln, scale=1.0 / 128, bias=EPS),
                          r=[R_bank[2]], w=[R_rstd[sl]])
                    S.act(lambda e: e.activation(out=rstd[sl][:, :], in_=rstd[sl][:, :], func=AF.Exp, scale=-0.5),
                          r=[R_rstd[sl]], w=[R_rstd[sl]])

                def apply(hd):
                    sl = hd % 2
                    tq = ftmp[hd % 2]
                    S.dve(lambda e: e.scalar_tensor_tensor(out=tq[:, :], in0=o32[:, hd, :], scalar=gnw[:, hd:hd + 1],
                                                           in1=rstd[sl][:], op0=ALU.mult, op1=ALU.mult),
                          r=[R_o32[hd], R_gnw, R_rstd[sl]], w=[R_ftmp[hd % 2]])
                    S.dve(lambda e: e.tensor_tensor(out=mixT[:, hd, blk], in0=tq[:, :], in1=mixT[:, hd, blk], op=ALU.mult),
                          r=[R_ftmp[hd % 2], R_mix[hd][n]], w=[R_mix[hd][n]])

                if parts_only:
                    return sq, mm_rstd, apply
                sq(0); sq(1)
                mm_rstd(0)
                sq(2)
                mm_rstd(1)
                apply(0)
                sq(3)
                mm_rstd(2)
                apply(1)
                mm_rstd(3)
                apply(2)
                apply(3)

            b1_k(0); b1_v(0); b1_pre(0); b1_tri(0)
            for tt in range(NT):
                nxt = tt + 1 < NT
                fin = gla_fin(tt // 4 - 1, parts_only=True) if tt >= 4 else None
                fh = tt % 4
                b2_inc(tt, 0)
                b2_inc(tt, 1)
                if fin:
                    fin[0](fh)
                if nxt:
                    b1_k(tt + 1)
                    b1_pre(tt + 1)
                if fin:
                    fin[1](fh)
                b2_chain(tt, 0)
                b2_chain(tt, 1)
                if fin:
                    fin[2](fh)
                if nxt:
                    b1_v(tt + 1)
                    b1_tri(tt + 1)
                b2_o(tt, 0)
                b2_o(tt, 1)
                b2_out(tt)
            gla_fin(NB - 1)

        if stop_after >= 0.8:
            car, j = new_phase()
            qa = car.take([2, T], BF16, parts=65)
            ka = car.take([2, T], BF16, parts=65)
            va = car.take([NT, 2, 128], BF16)
            PT = [car.take([512], BF16) for _ in range(3)]
            rden = ftmp[0][0:64, :]
            R_rden = R_ftmp[0]
            ncum = ncum_p[:, :].rearrange("p (a b) -> p a b", a=NT)
            R_PT = ares(3)
            seed(R_PT, j)

            R_qab, R_kab, R_vab = ares(NB), ares(NB), ares(NB)
            seed(R_qab + R_kab + R_vab, j)
            S.pool(lambda e: e.memset(va[:, :, 0, 64:128], 1.0), w=R_vab)
            S.pool(lambda e: e.memset(va[:, :, 1, 0:64], 1.0), w=R_vab)
            S.pool(lambda e: e.memset(ka[64:65, :, :], 1.0), w=R_kab)
            slot_qk = wload(w_in0_d[:, 1552:2064], 8, 512)
            slot_k = wload(w_in0_d[:, 2064:2576], 8, 512)
            slot_v = wload(w_in0_d[:, 2576:3088], 8, 512)

            def fox_inproj(hp, n):
                wc = slice(hp * 128, (hp + 1) * 128)
                blk = slice(n * 512, (n + 1) * 512)
                for (slot, dstt, R_dst, bk) in ((slot_qk, qa, R_qab[n], 0), (slot_k, ka, R_kab[n], 1)):
                    for c in range(8):
                        S.pe(lambda e, c=c, slot=slot, bk=bk: e.matmul(banks[bk][:, :], lhsT=wbuf[slot][:, c, wc], rhs=hT[:, c, blk],
                                                                      start=(c == 0), stop=(c == 7)),
                             r=[R_w[slot], R_hT[c][n]], w=[R_bank[bk]])
                    S.dve(lambda e, dstt=dstt, bk=bk: e.tensor_copy(out=dstt[0:64, 0, blk], in_=banks[bk][0:64, :]),
                          r=[R_bank[bk]], w=[R_dst])
                    S.dve(lambda e, dstt=dstt, bk=bk: e.tensor_copy(out=dstt[0:64, 1, blk], in_=banks[bk][64:128, :]),
                          r=[R_bank[bk]], w=[R_dst])
                for t4 in range(4):
                    tt = 4 * n + t4
                    tok = slice(tt * 128, (tt + 1) * 128)
                    for c in range(8):
                        S.pe(lambda e, c=c, t4=t4: e.matmul(banks[2][:, t4 * 128:(t4 + 1) * 128], lhsT=hT[:, c, tok], rhs=wbuf[slot_v][:, c, wc],
                                                            start=(c == 0), stop=(c == 7)),
                             r=[R_w[slot_v], R_hT[c][n]], w=[R_bank[2]])
                S.dve(lambda e: e.tensor_copy(
                    out=va[:, 4 * n:4 * n + 4, 0, 0:64],
                    in_=banks[2][:, :].rearrange("p (a b c) -> p a b c", a=4, b=2)[:, :, 0, :]),
                    r=[R_bank[2]], w=[R_vab[n]])
                S.dve(lambda e: e.tensor_copy(
                    out=va[:, 4 * n:4 * n + 4, 1, 64:128],
                    in_=banks[2][:, :].rearrange("p (a b c) -> p a b c", a=4, b=2)[:, :, 1, :]),
                    r=[R_bank[2]], w=[R_vab[n]])
                for hl in range(2):
                    hd = 2 * hp + hl
                    S.pe(lambda e, hd=hd, hl=hl: e.matmul(banks[hl][0:65, :], lhsT=ident_b[:, hd:hd + 65], rhs=mixT[:, 7, blk], start=True, stop=True),
                         r=[R_cb, R_mix[7][n]], w=[R_bank[hl]])
                    S.dve(lambda e, hl=hl: e.tensor_copy(out=qa[64:65, hl, blk], in_=banks[hl][64:65, :]),
                          r=[R_bank[hl]], w=[R_qab[n]])

            def fox_attn(hp, hl, qb):
                hd = 2 * hp + hl
                qs = qb * 512
                obk = 6 + hl
                nkt = 4 * (qb + 1)

                def qk_step(kt):
                    jd = kt - 4 * qb
                    c0 = 128 * jd if jd > 0 else 0
                    sb_i = kt % 3
                    sbk = 3 + sb_i
                    diag = jd >= 0
                    S.pe(lambda e: e.matmul(
                        banks[sbk][:, c0:512], lhsT=ka[0:65, hl, kt * 128:(kt + 1) * 128],
                        rhs=qa[0:65, hl, qs + c0:qs + 512], start=True, stop=(not diag)),
                        r=[R_kab[kt // 4], R_qab[qb]], w=[R_bank[sbk]])
                    if diag:
                        S.pe(lambda e: e.matmul(banks[sbk][:, c0:c0 + 128], lhsT=ident_b, rhs=trimask_b, start=False, stop=True),
                             r=[R_cb], w=[R_bank[sbk]])
                    S.act(lambda e: e.activation(
                        out=PT[sb_i][:, c0:512], in_=banks[sbk][:, c0:512], func=AF.Exp, scale=0.125,
                        bias=ncum[:, kt, hd:hd + 1]),
                        r=[R_bank[sbk], R_ncum], w=[R_PT[sb_i]])

                def pv_step(kt):
                    jd = kt - 4 * qb
                    c0 = 128 * jd if jd > 0 else 0
                    sb_i = kt % 3
                    S.pe(lambda e: e.matmul(
                        banks[obk][:, c0:512], lhsT=va[:, kt, hl, :], rhs=PT[sb_i][:, c0:512],
                        start=(kt == 0), stop=(kt == nkt - 1)),
                        r=[R_vab[kt // 4], R_PT[sb_i]], w=[R_bank[obk]])

                LA = 2
                for i in range(nkt + LA):
                    if i < nkt:
                        qk_step(i)
                    if i >= LA:
                        pv_step(i - LA)
                po = slice(hl * 64, (hl + 1) * 64)
                pd = slice((1 - hl) * 64, (2 - hl) * 64)
                S.act(lambda e: e.activation(out=ftmp[hl][po, :], in_=banks[obk][pd, :], func=AF.Ln), r=[R_bank[obk]], w=[R_ftmp[hl]])
                S.act(lambda e: e.activation(out=ftmp[hl][po, :], in_=ftmp[hl][po, :], func=AF.Exp, scale=-1.0), r=[R_ftmp[hl]], w=[R_ftmp[hl]])
                S.dve(lambda e: e.tensor_tensor(
                    out=mixT[po, 4 + hp, qs:qs + 512], in0=banks[obk][po, :], in1=ftmp[hl][po, :], op=ALU.mult),
                    r=[R_bank[obk], R_ftmp[hl]], w=[R_mix[4 + hp][qb]])

            for hp in range(4):
                fox_inproj(hp, 0)
                for qb in range(NB):
                    if qb + 1 < NB:
                        fox_inproj(hp, qb + 1)
                    for hl in range(2):
                        fox_attn(hp, hl, qb)

        if stop_after >= 1:
            out_proj_residual(0, w_out0_d)
        if stop_after >= 2:
            mlp(0)

        if stop_after >= 3:
            car, j = new_phase()
            qT2 = car.take([2, T], BF16)
            kT2 = car.take([T], BF16)
            va = car.take([NT, 2, 128], BF16)
            cstA = car.take([8, 128], BF16)
            cst0 = car.take([8, 128], BF16)
            PT5 = [car.take([640], BF16) for _ in range(2)]
            R_cst = ares()
            R_PT5 = ares(2)
            seed([R_cst] + R_PT5, j)
            assert car.off <= ARENA_BYTES - TAIL, car.off
            for n in range(NB):
                prenorm_block(1, 0, n, 4 + (n % 2), use_pool=True)
            bd, cw, cbias, lba, lbx, lam, c8, c16, c8h, lbah, lbxh, LN_HALF, rel34, cvec, tail_res = [TL[k] for k in (
                "bd", "cw", "cbias", "lba", "lbx", "lam", "c8", "c16", "c8h", "lbah", "lbxh", "LN_HALF", "rel34", "cvec", "tail_res")]
            R_bd, R_cw, R_cbias, R_lba, R_lbx, R_lam, R_c8, R_rel34, R_cvec = tail_res
            S.dve(lambda e: e.memset(rel34[64:128, :, 1, 0:64], NEG), r=[R_rel34], w=[R_rel34])
            S.act(lambda e: e.activation(out=c8, in_=lam, func=AF.Exp, scale=-1.0), r=[R_lam], w=[R_c8])
            S.act(lambda e: e.activation(out=c8, in_=c8, func=AF.Ln, bias=1.0), r=[R_c8], w=[R_c8])
            S.dve(lambda e: e.tensor_scalar(out=c16, in0=c8, scalar1=-16.0, scalar2=None, op0=ALU.mult), r=[R_c8], w=[R_c8])
            S.dve(lambda e: e.tensor_scalar(out=c8h, in0=c8, scalar1=-4.0, scalar2=None, op0=ALU.mult), r=[R_c8], w=[R_c8])
            S.dve(lambda e: e.tensor_scalar(out=c8, in0=c8, scalar1=-8.0, scalar2=None, op0=ALU.mult), r=[R_c8], w=[R_c8])
            S.dve(lambda e: e.memset(LN_HALF, -0.6931471805599453), w=[R_c8])
            S.dve(lambda e: e.tensor_scalar(out=lbah, in0=lba, scalar1=0.5, scalar2=None, op0=ALU.mult), r=[R_lba], w=[R_lba])
            S.dve(lambda e: e.tensor_scalar(out=lbxh, in0=lbx, scalar1=0.5, scalar2=None, op0=ALU.mult), r=[R_lbx], w=[R_lbx])
            S.dve(lambda e: e.memset(cst0[:, 0, :], 0.0), w=[R_cst])
            S.dve(lambda e: e.memset(cst0[0:64, 0, 64:128], NEG), r=[R_cst], w=[R_cst])
            for hd in range(8):
                S.dve(lambda e, hd=hd: e.tensor_scalar(out=rel34[:, hd, 0, :], in0=rel34[:, hd, 0, :], scalar1=cvec[:, hd:hd + 1],
                                                       scalar2=None, op0=ALU.subtract),
                      r=[R_rel34, R_cvec], w=[R_rel34])
            R_q2b, R_k2b, R_v2b = ares(NB), ares(NB), ares(NB)
            seed(R_q2b + R_k2b + R_v2b, j)
            S.add("pool", lambda e: e.memset(va[:, :, 0, 64:128], 1.0), (), R_v2b)
            S.add("pool", lambda e: e.memset(va[:, :, 1, 0:64], 1.0), (), R_v2b)
            S.add("pool", lambda e: e.memset(qT2[:, :, :], 0.0), (), R_q2b)
            slot_q = wload(w_in1_d[:, 0:512], 8, 512)
            slot_k = wload(w_in1_d[:, 512:1024], 8, 512)
            slot_v = wload(w_in1_d[:, 1024:1536], 8, 512)

            def ca_inproj(hp, n):
                wc = slice(hp * 128, (hp + 1) * 128)
                blk = slice(n * 512, (n + 1) * 512)
                for c in range(8):
                    S.pe(lambda e, c=c: e.matmul(banks[0][:, :], lhsT=wbuf[slot_q][:, c, wc], rhs=hT[:, c, blk], start=(c == 0), stop=(c == 7)),
                         r=[R_w[slot_q], R_hT[c][n]], w=[R_bank[0]])
                S.dve(lambda e: e.tensor_scalar(out=qT2[0:64, 0, blk], in0=banks[0][0:64, :], scalar1=0.125, scalar2=None, op0=ALU.mult),
                      r=[R_bank[0]], w=[R_q2b[n]])
                S.dve(lambda e: e.tensor_scalar(out=qT2[64:128, 1, blk], in0=banks[0][64:128, :], scalar1=0.125, scalar2=None, op0=ALU.mult),
                      r=[R_bank[0]], w=[R_q2b[n]])
                for c in range(8):
                    S.pe(lambda e, c=c: e.matmul(banks[1][:, :], lhsT=wbuf[slot_k][:, c, wc], rhs=hT[:, c, blk], start=(c == 0), stop=(c == 7)),
                         r=[R_w[slot_k], R_hT[c][n]], w=[R_bank[1]])
                S.dve(lambda e: e.tensor_copy(out=kT2[:, blk], in_=banks[1][:, :]), r=[R_bank[1]], w=[R_k2b[n]])
                for t4 in range(4):
                    tt = 4 * n + t4
                    tok = slice(tt * 128, (tt + 1) * 128)
                    for c in range(8):
                        S.pe(lambda e, c=c, t4=t4: e.matmul(banks[0][:, t4 * 128:(t4 + 1) * 128], lhsT=hT[:, c, tok], rhs=wbuf[slot_v][:, c, wc],
                                                            start=(c == 0), stop=(c == 7)),
                             r=[R_w[slot_v], R_hT[c][n]], w=[R_bank[0]])
                S.dve(lambda e: e.tensor_copy(
                    out=va[:, 4 * n:4 * n + 4, 0, 0:64],
                    in_=banks[0][:, :].rearrange("p (a b c) -> p a b c", a=4, b=2)[:, :, 0, :]),
                    r=[R_bank[0]], w=[R_v2b[n]])
                S.dve(lambda e: e.tensor_copy(
                    out=va[:, 4 * n:4 * n + 4, 1, 64:128],
                    in_=banks[0][:, :].rearrange("p (a b c) -> p a b c", a=4, b=2)[:, :, 1, :]),
                    r=[R_bank[0]], w=[R_v2b[n]])

            for hp in range(4):
                steps = [(hl, jq) for jq in range(NT) for hl in range(2)]

                def ca_qk(it):
                    hl, jq = steps[it]
                    hd = 2 * hp + hl
                    qsl = slice(jq * 128, (jq + 1) * 128)
                    par = it % 2
                    bA = 3 if par == 0 else 5
                    bB = 4 if par == 0 else 6
                    idxs = [i for i in range(5) if jq - 4 + i >= 0]
                    for idx in idxs:
                        kt = jq - 4 + idx
                        bk, col = (bA, idx * 128) if idx < 4 else (bB, 0)
                        nob = idx in (1, 2)
                        S.pe(lambda e, bk=bk, col=col, kt=kt, nob=nob: e.matmul(
                            banks[bk][:, col:col + 128], lhsT=kT2[:, kt * 128:(kt + 1) * 128], rhs=qT2[:, hl, qsl],
                            start=True, stop=nob),
                            r=[R_k2b[kt // 4], R_q2b[jq // 4]], w=[R_bank[bk]])
                        if not nob:
                            brhs = (cst0[:, 0, :], None, None, rel34[:, hd, 0, :], rel34[:, hd, 1, :])[idx]
                            S.pe(lambda e, bk=bk, col=col, brhs=brhs: e.matmul(
                                banks[bk][:, col:col + 128], lhsT=ident_b, rhs=brhs, start=False, stop=True),
                                r=[R_cb, R_cst, R_rel34], w=[R_bank[bk]])
                    i0 = idxs[0]
                    if i0 < 4:
                        S.act(lambda e: e.activation(out=PT5[par][:, i0 * 128:512], in_=banks[bA][:, i0 * 128:512], func=AF.Exp,
                                                     bias=cvec[:, hd:hd + 1]),
                              r=[R_bank[bA], R_cvec], w=[R_PT5[par]])
                    S.act(lambda e: e.activation(out=PT5[par][:, 512:640], in_=banks[bB][:, 0:128], func=AF.Exp),
                          r=[R_bank[bB]], w=[R_PT5[par]])

                def ca_pv(it):
                    hl, jq = steps[it]
                    par = it % 2
                    jq4 = jq % 4
                    obk = 7 if hl == 0 else 2
                    idxs = [i for i in range(5) if jq - 4 + i >= 0]
                    for ii, idx in enumerate(idxs):
                        kt = jq - 4 + idx
                        S.pe(lambda e, kt=kt, idx=idx, ii=ii, last=(idx == 4): e.matmul(
                            banks[obk][:, jq4 * 128:(jq4 + 1) * 128], lhsT=va[:, kt, hl, :], rhs=PT5[par][:, idx * 128:(idx + 1) * 128],
                            start=(ii == 0), stop=last),
                            r=[R_v2b[kt // 4], R_PT5[par]], w=[R_bank[obk]])
                    if jq4 == 3:
                        nq = jq // 4
                        po = slice(hl * 64, (hl + 1) * 64)
                        pd = slice((1 - hl) * 64, (2 - hl) * 64)
                        S.act(lambda e: e.activation(out=ftmp[hl][po, :], in_=banks[obk][pd, :], func=AF.Ln), r=[R_bank[obk]], w=[R_ftmp[hl]])
                        S.act(lambda e: e.activation(out=ftmp[hl][po, :], in_=ftmp[hl][po, :], func=AF.Exp, scale=-1.0), r=[R_ftmp[hl]], w=[R_ftmp[hl]])
                        S.dve(lambda e: e.tensor_tensor(
                            out=mixT[po, hp, nq * 512:(nq + 1) * 512], in0=banks[obk][po, :], in1=ftmp[hl][po, :], op=ALU.mult),
                            r=[R_bank[obk], R_ftmp[hl]], w=[R_mix[hp][nq]])

                ca_inproj(hp, 0)
                ca_qk(0)
                for it in range(len(steps)):
                    if it % 8 == 0 and it // 8 + 1 < NB:
                        ca_inproj(hp, it // 8 + 1)
                    if it + 1 < len(steps):
                        ca_qk(it + 1)
                    ca_pv(it)

            car, j = new_phase()
            arena_res.extend(tail_res)
            XH = [car.take([3 + 512], F32) for _ in range(3)]
            XC_ = [car.take([512], F32) for _ in range(2)]
            XCb_ = [car.take([512], BF16) for _ in range(2)]
            RR_ = [car.take([512], F32) for _ in range(2)]
            II_ = [car.take([512], F32) for _ in range(2)]
            AA_ = [car.take([512], F32) for _ in range(2)]
            MM_ = [car.take([512], F32) for _ in range(2)]
            HH = [car.take([512], F32) for _ in range(2)]
            R_XC_, R_XCb_, R_RR_, R_II_, R_AA_, R_MM_ = ares(2), ares(2), ares(2), ares(2), ares(2), ares(2)
            R_XH = ares(3); R_HH = ares(2)
            seed(R_XC_ + R_XCb_ + R_RR_ + R_II_ + R_AA_ + R_MM_ + R_XH + R_HH, j)
            assert car.off <= ARENA_BYTES - TAIL, car.off
            slot_g = wload(w_in1_d[:, 1536:2048], 8, 512)
            slot_x = wload(w_in1_d[:, 2048:2560], 8, 512)
            lsteps = [(c, n) for c in range(4) for n in range(NB)]

            def lru_sets(it):
                s = it % 2
                return (s, XC_[s], XCb_[s], RR_[s], II_[s], AA_[s], MM_[s],
                        R_XC_[s], R_XCb_[s], R_RR_[s], R_II_[s], R_AA_[s], R_MM_[s],
                        (0, 1, 2, 3) if s == 0 else (4, 5, 6, 7))

            def lru_front_a(it):
                c, n = lsteps[it]
                blk = slice(n * 512, (n + 1) * 512)
                s3 = it % 3
                p3 = (it - 1) % 3
                b0, b1 = (0, 1) if it % 2 == 0 else (4, 5)
                for kc in range(8):
                    S.pe(lambda e, kc=kc: e.matmul(banks[b0][:, :], lhsT=wbuf[slot_g][:, kc, c * 128:(c + 1) * 128], rhs=hT[:, kc, blk],
                                                   start=(kc == 0), stop=(kc == 7)),
                         r=[R_w[slot_g], R_hT[kc][n]], w=[R_bank[b0]])
                for kc in range(8):
                    S.pe(lambda e, kc=kc: e.matmul(banks[b1][:, :], lhsT=wbuf[slot_x][:, kc, c * 128:(c + 1) * 128], rhs=hT[:, kc, blk],
                                                   start=(kc == 0), stop=(kc == 7)),
                         r=[R_w[slot_x], R_hT[kc][n]], w=[R_bank[b1]])
                S.act(lambda e: e.activation(out=mixT[:, 4 + c, blk], in_=banks[b0][:, :], func=AF.Gelu_apprx_tanh),
                      r=[R_bank[b0]], w=[R_mix[4 + c][n]])
                if n == 0:
                    S.dve(lambda e: e.memset(XH[s3][:, 0:3], 0.0), w=[R_XH[s3]])
                else:
                    S.dve(lambda e: e.tensor_copy(out=XH[s3][:, 0:3], in_=XH[p3][:, 512:515]), r=[R_XH[p3]], w=[R_XH[s3]])
                S.act(lambda e: e.activation(out=XH[s3][:, 3:515], in_=banks[b1][:, :], func=AF.Copy), r=[R_bank[b1]], w=[R_XH[s3]])

            def lru_front_b(it):
                c, n = lsteps[it]
                s3 = it % 3
                s, XC, XCb, RR, II, AA, MM, R_XC, R_XCb, R_RR, R_II, R_AA, R_MM, (b0, b1, b2, b3) = lru_sets(it)
                S.dve(lambda e: e.tensor_scalar(out=XC, in0=XH[s3][:, 3:515], scalar1=cw[:, c * 4 + 3:c * 4 + 4],
                                                scalar2=cbias[:, c:c + 1], op0=ALU.mult, op1=ALU.add),
                      r=[R_XH[s3], R_cw, R_cbias], w=[R_XC])
                for jt in range(3):
                    S.dve(lambda e, jt=jt: e.scalar_tensor_tensor(out=XC, in0=XH[s3][:, jt:jt + 512], scalar=cw[:, c * 4 + jt:c * 4 + jt + 1],
                                                                  in1=XC, op0=ALU.mult, op1=ALU.add),
                          r=[R_XH[s3], R_cw, R_XC], w=[R_XC])
                S.dve(lambda e: e.tensor_copy(out=XCb, in_=XC), r=[R_XC], w=[R_XCb])
                S.pe(lambda e: e.matmul(banks[b2][:, :], lhsT=bd[:, 0, c, :], rhs=XCb, start=True, stop=True), r=[R_bd, R_XCb], w=[R_bank[b2]])
                S.pe(lambda e: e.matmul(banks[b3][:, :], lhsT=bd[:, 1, c, :], rhs=XCb, start=True, stop=True), r=[R_bd, R_XCb], w=[R_bank[b3]])

            def lru_back(it):
                c, n = lsteps[it]
                blk = slice(n * 512, (n + 1) * 512)
                s3 = it % 3
                s, XC, XCb, RR, II, AA, MM, R_XC, R_XCb, R_RR, R_II, R_AA, R_MM, (b0, b1, b2, b3) = lru_sets(it)
                S.act(lambda e: e.activation(out=RR, in_=banks[b2][:, :], func=AF.Tanh, scale=0.5, bias=lbah[:, c:c + 1]), r=[R_bank[b2], R_lba], w=[R_RR])
                S.act(lambda e: e.activation(out=II, in_=banks[b3][:, :], func=AF.Tanh, scale=0.5, bias=lbxh[:, c:c + 1]), r=[R_bank[b3], R_lbx], w=[R_II])
                S.act(lambda e: e.activation(out=AA, in_=RR, func=AF.Exp, scale=c8h[:, c:c + 1], bias=c8h[:, c:c + 1]), r=[R_RR, R_c8], w=[R_AA])
                S.act(lambda e: e.activation(out=MM, in_=RR, func=AF.Exp, scale=c8[:, c:c + 1], bias=c8[:, c:c + 1]), r=[R_RR, R_c8], w=[R_MM])
                S.act(lambda e: e.activation(out=MM, in_=MM, func=AF.Ln, scale=-1.0, bias=1.0), r=[R_MM], w=[R_MM])
                S.act(lambda e: e.activation(out=MM, in_=MM, func=AF.Exp, scale=0.5, bias=LN_HALF[:, :]), r=[R_MM, R_c8], w=[R_MM])
                S.dve(lambda e: e.scalar_tensor_tensor(out=II, in0=II, scalar=1.0, in1=XC, op0=ALU.add, op1=ALU.mult), r=[R_II, R_XC], w=[R_II])
                S.dve(lambda e: e.tensor_tensor(out=MM, in0=MM, in1=II, op=ALU.mult), r=[R_II, R_MM], w=[R_MM])
                init = 0.0 if n == 0 else HH[1 - s][:, 511:512]
                rinit = [] if n == 0 else [R_HH[1 - s]]
                S.dve(lambda e: e.tensor_tensor_scan(out=HH[s], data0=AA, data1=MM, initial=init, op0=ALU.mult, op1=ALU.add),
                      r=[R_AA, R_MM] + rinit, w=[R_HH[s]])
                S.dve(lambda e: e.tensor_tensor(out=mixT[:, 4 + c, blk], in0=HH[s], in1=mixT[:, 4 + c, blk], op=ALU.mult),
                      r=[R_HH[s], R_mix[4 + c][n]], w=[R_mix[4 + c][n]])

            NL = len(lsteps)
            lru_front_a(0)
            lru_front_a(1)
            lru_front_b(0)
            for it in range(NL):
                if it + 2 < NL:
                    lru_front_a(it + 2)
                if it + 1 < NL:
                    lru_front_b(it + 1)
                lru_back(it)

            out_proj_residual(1, w_out1_d)
        if stop_after >= 4:
            mlp(1)

        if abs(stop_after - 0.9) < 1e-6 or abs(stop_after - 2.9) < 1e-6:
            for c in range(8):
                for n in range(NB):
                    blk = slice(n * 512, (n + 1) * 512)
                    S.dve(lambda e, c=c, blk=blk: e.tensor_copy(out=xT[:, c, blk], in_=mixT[:, c, blk]), r=[R_mix[c][n]], w=[R_xT[c][n]])
        car, j = new_phase()
        NYS = 8
        ys = [car.take([D], F32) for _ in range(NYS)]
        R_ys = ares(NYS)
        seed(R_ys, j)
        for tt in range(NT):
            s = tt % NYS
            n = tt // 4
            for half in range(2):
                bk = 2 * (tt % 2) + half
                for c4 in range(4):
                    c = half * 4 + c4
                    S.pe(lambda e, bk=bk, c4=c4, c=c, tt=tt: e.transpose(
                        out=banks[bk][:, c4 * 128:(c4 + 1) * 128], in_=xT[:, c, tt * 128:(tt + 1) * 128], identity=ident_f),
                        r=[R_xT[c][n], R_cf], w=[R_bank[bk]])
                if half == 0:
                    S.act(lambda e, bk=bk, s=s: e.activation(out=ys[s][:, 0:512], in_=banks[bk][:, :], func=AF.Copy),
                          r=[R_bank[bk]], w=[R_ys[s]])
                else:
                    S.dve(lambda e, bk=bk, s=s: e.tensor_copy(out=ys[s][:, 512:1024], in_=banks[bk][:, :]),
                          r=[R_bank[bk]], w=[R_ys[s]])
            S.dma(lambda e, s=s, tt=tt: e.dma_start(out=out_d[tt * 128:(tt + 1) * 128, :], in_=ys[s]), r=[R_ys[s]])

        for op in conv_ops[n_early_conv:]:
            op.deps.extend(early_sync)
        npre = 1
        S.q["pool"] = S.q["pool"][:npre] + conv_ops + S.q["pool"][npre:]
        S.emit()
    return nc


def _consts():
    c = np.zeros((128, 520), np.float32)
    c[:, 0:128] = np.eye(128, dtype=np.float32)
    c[:, 128:256] = 1.0
    s = np.arange(128)[:, None]
    t = np.arange(128)[None, :]
    c[:, 256:384] = np.where((s // 64 == t // 64) & (s > t), -1.0 / 16.0, 0.0)
    c[:, 384:512] = np.where(s > t, NEG, 0.0)
    c[:, 512:514] = np.where(s // 64 == np.arange(2)[None, :], -1.0 / 16.0, 0.0)
    return c


def _pc(v, nchunk):
    return np.ascontiguousarray(np.asarray(v, np.float32).reshape(nchunk, 128).T)


def _layout_inputs(inp):
    f = lambda a: np.ascontiguousarray(np.asarray(a, np.float32))
    nw = f(inp["norm_w"]).reshape(2, 4, 8, 128).transpose(3, 0, 1, 2).reshape(128, 64)
    k = np.arange(128)[:, None, None]
    idx = np.arange(5)[None, :, None]
    q = np.arange(128)[None, None, :]
    d = 512 - 128 * idx + q - k
    gidx = np.clip(d, -128, 128) + 128
    rb = f(inp["rel_bias"])[0]
    relT = rb[:, gidx]
    rel34 = np.ascontiguousarray(relT[:, :, 3:5, :].transpose(1, 0, 2, 3)).reshape(128, 8 * 2 * 128)
    cvec = np.ascontiguousarray(np.broadcast_to(rb[:, 256][None, :], (128, 8)))
    cw = f(inp["conv_w"])[0]
    cwl = np.ascontiguousarray(cw.reshape(4, 4, 128).transpose(2, 1, 0)).reshape(128, 16)
    shared = {
        "nw": np.ascontiguousarray(nw),
        "consts": _consts(),
        "w_in0": f(inp["w_in_even"])[0],
        "wa_aug": np.ascontiguousarray(np.concatenate([f(inp["gla_w_a_up"])[0], f(inp["gla_b_a"])[0][None, :]], axis=0)),
        "gnw": _pc(f(inp["gla_norm_w"])[0], 4),
        "fbf": f(inp["fox_b_f"])[0].reshape(8, 1),
        "w_out0": f(inp["w_out_even"])[0],
        "w_in1": f(inp["w_in_odd"])[0],
        "rel34": rel34,
        "cvec": cvec,
        "cw": cwl,
        "cb": _pc(f(inp["conv_b"])[0], 4),
        "lwa": f(inp["lru_w_a"])[0],
        "lwx": f(inp["lru_w_x"])[0],
        "lba": _pc(f(inp["lru_b_a"])[0], 4),
        "lbx": _pc(f(inp["lru_b_x"])[0], 4),
        "lam": _pc(f(inp["lru_lambda"])[0], 4),
        "w_out1": f(inp["w_out_odd"])[0],
        "w_up": f(inp["w_mlp_up"]),
        "w_dn": f(inp["w_mlp_down"]),
    }
    x = f(inp["x"])
    return [dict(shared, x=np.ascontiguousarray(x[b])) for b in range(x.shape[0])]


_NC_CACHE = {}


def kernel(**inputs):
    stop_after = float(inputs.pop("_stop_after", 99))
    ncores = int(inputs.pop("_ncores", 8))
    if stop_after not in _NC_CACHE:
        _NC_CACHE[stop_after] = build_program(stop_after)
    nc = _NC_CACHE[stop_after]
    in_maps = _layout_inputs(inputs)[:ncores]
    res = run_bass_kernel_spmd(nc, in_maps, core_ids=list(range(ncores)))
    return np.stack([np.asarray(r["out"], np.float32) for r in res.results], axis=0)
```

```python
from contextlib import ExitStack
import numpy as np
import concourse.bass as bass
import concourse.mybir as mybir
from concourse.bass_utils import run_bass_kernel_spmd

F32 = mybir.dt.float32
BF16 = mybir.dt.bfloat16
AF = mybir.ActivationFunctionType
ALU = mybir.AluOpType

T = 2048
D = 1024
NB = 4
NT = 16
EPS = 1e-6
NEG = -30000.0


import types


def _freeze(fn):
    if fn.__closure__ is None:
        return fn
    cells = []
    for c in fn.__closure__:
        try:
            cells.append(types.CellType(c.cell_contents))
        except ValueError:
            cells.append(c)
    return types.FunctionType(fn.__code__, fn.__globals__, fn.__name__, fn.__defaults__, tuple(cells))


class Res:
    __slots__ = ("name", "lw", "rs")

    def __init__(self, name="r"):
        self.name = name
        self.lw = None
        self.rs = []


class Op:
    __slots__ = ("eng", "fn", "deps", "signal", "tick", "dma", "dsem", "dval", "prev_dval")

    def __init__(self, eng, fn, dma):
        self.eng = eng
        self.fn = fn
        self.dma = dma
        self.deps = []
        self.signal = False
        self.tick = 0
        self.dsem = None
        self.dval = 0
        self.prev_dval = 0


class Sched:
    ENGS = ("pe", "act", "dve", "pool", "sp")
    NDSEM = {"sp": 16, "pool": 6}

    def __init__(self, nc):
        self.nc = nc
        self.q = {e: [] for e in self.ENGS}
        self.dcount = {e: 0 for e in self.NDSEM}

    def add(self, eng, fn, reads=(), writes=(), dma=False):
        op = Op(eng, _freeze(fn), dma)
        deps = []
        for r in reads:
            if r.lw is not None:
                deps.append(r.lw)
        for w in writes:
            if w.lw is not None:
                deps.append(w.lw)
            deps.extend(w.rs)
        seen = set()
        for d in deps:
            if d is op or id(d) in seen:
                continue
            seen.add(id(d))
            if (not d.dma) and (not dma) and d.eng == "pe" and eng == "pe":
                continue
            op.deps.append(d)
        for r in reads:
            r.rs.append(op)
        for w in writes:
            w.lw = op
            w.rs = []
        if dma:
            i = self.dcount[eng]
            self.dcount[eng] += 1
            op.dsem = (eng, i % self.NDSEM[eng])
        self.q[eng].append(op)
        return op

    def pe(self, fn, r=(), w=()):
        return self.add("pe", fn, r, w)

    def act(self, fn, r=(), w=()):
        return self.add("act", fn, r, w)

    def dve(self, fn, r=(), w=()):
        return self.add("dve", fn, r, w)

    def pool(self, fn, r=(), w=()):
        return self.add("dve", fn, r, w)

    def dma(self, fn, r=(), w=(), q="sp"):
        return self.add(q, fn, r, w, dma=True)

    def emit(self):
        nc = self.nc
        for e in self.ENGS:
            for op in self.q[e]:
                for d in op.deps:
                    if not d.dma:
                        d.signal = True
        for e in self.ENGS:
            t = 0
            for op in self.q[e]:
                if op.dma:
                    continue
                if op.signal:
                    t += 1
                    op.tick = t
        dvals = {}
        for e in self.NDSEM:
            k = 0
            for op in self.q[e]:
                if op.dma:
                    op.dsem = (e, k % self.NDSEM[e])
                    k += 1
                    v = dvals.get(op.dsem, 0)
                    op.prev_dval = v
                    op.dval = v + 16
                    dvals[op.dsem] = op.dval
        with ExitStack() as st:
            esem = {e: st.enter_context(nc.semaphore("s_" + e)) for e in ("pe", "act", "dve", "pool")}
            dsem = {}
            for e, n in self.NDSEM.items():
                for i in range(min(n, self.dcount[e])):
                    dsem[(e, i)] = st.enter_context(nc.semaphore("d_%s%d" % (e, i)))
            block = st.enter_context(nc.Block())
            q = self.q

            def run(ename, eng):
                waited = {}

                def wait(key, sem, val):
                    if waited.get(key, 0) >= val:
                        return
                    waited[key] = val
                    eng.wait_ge(sem, val)

                for op in q[ename]:
                    need = {}
                    for d in op.deps:
                        if d.dma:
                            k = ("d",) + d.dsem
                            need[k] = max(need.get(k, 0), d.dval)
                        else:
                            k = ("e", d.eng)
                            need[k] = max(need.get(k, 0), d.tick)
                    for k, v in need.items():
                        if k[0] == "d":
                            wait(k, dsem[(k[1], k[2])], v)
                        else:
                            wait(k, esem[k[1]], v)
                    if op.dma:
                        if op.prev_dval:
                            wait(("d",) + op.dsem, dsem[op.dsem], op.prev_dval)
                        op.fn(eng).then_inc(dsem[op.dsem], 16)
                    else:
                        ins = op.fn(eng)
                        if op.signal:
                            ins.then_inc(esem[ename], 1)
                if ename in self.NDSEM:
                    last = {}
                    for op in q[ename]:
                        if op.dma:
                            last[op.dsem] = op.dval
                    for k, v in last.items():
                        wait(("d",) + k, dsem[k], v)

            if q["pe"]:
                block.tensor(lambda e: run("pe", e))
            if q["act"]:
                block.scalar(lambda e: run("act", e))
            if q["dve"]:
                block.vector(lambda e: run("dve", e))
            if q["pool"]:
                block.gpsimd(lambda e: run("pool", e))
            if q["sp"]:
                block.sync(lambda e: run("sp", e))


ARENA_BYTES = 42 * 1024


def build_program(stop_after=99):
    nc = bass.Bass("TRN2", target_bir_lowering=False)
    dt_in = {}

    def din(name, shape):
        dt_in[name] = nc.dram_tensor(name, list(shape), F32, kind="ExternalInput").ap()
        return dt_in[name]

    x_d = din("x", [T, D])
    nw_d = din("nw", [128, 64])
    consts_d = din("consts", [128, 520])
    w_in0_d = din("w_in0", [D, 3096])
    wa_aug_d = din("wa_aug", [17, 256])
    gnw_d = din("gnw", [128, 4])
    fbf_d = din("fbf", [8, 1])
    w_out0_d = din("w_out0", [D, D])
    w_in1_d = din("w_in1", [D, 2560])
    rel34_d = din("rel34", [128, 8 * 2 * 128])
    cvec_d = din("cvec", [128, 8])
    cw_d = din("cw", [128, 16])
    cb_d = din("cb", [128, 4])
    lwa_d = din("lwa", [8, 64, 64])
    lwx_d = din("lwx", [8, 64, 64])
    lba_d = din("lba", [128, 4])
    lbx_d = din("lbx", [128, 4])
    lam_d = din("lam", [128, 4])
    w_out1_d = din("w_out1", [D, D])
    w_up_d = din("w_up", [2, D, 4 * D])
    w_dn_d = din("w_dn", [2, 4 * D, D])
    out_d = nc.dram_tensor("out", [T, D], F32, kind="ExternalOutput").ap()
    wsb = nc.dram_tensor("wsb", [72, 128, 4096], BF16, kind="Internal").ap()

    S = Sched(nc)
    with ExitStack() as st:
        def sb(name, shape, dt):
            return st.enter_context(nc.sbuf_tensor(name, list(shape), dt))

        xT = sb("xT", [128, 8, T], F32)
        hT = sb("hT", [128, 8, T], BF16)
        mixT = sb("mixT", [128, 8, T], BF16)
        wbuf = [sb("wbuf%d" % i, [128, 8, 512], BF16) for i in range(3)]
        nw = sb("nw_sb", [128, 64], F32)
        cf = sb("cf", [128, 128], F32)
        cb16 = sb("cb16", [128, 520], BF16)
        ftmp = [sb("ftmp%d" % i, [128, 512], F32) for i in range(2)]
        dummy = sb("dummyt", [128, 8], F32)
        rstd = [sb("rstd%d" % i, [128, 512], F32) for i in range(2)]
        sqt = [sb("sqt%d" % i, [128, 512], BF16) for i in range(2)]
        ncum_p = sb("ncum_p", [128, NT * 8], F32)
        fbf_p = sb("fbf_p", [8, 1], F32)
        arena = sb("arena", [128, ARENA_BYTES // 2], BF16)
        banks = [st.enter_context(nc.psum_tensor("bank%d" % i, [128, 512], F32)) for i in range(8)]

        ident_f = cf[:, 0:128]
        R_ftmp = [Res(), Res()]
        ident_b = cb16[:, 0:128]
        ones_b = cb16[:, 128:256]
        tri_b = cb16[:, 256:384]
        trimask_b = cb16[:, 384:512]
        ind_b = cb16[:, 512:514]

        R_xT = [[Res() for _ in range(NB)] for _ in range(8)]
        R_hT = [[Res() for _ in range(NB)] for _ in range(8)]
        R_mix = [[Res() for _ in range(NB)] for _ in range(8)]
        R_w = [Res() for _ in range(3)]
        R_nw, R_cf, R_cb = Res(), Res(), Res()
        R_rstd = [Res(), Res()]
        R_sq = [Res(), Res()]
        R_bank = [Res() for _ in range(8)]
        R_dummy = Res()
        wcount = [0]

        arena_res = []

        class Carver:
            def __init__(self):
                self.off = 0

            def take(self, shape_free, dt, parts=128):
                n = 1
                for s in shape_free:
                    n *= s
                nbytes = n * (4 if dt == F32 else 2)
                nbytes = (nbytes + 31) // 32 * 32
                assert self.off + nbytes <= ARENA_BYTES, (self.off, nbytes)
                v = arena[0:parts, self.off // 2:(self.off + nbytes) // 2]
                if dt == F32:
                    v = v.bitcast(F32)
                v = v[:, 0:n]
                if len(shape_free) == 2:
                    v = v.rearrange("p (a b) -> p a b", a=shape_free[0])
                elif len(shape_free) == 3:
                    v = v.rearrange("p (a b c) -> p a b c", a=shape_free[0], b=shape_free[1])
                elif len(shape_free) == 4:
                    v = v.rearrange("p (a b c d) -> p a b c d", a=shape_free[0], b=shape_free[1], c=shape_free[2])
                self.off += nbytes
                return v

        def ares(n=1):
            rs = [Res() for _ in range(n)]
            arena_res.extend(rs)
            return rs if n > 1 else rs[0]

        def phase_switch():
            old = list(arena_res)
            del arena_res[:]
            j = S.pool(lambda e: e.memset(dummy[:, :], 0.0), w=old + [R_dummy])
            return j

        def new_phase():
            j = phase_switch()
            return Carver(), j

        def seed(rs, j):
            for r in (rs if isinstance(rs, (list, tuple)) else [rs]):
                r.lw = j

        scr = {}
        conv_ops = []

        def wconv(src_ap, kc, ncols):
            key = repr(src_ap)
            if key in scr:
                return scr[key]
            idx = len(scr)
            R = Res()
            src = src_ap.rearrange("(k p) n -> p k n", p=128)
            dst = wsb[idx][:, 0:kc * ncols].rearrange("p (k n) -> p k n", k=kc)
            op = S.dma(lambda e: e.dma_start(out=dst, in_=src), w=[R], q="pool")
            S.q["pool"].remove(op)
            conv_ops.append(op)
            scr[key] = (idx, R)
            return scr[key]

        def wload(src_ap, kc, ncols, dst=None, R_dst=None):
            idx, R = wconv(src_ap, kc, ncols)
            slot = None
            if dst is None:
                slot = wcount[0] % 3
                wcount[0] += 1
                dst = wbuf[slot][:, 0:kc, 0:ncols]
                R_dst = R_w[slot]
            src = wsb[idx][:, 0:kc * ncols].rearrange("p (k n) -> p k n", k=kc)
            S.dma(lambda e: e.dma_start(out=dst, in_=src), r=[R], w=[R_dst], q="sp")
            return slot

        def mlp_tile_src(l, idx):
            if idx < 8:
                return w_up_d[l, :, idx * 512:(idx + 1) * 512]
            mg, fg = (idx - 8) // 4, (idx - 8) % 4
            return w_dn_d[l, fg * 1024:(fg + 1) * 1024, mg * 512:(mg + 1) * 512]

        def wload_bf(l, idx):
            return wload(mlp_tile_src(l, idx), 8, 512)

        S.dma(lambda e: e.dma_start(out=nw[:], in_=nw_d[:, :]), w=[R_nw])
        S.dma(lambda e: e.dma_start(out=cf[:], in_=consts_d[:, 0:128]), w=[R_cf])
        S.dma(lambda e: e.dma_start(out=cb16[:], in_=consts_d[:, :]), w=[R_cb], q="pool")

        def nwcol(l, j, c):
            i = (l * 4 + j) * 8 + c
            return nw[:, i:i + 1]

        car, j0 = new_phase()
        NXS = 8
        xs = [car.take([D], F32) for _ in range(NXS)]
        R_xs = ares(NXS)
        seed(R_xs, j0)
        for tt in range(NT):
            s = tt % NXS
            n = tt // 4
            S.dma(lambda e, s=s, tt=tt: e.dma_start(out=xs[s], in_=x_d[tt * 128:(tt + 1) * 128, :]), w=[R_xs[s]])
            for half in range(2):
                bk = 2 * (tt % 2) + half
                for c4 in range(4):
                    c = half * 4 + c4
                    S.pe(lambda e, bk=bk, c4=c4, c=c, s=s: e.transpose(
                        out=banks[bk][:, c4 * 128:(c4 + 1) * 128], in_=xs[s][:, c * 128:(c + 1) * 128], identity=ident_f),
                        r=[R_xs[s], R_cf], w=[R_bank[bk]])
                dst = xT[:, half * 4:half * 4 + 4, tt * 128:(tt + 1) * 128]
                src = banks[bk][:, :].rearrange("p (a b) -> p a b", a=4)
                wr = [R_xT[half * 4 + c4][n] for c4 in range(4)]
                if half == 0:
                    S.act(lambda e, dst=dst, src=src: e.activation(out=dst, in_=src, func=AF.Copy), r=[R_bank[bk]], w=wr)
                else:
                    S.dve(lambda e, dst=dst, src=src: e.tensor_copy(out=dst, in_=src), r=[R_bank[bk]], w=wr)

        def rs_of(slot):
            if isinstance(slot, int):
                return rstd[slot], R_rstd[slot]
            return slot

        def stats_rstd(src_fn, src_res_fn, n, slot, bank, nchunks, scale, use_pool=False):
            rt, R_rt = rs_of(slot)

            def st_mm(c):
                sq = c % 2
                S.pe(lambda e, sq=sq, c=c: e.matmul(banks[bank][:, :], lhsT=ones_b, rhs=sqt[sq][:],
                                                   start=(c == 0), stop=(c == nchunks - 1)),
                     r=[R_sq[sq], R_cb], w=[R_bank[bank]])

            for c in range(nchunks):
                sq = c % 2
                src = src_fn(c)
                if use_pool and c % 2 == 1:
                    S.add("pool", lambda e, src=src, sq=sq: e.tensor_tensor(out=sqt[sq][:], in0=src, in1=src, op=ALU.mult),
                          [src_res_fn(c)], [R_sq[sq]])
                else:
                    S.act(lambda e, src=src, sq=sq: e.activation(out=sqt[sq][:], in_=src, func=AF.Square),
                          r=[src_res_fn(c)], w=[R_sq[sq]])
                if c >= 1:
                    st_mm(c - 1)
            st_mm(nchunks - 1)
            S.act(lambda e: e.activation(out=rt[:, :], in_=banks[bank][:, :], func=AF.Ln, scale=scale, bias=EPS),
                  r=[R_bank[bank]], w=[R_rt])
            S.act(lambda e: e.activation(out=rt[:, :], in_=rt[:, :], func=AF.Exp, scale=-0.5), r=[R_rt], w=[R_rt])

        def prenorm_block(l, j, n, bank, use_pool=False):
            slot = n % 2
            blk = slice(n * 512, (n + 1) * 512)
            stats_rstd(lambda c: xT[:, c, blk], lambda c: R_xT[c][n], n, slot, bank, 8, 1.0 / D, use_pool=use_pool)
            for c in range(8):
                S.dve(lambda e, c=c: e.scalar_tensor_tensor(out=hT[:, c, blk], in0=xT[:, c, blk], scalar=nwcol(l, j, c),
                                                            in1=rstd[slot][:], op0=ALU.mult, op1=ALU.mult),
                      r=[R_xT[c][n], R_rstd[slot], R_nw], w=[R_hT[c][n]])

        def evac_y(bk, y32, R_y, m, l, j, stats_bank):
            sq = m % 2
            S.act(lambda e, bk=bk, sq=sq: e.activation(out=sqt[sq][:], in_=banks[bk][:, :], func=AF.Square),
                  r=[R_bank[bk]], w=[R_sq[sq]])
            S.act(lambda e, bk=bk, m=m: e.activation(out=y32[:, m, :], in_=banks[bk][:, :], func=AF.Identity, scale=nwcol(l, j, m)),
                  r=[R_bank[bk], R_nw], w=[R_y[m]])

            def stats_mm():
                S.pe(lambda e, sq=sq, m=m: e.matmul(banks[stats_bank][:, :], lhsT=ones_b, rhs=sqt[sq][:],
                                                   start=(m == 0), stop=(m == 7)),
                     r=[R_sq[sq], R_cb], w=[R_bank[stats_bank]])
            return stats_mm

        def post_rstd(bank, slot):
            rt, R_rt = rs_of(slot)
            S.act(lambda e: e.activation(out=rt[:, :], in_=banks[bank][:, :], func=AF.Ln, scale=1.0 / D, bias=EPS),
                  r=[R_bank[bank]], w=[R_rt])
            S.act(lambda e: e.activation(out=rt[:, :], in_=rt[:, :], func=AF.Exp, scale=-0.5), r=[R_rt], w=[R_rt])

        def postnorm_residual(l, j, n, y32, R_y, bank, slot):
            rt, R_rt = rs_of(slot)
            blk = slice(n * 512, (n + 1) * 512)
            for hf in range(2):
                cs = slice(hf * 4, hf * 4 + 4)
                S.dve(lambda e, cs=cs: e.tensor_tensor(out=y32[:, cs, :], in0=y32[:, cs, :],
                                                      in1=rt[:, :].unsqueeze(1).broadcast_to([128, 4, 512]), op=ALU.mult),
                      r=list(R_y[cs]) + [R_rt], w=list(R_y[cs]))
                S.dve(lambda e, cs=cs: e.tensor_tensor(out=xT[:, cs, blk], in0=xT[:, cs, blk], in1=y32[:, cs, :], op=ALU.add),
                      r=list(R_y[cs]) + [R_xT[c][n] for c in range(hf * 4, hf * 4 + 4)], w=[R_xT[c][n] for c in range(hf * 4, hf * 4 + 4)])

        def out_proj_residual(l, w_out_d):
            car, j = new_phase()
            y32s = [car.take([8, 512], F32) for _ in range(2)]
            prs = [car.take([512], F32) for _ in range(2)]
            R_ys = [ares(8), ares(8)]
            R_prs = ares(2)
            seed(R_ys[0] + R_ys[1] + R_prs, j)
            pend = [None]
            for n in range(NB):
                blk = slice(n * 512, (n + 1) * 512)
                y32, R_y = y32s[n % 2], R_ys[n % 2]
                for half in range(2):
                    slot = wload(w_out_d[:, half * 512:(half + 1) * 512], 8, 512)
                    for m4 in range(4):
                        m = half * 4 + m4
                        bk = m % 2
                        for c in range(8):
                            S.pe(lambda e, bk=bk, slot=slot, c=c, m4=m4: e.matmul(
                                banks[bk][:, :], lhsT=wbuf[slot][:, c, m4 * 128:(m4 + 1) * 128], rhs=mixT[:, c, blk],
                                start=(c == 0), stop=(c == 7)),
                                r=[R_w[slot], R_mix[c][n]], w=[R_bank[bk]])
                        if pend[0] is not None:
                            pend[0]()
                        pend[0] = evac_y(bk, y32, R_y, m, l, 1, 2)
                pend[0]()
                pend[0] = None
                post_rstd(2, (prs[n % 2], R_prs[n % 2]))
                if n > 0:
                    postnorm_residual(l, 1, n - 1, y32s[(n - 1) % 2], R_ys[(n - 1) % 2], 2, slot=(prs[(n - 1) % 2], R_prs[(n - 1) % 2]))
                if n == 2:
                    prenorm_block(l, 2, 0, 3)
            postnorm_residual(l, 1, NB - 1, y32s[(NB - 1) % 2], R_ys[(NB - 1) % 2], 2, slot=(prs[(NB - 1) % 2], R_prs[(NB - 1) % 2]))

        TL = {}
        TAIL = 7040

        def alloc_tail(j):
            tailc = Carver()
            tailc.off = ARENA_BYTES - TAIL
            bd = tailc.take([2, 4, 128], BF16)
            cw = tailc.take([16], F32)
            cbias = tailc.take([4], F32)
            lba = tailc.take([4], F32)
            lbx = tailc.take([4], F32)
            lam = tailc.take([4], F32)
            c8 = tailc.take([4], F32)
            c16 = tailc.take([4], F32)
            c8h = tailc.take([4], F32)
            lbah = tailc.take([4], F32)
            lbxh = tailc.take([4], F32)
            LN_HALF = tailc.take([1], F32)
            rel34 = tailc.take([8, 2, 128], BF16)
            cvec = tailc.take([8], F32)
            tail_res = [Res() for _ in range(9)]
            R_bd, R_cw, R_cbias, R_lba, R_lbx, R_lam, R_c8, R_rel34, R_cvec = tail_res
            seed(tail_res, j)
            S.dve(lambda e: e.memset(bd[:, :, :, :], 0.0), w=[R_bd])
            for wi, wd in enumerate((lwa_d, lwx_d)):
                for blk8 in range(8):
                    c = blk8 // 2
                    o = (blk8 % 2) * 64
                    S.dma(lambda e, wi=wi, wd=wd, blk8=blk8, c=c, o=o: e.dma_start(out=bd[o:o + 64, wi, c, o:o + 64], in_=wd[blk8, :, :]),
                          r=[], w=[R_bd], q="pool")
            S.dma(lambda e: e.dma_start(out=rel34[:, :, :, :], in_=rel34_d[:, :].rearrange("p (h a b) -> p h a b", h=8, a=2)),
                  w=[R_rel34], q="pool")
            S.dma(lambda e: e.dma_start(out=cvec, in_=cvec_d[:, :]), w=[R_cvec])
            S.dma(lambda e: e.dma_start(out=cw, in_=cw_d[:, :]), w=[R_cw])
            S.dma(lambda e: e.dma_start(out=cbias, in_=cb_d[:, :]), w=[R_cbias])
            S.dma(lambda e: e.dma_start(out=lba, in_=lba_d[:, :]), w=[R_lba])
            S.dma(lambda e: e.dma_start(out=lbx, in_=lbx_d[:, :]), w=[R_lbx])
            S.dma(lambda e: e.dma_start(out=lam, in_=lam_d[:, :]), w=[R_lam])
            TL.update(dict(bd=bd, cw=cw, cbias=cbias, lba=lba, lbx=lbx, lam=lam, c8=c8, c16=c16, c8h=c8h, lbah=lbah, lbxh=lbxh,
                           LN_HALF=LN_HALF, rel34=rel34, cvec=cvec, tail_res=tail_res))

        def mlp(l):
            car, j = new_phase()
            uT = mixT[:, :, :].rearrange("p c (a b) -> p (c a) b", b=512)
            y32s = [car.take([8, 512], F32) for _ in range(2)]
            prs1 = car.take([512], F32)
            prs = [prs1, prs1]
            rl = [ftmp[0][:, :], ftmp[1][:, :]]
            R_u = [R_mix[f // 4][f % 4] for f in range(32)]
            R_ys = [ares(8), ares(8)]
            R_prs1 = ares()
            R_prs = [R_prs1, R_prs1]
            R_rl = R_ftmp
            seed(R_ys[0] + R_ys[1] + [R_prs1], j)
            assert car.off <= ARENA_BYTES - TAIL, car.off

            def up(n):
                blk = slice(n * 512, (n + 1) * 512)
                for fg in range(8):
                    slot = wload_bf(l, fg)
                    for f4 in range(4):
                        f = fg * 4 + f4
                        bk = f % 2
                        for c in range(8):
                            S.pe(lambda e, bk=bk, slot=slot, c=c, f4=f4: e.matmul(
                                banks[bk][:, :], lhsT=wbuf[slot][:, c, f4 * 128:(f4 + 1) * 128], rhs=hT[:, c, blk],
                                start=(c == 0), stop=(c == 7)),
                                r=[R_w[slot], R_hT[c][n]], w=[R_bank[bk]])
                        S.act(lambda e, bk=bk: e.activation(out=rl[bk], in_=banks[bk][:, :], func=AF.Relu),
                              r=[R_bank[bk]], w=[R_rl[bk]])
                        S.dve(lambda e, bk=bk, f=f: e.tensor_tensor(out=uT[:, f, :], in0=rl[bk], in1=rl[bk], op=ALU.mult),
                              r=[R_rl[bk]], w=[R_u[f]])

            def down(n):
                y32, R_y = y32s[n % 2], R_ys[n % 2]
                pend_dn = None
                for mg in range(2):
                    for fg in range(4):
                        slot = wload_bf(l, 8 + mg * 4 + fg)
                        for m4 in range(4):
                            if mg == 1 and fg == 0 and m4 == 1 and pend_dn is not None:
                                pend_dn()
                                pend_dn = None
                            bk = 4 + m4
                            for f8 in range(8):
                                f = fg * 8 + f8
                                S.pe(lambda e, bk=bk, slot=slot, f8=f8, f=f, m4=m4, fg=fg: e.matmul(
                                    banks[bk][:, :], lhsT=wbuf[slot][:, f8, m4 * 128:(m4 + 1) * 128], rhs=uT[:, f, :],
                                    start=(fg == 0 and f8 == 0), stop=(fg == 3 and f8 == 7)),
                                    r=[R_w[slot], R_u[f]], w=[R_bank[bk]])
                    prev = None
                    for m4 in range(4):
                        m = mg * 4 + m4
                        bk = 4 + m4
                        cur = evac_y(bk, y32, R_y, m, l, 3, 3)
                        if prev is not None:
                            prev()
                        prev = cur
                    if mg == 0:
                        pend_dn = prev
                    else:
                        prev()
                post_rstd(3, (prs[n % 2], R_prs[n % 2]))

            def post(n):
                postnorm_residual(l, 3, n, y32s[n % 2], R_ys[n % 2], 3, slot=(prs[n % 2], R_prs[n % 2]))

            def next_prenorm(n):
                pass

            for n in range(NB):
                up(n)
                if n == 0 and l == 0:
                    alloc_tail(j)
                if n > 0:
                    post(n - 1)
                if n > 1:
                    next_prenorm(n - 2)
                if n + 1 < NB:
                    prenorm_block(l, 2, n + 1, 2)
                down(n)
            post(NB - 1)
            next_prenorm(NB - 2)
            next_prenorm(NB - 1)

        if stop_after >= 0.25:

            car, j = new_phase()
            wk = car.take([8, 256], BF16)
            qg = car.take([2, T], BF16)
            a_aug = car.take([T], BF16, parts=32)
            wa_f = ftmp[0][0:32, 0:256]
            wa_b = car.take([256], BF16, parts=32)
            gnw = car.take([4], F32)
            tmpE = [car.take([256], F32) for _ in range(2)]
            L_bf = [car.take([256], BF16) for _ in range(2)]
            kdec = [car.take([2, 2, 128], BF16) for _ in range(2)]
            v_bf = [car.take([512], BF16) for _ in range(2)]
            dec = car.take([2, 32], F32)
            S32 = car.take([2, 128], F32)
            S_bf = [car.take([2, 128], BF16) for _ in range(2)]
            o32s = [car.take([4, 512], F32) for _ in range(2)]
            gate = ftmp[0][:, :]
            t1 = ftmp[1][:, :]
            R_gate, R_t1 = R_ftmp
            R_wk, R_qg, R_aaug, R_wab, R_gnw, R_S32 = ares(6)
            R_waf = R_ftmp[0]
            R_decs = ares(NT)
            R_tmpE = ares(2); R_L = ares(2); R_kdec = ares(2); R_vbf = ares(2); R_Sbf = ares(2)
            R_o32s = [ares(4), ares(4)]
            seed([R_wk, R_qg, R_aaug, R_wab, R_gnw, R_S32], j)
            seed(R_decs + R_tmpE + R_L + R_kdec + R_vbf + R_Sbf + R_o32s[0] + R_o32s[1], j)

            wload(w_in0_d[:, 256:512], 8, 256, dst=wk[:, :, :], R_dst=R_wk)
            S.dma(lambda e: e.dma_start(out=wa_f[0:17, :], in_=wa_aug_d[:, :]), w=[R_waf])
            S.dma(lambda e: e.dma_start(out=gnw, in_=gnw_d[:, :]), w=[R_gnw])
            S.dve(lambda e: e.tensor_copy(out=wa_b[0:17, :], in_=wa_f[0:17, :]), r=[R_waf], w=[R_wab])
            S.pool(lambda e: e.memset(a_aug[0:32, :], 1.0), w=[R_aaug])
            S.pool(lambda e: e.memset(S32[:, :, :], 0.0), w=[R_S32])
            for _s in range(2):
                S.pool(lambda e, _s=_s: e.memset(kdec[_s][:, :, :, :], 0.0), w=[R_kdec[_s]])
            slot_q = wload(w_in0_d[:, 0:256], 8, 256)
            slot_a = wload(w_in0_d[:, 1536:1552], 8, 16)
            slot_r = wload(w_in0_d[:, 1024:1536], 8, 512)
            fw = car.take([8, 8], BF16)
            R_fw = ares()
            seed(R_fw, j)
            wload(w_in0_d[:, 3088:3096], 8, 8, dst=fw[:, :, :], R_dst=R_fw)
            early_sync = [op for op in S.q["sp"] if op.dma]
            n_early_conv = len(conv_ops)
            R_fbf = Res()
            R_ncum = Res()
            S.dma(lambda e: e.dma_start(out=fbf_p[:, :], in_=fbf_d[:, :]), w=[R_fbf])
            S.dve(lambda e: e.tensor_scalar(out=fbf_p[:, :], in0=fbf_p[:, :], scalar1=-1.0, scalar2=None, op0=ALU.mult), r=[R_fbf], w=[R_fbf])
            ones8 = cb16[0:8, 128:129].broadcast_to([8, 512])
            lf = rstd[1][0:8, :]
            R_lf = R_rstd[1]
            for _n in range(NB):
                S.dve(lambda e, _n=_n: e.memset(mixT[:, 7, _n * 512:(_n + 1) * 512], 0.0), w=[R_mix[7][_n]])
            prenorm_block(0, 0, 0, 5)
            pend_tr = []
            for n in range(NB):
                blk = slice(n * 512, (n + 1) * 512)
                for pr in range(2):
                    bk = pr
                    for c in range(8):
                        S.pe(lambda e, bk=bk, c=c, pr=pr: e.matmul(banks[bk][:, :], lhsT=wbuf[slot_q][:, c, pr * 128:(pr + 1) * 128],
                                                                   rhs=hT[:, c, blk], start=(c == 0), stop=(c == 7)),
                             r=[R_w[slot_q], R_hT[c][n]], w=[R_bank[bk]])
                for c in range(8):
                    S.pe(lambda e, c=c: e.matmul(banks[2][0:16, :], lhsT=wbuf[slot_a][:, c, 0:16], rhs=hT[:, c, blk],
                                                 start=(c == 0), stop=(c == 7)),
                         r=[R_w[slot_a], R_hT[c][n]], w=[R_bank[2]])
                while pend_tr:
                    pend_tr.pop(0)()
                if n + 1 < NB:
                    prenorm_block(0, 0, n + 1, 5 + ((n + 1) % 2))
                for pr in range(2):
                    bk = pr
                    S.act(lambda e, bk=bk, pr=pr: e.activation(out=qg[:, pr, blk], in_=banks[bk][:, :], func=AF.Copy, scale=0.125),
                          r=[R_bank[bk]], w=[R_qg])
                S.dve(lambda e: e.tensor_copy(out=a_aug[0:16, blk], in_=banks[2][0:16, :]), r=[R_bank[2]], w=[R_aaug])
                for hd in range(4):
                    bk = 3 + hd % 2
                    for c in range(8):
                        S.pe(lambda e, c=c, hd=hd, bk=bk: e.matmul(banks[bk][:, :], lhsT=wbuf[slot_r][:, c, hd * 128:(hd + 1) * 128],
                                                                   rhs=hT[:, c, blk], start=(c == 0), stop=(c == 7)),
                             r=[R_w[slot_r], R_hT[c][n]], w=[R_bank[bk]])
                    S.act(lambda e, hd=hd, bk=bk: e.activation(out=mixT[:, hd, blk], in_=banks[bk][:, :], func=AF.Silu),
                          r=[R_bank[bk]], w=[R_mix[hd][n]])
                cumb = ftmp[n % 2][0:8, :]
                for c in range(8):
                    S.pe(lambda e, c=c: e.matmul(banks[2][0:8, :], lhsT=fw[:, c, :], rhs=hT[:, c, blk], start=(c == 0), stop=(c == 7)),
                         r=[R_fw, R_hT[c][n]], w=[R_bank[2]])
                S.act(lambda e: e.activation(out=lf, in_=banks[2][0:8, :], func=AF.Exp, scale=-1.0, bias=fbf_p[:, :]),
                      r=[R_bank[2], R_fbf], w=[R_lf])
                S.act(lambda e: e.activation(out=lf, in_=lf, func=AF.Ln, bias=1.0), r=[R_lf], w=[R_lf])
                S.dve(lambda e: e.tensor_scalar(out=lf, in0=lf, scalar1=-1.0, scalar2=None, op0=ALU.mult), r=[R_lf], w=[R_lf])
                init = 0.0 if n == 0 else ftmp[(n - 1) % 2][0:8, 511:512]
                rinit = [] if n == 0 else [R_ftmp[(n - 1) % 2]]
                S.dve(lambda e: e.tensor_tensor_scan(out=cumb, data0=ones8, data1=lf, initial=init, op0=ALU.mult, op1=ALU.add),
                      r=[R_lf, R_cb] + rinit, w=[R_ftmp[n % 2]])
                S.act(lambda e: e.activation(out=mixT[64:72, 7, blk], in_=cumb, func=AF.Copy, scale=8.0),
                      r=[R_ftmp[n % 2]], w=[R_mix[7][n]])
                def cum_transposes(n=n, cumb=cumb):
                    for t4 in range(4):
                        tt = 4 * n + t4
                        S.pe(lambda e, tt=tt, t4=t4: e.transpose(out=banks[7][:, tt * 8:(tt + 1) * 8], in_=cumb[:, t4 * 128:(t4 + 1) * 128],
                                                                 identity=ident_f[0:8, 0:8]),
                             r=[R_ftmp[n % 2], R_cf], w=[R_bank[7]])
                pend_tr.append(cum_transposes)
            while pend_tr:
                pend_tr.pop(0)()
            S.act(lambda e: e.activation(out=ncum_p[:, :], in_=banks[7][:, 0:128], func=AF.Copy, scale=-1.0),
                  r=[R_bank[7]], w=[R_ncum])
            slot_v = wload(w_in0_d[:, 512:1024], 8, 512)

            def b1_k(tt):
                n = tt // 4
                tok = slice(tt * 128, (tt + 1) * 128)
                for c in range(8):
                    S.pe(lambda e, c=c: e.matmul(banks[0][:, 0:256], lhsT=hT[:, c, tok], rhs=wk[:, c, :],
                                                 start=(c == 0), stop=(c == 7)),
                         r=[R_hT[c][n], R_wk], w=[R_bank[0]])

            def b1_v(tt):
                n = tt // 4
                tok = slice(tt * 128, (tt + 1) * 128)
                for c in range(8):
                    S.pe(lambda e, c=c: e.matmul(banks[1][:, :], lhsT=hT[:, c, tok], rhs=wbuf[slot_v][:, c, :],
                                                 start=(c == 0), stop=(c == 7)),
                         r=[R_hT[c][n], R_w[slot_v]], w=[R_bank[1]])

            def b1_pre(tt):
                s = tt % 2
                tok = slice(tt * 128, (tt + 1) * 128)
                S.pe(lambda e: e.matmul(banks[0][:, 256:512], lhsT=a_aug[0:17, tok], rhs=wa_b[0:17, :], start=True, stop=True),
                     r=[R_aaug, R_wab], w=[R_bank[0]])
                S.act(lambda e: e.activation(out=tmpE[s], in_=banks[0][:, 256:512], func=AF.Exp, scale=-1.0),
                      r=[R_bank[0]], w=[R_tmpE[s]])
                S.act(lambda e: e.activation(out=L_bf[s], in_=tmpE[s], func=AF.Ln, bias=1.0),
                      r=[R_tmpE[s]], w=[R_L[s]])

            def b1_tri(tt):
                s = tt % 2
                S.pe(lambda e: e.matmul(banks[2][:, 0:256], lhsT=tri_b, rhs=L_bf[s], start=True, stop=True),
                     r=[R_L[s], R_cb], w=[R_bank[2]])
                for pr in range(2):
                    S.pe(lambda e, pr=pr: e.matmul(banks[2][:, 256 + 2 * pr:258 + 2 * pr], lhsT=L_bf[s][:, pr * 128:(pr + 1) * 128],
                                                   rhs=ind_b, start=True, stop=True),
                         r=[R_L[s], R_cb], w=[R_bank[2]])
                S.act(lambda e: e.activation(out=tmpE[s], in_=banks[2][:, 0:256], func=AF.Exp),
                      r=[R_bank[2]], w=[R_tmpE[s]])
                S.act(lambda e: e.activation(out=dec[:, :, 2 * tt:2 * tt + 2],
                                             in_=banks[2][:, 256:260].rearrange("p (a b) -> p a b", a=2), func=AF.Exp),
                      r=[R_bank[2]], w=[R_decs[tt]])
                for hh in range(2):
                    S.dve(lambda e, hh=hh: e.tensor_tensor(
                        out=kdec[s][:, :, hh, hh * 64:(hh + 1) * 64],
                        in0=banks[0][:, 0:256].rearrange("p (a b c) -> p a b c", a=2, b=2)[:, :, hh, :],
                        in1=tmpE[s].rearrange("p (a b c) -> p a b c", a=2, b=2)[:, :, hh, :], op=ALU.mult),
                        r=[R_bank[0], R_tmpE[s]], w=[R_kdec[s]])
                S.act(lambda e: e.activation(out=v_bf[s], in_=banks[1][:, :], func=AF.Copy),
                      r=[R_bank[1]], w=[R_vbf[s]])

            def b2_inc(tt, jc):
                s = tt % 2
                rows = slice(jc * 64, (jc + 1) * 64)
                bki = 3 + jc
                for pr in range(2):
                    for hh in range(2):
                        hd = 2 * pr + hh
                        S.pe(lambda e, pr=pr, hh=hh, hd=hd: e.matmul(
                            banks[bki][:, pr * 128:(pr + 1) * 128], lhsT=kdec[s][rows, pr, hh, :],
                            rhs=v_bf[s][rows, hd * 128:(hd + 1) * 128], start=(hh == 0), stop=(hh == 1)),
                            r=[R_kdec[s], R_vbf[s]], w=[R_bank[bki]])

            def b2_chain(tt, jc):
                cg = 2 * tt + jc
                ss = cg % 2
                bki = 3 + jc
                for pr in range(2):
                    S.dve(lambda e, pr=pr: e.scalar_tensor_tensor(
                        out=S32[:, pr, :], in0=S32[:, pr, :], scalar=dec[:, pr, cg:cg + 1],
                        in1=banks[bki][:, pr * 128:(pr + 1) * 128], op0=ALU.mult, op1=ALU.add),
                        r=[R_S32, R_decs[tt], R_bank[bki]], w=[R_S32])
                S.dve(lambda e: e.tensor_copy(out=S_bf[ss], in_=S32), r=[R_S32], w=[R_Sbf[ss]])

            def b2_o(tt, jc):
                cg = 2 * tt + jc
                ss = cg % 2
                for pr in range(2):
                    for hh in range(2):
                        pp = slice(hh * 64, (hh + 1) * 64)
                        S.pe(lambda e, pr=pr, hh=hh, pp=pp: e.matmul(
                            banks[5 + hh][:, pr * 128 + jc * 64:pr * 128 + (jc + 1) * 64], lhsT=S_bf[ss][pp, pr, :],
                            rhs=qg[pp, pr, cg * 64:(cg + 1) * 64], start=True, stop=True),
                            r=[R_Sbf[ss], R_qg], w=[R_bank[5 + hh]])

            def b2_out(tt):
                n = tt // 4
                t4 = tt % 4
                ob = n % 2
                for hh in range(2):
                    for pr in range(2):
                        hd = 2 * pr + hh
                        S.act(lambda e, hh=hh, pr=pr, hd=hd: e.activation(out=o32s[ob][:, hd, t4 * 128:(t4 + 1) * 128],
                                                                          in_=banks[5 + hh][:, pr * 128:(pr + 1) * 128], func=AF.Copy),
                              r=[R_bank[5 + hh]], w=[R_o32s[ob][hd]])

            def gla_fin(n, parts_only=False):
                blk = slice(n * 512, (n + 1) * 512)
                ob = n % 2
                o32, R_o32 = o32s[ob], R_o32s[ob]

                def sq(hd):
                    S.act(lambda e: e.activation(out=sqt[hd % 2][:], in_=o32[:, hd, :], func=AF.Square),
                          r=[R_o32[hd]], w=[R_sq[hd % 2]])

                def mm_rstd(hd):
                    sl = hd % 2
                    S.pe(lambda e: e.matmul(banks[2][:, :], lhsT=ones_b, rhs=sqt[hd % 2][:], start=True, stop=True),
                         r=[R_sq[hd % 2], R_cb], w=[R_bank[2]])
                    S.act(lambda e: e.activation(out=rstd[sl][:, :], in_=banks[2][:, :], func=AF.Ln, scale=1.0 / 128, bias=EPS),
                          r=[R_bank[2]], w=[R_rstd[sl]])
                    S.act(lambda e: e.activation(out=rstd[sl][:, :], in_=rstd[sl][:, :], func=AF.Exp, scale=-0.5),
                          r=[R_rstd[sl]], w=[R_rstd[sl]])

                def apply(hd):
                    sl = hd % 2
                    tq = ftmp[hd % 2]
                    S.dve(lambda e: e.scalar_tensor_tensor(out=tq[:, :], in0=o32[:, hd, :], scalar=gnw[:, hd:hd + 1],
                                                           in1=rstd[sl][:], op0=ALU.mult, op1=ALU.mult),
                          r=[R_o32[hd], R_gnw, R_rstd[sl]], w=[R_ftmp[hd % 2]])
                    S.dve(lambda e: e.tensor_tensor(out=mixT[:, hd, blk], in0=tq[:, :], in1=mixT[:, hd, blk], op=ALU.mult),
                          r=[R_ftmp[hd % 2], R_mix[hd][n]], w=[R_mix[hd][n]])

                if parts_only:
                    return sq, mm_rstd, apply
                sq(0); sq(1)
                mm_rstd(0)
                sq(2)
                mm_rstd(1)
                apply(0)
                sq(3)
                mm_rstd(2)
                apply(1)
                mm_rstd(3)
                apply(2)
                apply(3)

            b1_k(0); b1_v(0); b1_pre(0); b1_tri(0)
            for tt in range(NT):
                nxt = tt + 1 < NT
                fin = gla_fin(tt // 4 - 1, parts_only=True) if tt >= 4 else None
                fh = tt % 4
                b2_inc(tt, 0)
                b2_inc(tt, 1)
                if fin:
                    fin[0](fh)
                if nxt:
                    b1_k(tt + 1)
                    b1_pre(tt + 1)
                if fin:
                    fin[1](fh)
                b2_chain(tt, 0)
                b2_chain(tt, 1)
                if fin:
                    fin[2](fh)
                if nxt:
                    b1_v(tt + 1)
                    b1_tri(tt + 1)
                b2_o(tt, 0)
                b2_o(tt, 1)
                b2_out(tt)
            gla_fin(NB - 1)

        if stop_after >= 0.8:
            car, j = new_phase()
            qa = car.take([2, T], BF16, parts=65)
            ka = car.take([2, T], BF16, parts=65)
            va = car.take([NT, 2, 128], BF16)
            PT = [car.take([512], BF16) for _ in range(3)]
            rden = ftmp[0][0:64, :]
            R_rden = R_ftmp[0]
            ncum = ncum_p[:, :].rearrange("p (a b) -> p a b", a=NT)
            R_PT = ares(3)
            seed(R_PT, j)

            R_qab, R_kab, R_vab = ares(NB), ares(NB), ares(NB)
            seed(R_qab + R_kab + R_vab, j)
            S.pool(lambda e: e.memset(va[:, :, 0, 64:128], 1.0), w=R_vab)
            S.pool(lambda e: e.memset(va[:, :, 1, 0:64], 1.0), w=R_vab)
            S.pool(lambda e: e.memset(ka[64:65, :, :], 1.0), w=R_kab)
            slot_qk = wload(w_in0_d[:, 1552:2064], 8, 512)
            slot_k = wload(w_in0_d[:, 2064:2576], 8, 512)
            slot_v = wload(w_in0_d[:, 2576:3088], 8, 512)

            def fox_inproj(hp, n):
                wc = slice(hp * 128, (hp + 1) * 128)
                blk = slice(n * 512, (n + 1) * 512)
                for (slot, dstt, R_dst, bk) in ((slot_qk, qa, R_qab[n], 0), (slot_k, ka, R_kab[n], 1)):
                    for c in range(8):
                        S.pe(lambda e, c=c, slot=slot, bk=bk: e.matmul(banks[bk][:, :], lhsT=wbuf[slot][:, c, wc], rhs=hT[:, c, blk],
                                                                      start=(c == 0), stop=(c == 7)),
                             r=[R_w[slot], R_hT[c][n]], w=[R_bank[bk]])
                    S.dve(lambda e, dstt=dstt, bk=bk: e.tensor_copy(out=dstt[0:64, 0, blk], in_=banks[bk][0:64, :]),
                          r=[R_bank[bk]], w=[R_dst])
                    S.dve(lambda e, dstt=dstt, bk=bk: e.tensor_copy(out=dstt[0:64, 1, blk], in_=banks[bk][64:128, :]),
                          r=[R_bank[bk]], w=[R_dst])
                for t4 in range(4):
                    tt = 4 * n + t4
                    tok = slice(tt * 128, (tt + 1) * 128)
                    for c in range(8):
                        S.pe(lambda e, c=c, t4=t4: e.matmul(banks[2][:, t4 * 128:(t4 + 1) * 128], lhsT=hT[:, c, tok], rhs=wbuf[slot_v][:, c, wc],
                                                            start=(c == 0), stop=(c == 7)),
                             r=[R_w[slot_v], R_hT[c][n]], w=[R_bank[2]])
                S.dve(lambda e: e.tensor_copy(
                    out=va[:, 4 * n:4 * n + 4, 0, 0:64],
                    in_=banks[2][:, :].rearrange("p (a b c) -> p a b c", a=4, b=2)[:, :, 0, :]),
                    r=[R_bank[2]], w=[R_vab[n]])
                S.dve(lambda e: e.tensor_copy(
                    out=va[:, 4 * n:4 * n + 4, 1, 64:128],
                    in_=banks[2][:, :].rearrange("p (a b c) -> p a b c", a=4, b=2)[:, :, 1, :]),
                    r=[R_bank[2]], w=[R_vab[n]])
                for hl in range(2):
                    hd = 2 * hp + hl
                    S.pe(lambda e, hd=hd, hl=hl: e.matmul(banks[hl][0:65, :], lhsT=ident_b[:, hd:hd + 65], rhs=mixT[:, 7, blk], start=True, stop=True),
                         r=[R_cb, R_mix[7][n]], w=[R_bank[hl]])
                    S.dve(lambda e, hl=hl: e.tensor_copy(out=qa[64:65, hl, blk], in_=banks[hl][64:65, :]),
                          r=[R_bank[hl]], w=[R_qab[n]])

            def fox_attn(hp, hl, qb):
                hd = 2 * hp + hl
                qs = qb * 512
                obk = 6 + hl
                nkt = 4 * (qb + 1)

                def qk_step(kt):
                    jd = kt - 4 * qb
                    c0 = 128 * jd if jd > 0 else 0
                    sb_i = kt % 3
                    sbk = 3 + sb_i
                    diag = jd >= 0
                    S.pe(lambda e: e.matmul(
                        banks[sbk][:, c0:512], lhsT=ka[0:65, hl, kt * 128:(kt + 1) * 128],
                        rhs=qa[0:65, hl, qs + c0:qs + 512], start=True, stop=(not diag)),
                        r=[R_kab[kt // 4], R_qab[qb]], w=[R_bank[sbk]])
                    if diag:
                        S.pe(lambda e: e.matmul(banks[sbk][:, c0:c0 + 128], lhsT=ident_b, rhs=trimask_b, start=False, stop=True),
                             r=[R_cb], w=[R_bank[sbk]])
                    S.act(lambda e: e.activation(
                        out=PT[sb_i][:, c0:512], in_=banks[sbk][:, c0:512], func=AF.Exp, scale=0.125,
                        bias=ncum[:, kt, hd:hd + 1]),
                        r=[R_bank[sbk], R_ncum], w=[R_PT[sb_i]])

                def pv_step(kt):
                    jd = kt - 4 * qb
                    c0 = 128 * jd if jd > 0 else 0
                    sb_i = kt % 3
                    S.pe(lambda e: e.matmul(
                        banks[obk][:, c0:512], lhsT=va[:, kt, hl, :], rhs=PT[sb_i][:, c0:512],
                        start=(kt == 0), stop=(kt == nkt - 1)),
                        r=[R_vab[kt // 4], R_PT[sb_i]], w=[R_bank[obk]])

                LA = 2
                for i in range(nkt + LA):
                    if i < nkt:
                        qk_step(i)
                    if i >= LA:
                        pv_step(i - LA)
                po = slice(hl * 64, (hl + 1) * 64)
                pd = slice((1 - hl) * 64, (2 - hl) * 64)
                S.act(lambda e: e.activation(out=ftmp[hl][po, :], in_=banks[obk][pd, :], func=AF.Ln), r=[R_bank[obk]], w=[R_ftmp[hl]])
                S.act(lambda e: e.activation(out=ftmp[hl][po, :], in_=ftmp[hl][po, :], func=AF.Exp, scale=-1.0), r=[R_ftmp[hl]], w=[R_ftmp[hl]])
                S.dve(lambda e: e.tensor_tensor(
                    out=mixT[po, 4 + hp, qs:qs + 512], in0=banks[obk][po, :], in1=ftmp[hl][po, :], op=ALU.mult),
                    r=[R_bank[obk], R_ftmp[hl]], w=[R_mix[4 + hp][qb]])

            for hp in range(4):
                fox_inproj(hp, 0)
                for qb in range(NB):
                    if qb + 1 < NB:
                        fox_inproj(hp, qb + 1)
                    for hl in range(2):
                        fox_attn(hp, hl, qb)

        if stop_after >= 1:
            out_proj_residual(0, w_out0_d)
        if stop_after >= 2:
            mlp(0)

        if stop_after >= 3:
            car, j = new_phase()
            qT2 = car.take([2, T], BF16)
            kT2 = car.take([T], BF16)
            va = car.take([NT, 2, 128], BF16)
            cstA = car.take([8, 128], BF16)
            cst0 = car.take([8, 128], BF16)
            PT5 = [car.take([640], BF16) for _ in range(2)]
            R_cst = ares()
            R_PT5 = ares(2)
            seed([R_cst] + R_PT5, j)
            assert car.off <= ARENA_BYTES - TAIL, car.off
            for n in range(NB):
                prenorm_block(1, 0, n, 4 + (n % 2), use_pool=True)
            bd, cw, cbias, lba, lbx, lam, c8, c16, c8h, lbah, lbxh, LN_HALF, rel34, cvec, tail_res = [TL[k] for k in (
                "bd", "cw", "cbias", "lba", "lbx", "lam", "c8", "c16", "c8h", "lbah", "lbxh", "LN_HALF", "rel34", "cvec", "tail_res")]
            R_bd, R_cw, R_cbias, R_lba, R_lbx, R_lam, R_c8, R_rel34, R_cvec = tail_res
            S.dve(lambda e: e.memset(rel34[64:128, :, 1, 0:64], NEG), r=[R_rel34], w=[R_rel34])
            S.act(lambda e: e.activation(out=c8, in_=lam, func=AF.Exp, scale=-1.0), r=[R_lam], w=[R_c8])
            S.act(lambda e: e.activation(out=c8, in_=c8, func=AF.Ln, bias=1.0), r=[R_c8], w=[R_c8])
            S.dve(lambda e: e.tensor_scalar(out=c16, in0=c8, scalar1=-16.0, scalar2=None, op0=ALU.mult), r=[R_c8], w=[R_c8])
            S.dve(lambda e: e.tensor_scalar(out=c8h, in0=c8, scalar1=-4.0, scalar2=None, op0=ALU.mult), r=[R_c8], w=[R_c8])
            S.dve(lambda e: e.tensor_scalar(out=c8, in0=c8, scalar1=-8.0, scalar2=None, op0=ALU.mult), r=[R_c8], w=[R_c8])
            S.dve(lambda e: e.memset(LN_HALF, -0.6931471805599453), w=[R_c8])
            S.dve(lambda e: e.tensor_scalar(out=lbah, in0=lba, scalar1=0.5, scalar2=None, op0=ALU.mult), r=[R_lba], w=[R_lba])
            S.dve(lambda e: e.tensor_scalar(out=lbxh, in0=lbx, scalar1=0.5, scalar2=None, op0=ALU.mult), r=[R_lbx], w=[R_lbx])
            S.dve(lambda e: e.memset(cst0[:, 0, :], 0.0), w=[R_cst])
            S.dve(lambda e: e.memset(cst0[0:64, 0, 64:128], NEG), r=[R_cst], w=[R_cst])
            for hd in range(8):
                S.dve(lambda e, hd=hd: e.tensor_scalar(out=rel34[:, hd, 0, :], in0=rel34[:, hd, 0, :], scalar1=cvec[:, hd:hd + 1],
                                                       scalar2=None, op0=ALU.subtract),
                      r=[R_rel34, R_cvec], w=[R_rel34])
            R_q2b, R_k2b, R_v2b = ares(NB), ares(NB), ares(NB)
            seed(R_q2b + R_k2b + R_v2b, j)
            S.add("pool", lambda e: e.memset(va[:, :, 0, 64:128], 1.0), (), R_v2b)
            S.add("pool", lambda e: e.memset(va[:, :, 1, 0:64], 1.0), (), R_v2b)
            S.add("pool", lambda e: e.memset(qT2[:, :, :], 0.0), (), R_q2b)
            slot_q = wload(w_in1_d[:, 0:512], 8, 512)
            slot_k = wload(w_in1_d[:, 512:1024], 8, 512)
            slot_v = wload(w_in1_d[:, 1024:1536], 8, 512)

            def ca_inproj(hp, n):
                wc = slice(hp * 128, (hp + 1) * 128)
                blk = slice(n * 512, (n + 1) * 512)
                for c in range(8):
                    S.pe(lambda e, c=c: e.matmul(banks[0][:, :], lhsT=wbuf[slot_q][:, c, wc], rhs=hT[:, c, blk], start=(c == 0), stop=(c == 7)),
                         r=[R_w[slot_q], R_hT[c][n]], w=[R_bank[0]])
                S.dve(lambda e: e.tensor_scalar(out=qT2[0:64, 0, blk], in0=banks[0][0:64, :], scalar1=0.125, scalar2=None, op0=ALU.mult),
                      r=[R_bank[0]], w=[R_q2b[n]])
                S.dve(lambda e: e.tensor_scalar(out=qT2[64:128, 1, blk], in0=banks[0][64:128, :], scalar1=0.125, scalar2=None, op0=ALU.mult),
                      r=[R_bank[0]], w=[R_q2b[n]])
                for c in range(8):
                    S.pe(lambda e, c=c: e.matmul(banks[1][:, :], lhsT=wbuf[slot_k][:, c, wc], rhs=hT[:, c, blk], start=(c == 0), stop=(c == 7)),
                         r=[R_w[slot_k], R_hT[c][n]], w=[R_bank[1]])
                S.dve(lambda e: e.tensor_copy(out=kT2[:, blk], in_=banks[1][:, :]), r=[R_bank[1]], w=[R_k2b[n]])
                for t4 in range(4):
                    tt = 4 * n + t4
                    tok = slice(tt * 128, (tt + 1) * 128)
                    for c in range(8):
                        S.pe(lambda e, c=c, t4=t4: e.matmul(banks[0][:, t4 * 128:(t4 + 1) * 128], lhsT=hT[:, c, tok], rhs=wbuf[slot_v][:, c, wc],
                                                            start=(c == 0), stop=(c == 7)),
                             r=[R_w[slot_v], R_hT[c][n]], w=[R_bank[0]])
                S.dve(lambda e: e.tensor_copy(
                    out=va[:, 4 * n:4 * n + 4, 0, 0:64],
                    in_=banks[0][:, :].rearrange("p (a b c) -> p a b c", a=4, b=2)[:, :, 0, :]),
                    r=[R_bank[0]], w=[R_v2b[n]])
                S.dve(lambda e: e.tensor_copy(
                    out=va[:, 4 * n:4 * n + 4, 1, 64:128],
                    in_=banks[0][:, :].rearrange("p (a b c) -> p a b c", a=4, b=2)[:, :, 1, :]),
                    r=[R_bank[0]], w=[R_v2b[n]])

            for hp in range(4):
                steps = [(hl, jq) for jq in range(NT) for hl in range(2)]

                def ca_qk(it):
                    hl, jq = steps[it]
                    hd = 2 * hp + hl
                    qsl = slice(jq * 128, (jq + 1) * 128)
                    par = it % 2
                    bA = 3 if par == 0 else 5
                    bB = 4 if par == 0 else 6
                    idxs = [i for i in range(5) if jq - 4 + i >= 0]
                    for idx in idxs:
                        kt = jq - 4 + idx
                        bk, col = (bA, idx * 128) if idx < 4 else (bB, 0)
                        nob = idx in (1, 2)
                        S.pe(lambda e, bk=bk, col=col, kt=kt, nob=nob: e.matmul(
                            banks[bk][:, col:col + 128], lhsT=kT2[:, kt * 128:(kt + 1) * 128], rhs=qT2[:, hl, qsl],
                            start=True, stop=nob),
                            r=[R_k2b[kt // 4], R_q2b[jq // 4]], w=[R_bank[bk]])
                        if not nob:
                            brhs = (cst0[:, 0, :], None, None, rel34[:, hd, 0, :], rel34[:, hd, 1, :])[idx]
                            S.pe(lambda e, bk=bk, col=col, brhs=brhs: e.matmul(
                                banks[bk][:, col:col + 128], lhsT=ident_b, rhs=brhs, start=False, stop=True),
                                r=[R_cb, R_cst, R_rel34], w=[R_bank[bk]])
                    i0 = idxs[0]
                    if i0 < 4:
                        S.act(lambda e: e.activation(out=PT5[par][:, i0 * 128:512], in_=banks[bA][:, i0 * 128:512], func=AF.Exp,
                                                     bias=cvec[:, hd:hd + 1]),
                              r=[R_bank[bA], R_cvec], w=[R_PT5[par]])
                    S.act(lambda e: e.activation(out=PT5[par][:, 512:640], in_=banks[bB][:, 0:128], func=AF.Exp),
                          r=[R_bank[bB]], w=[R_PT5[par]])

                def ca_pv(it):
                    hl, jq = steps[it]
                    par = it % 2
                    jq4 = jq % 4
                    obk = 7 if hl == 0 else 2
                    idxs = [i for i in range(5) if jq - 4 + i >= 0]
                    for ii, idx in enumerate(idxs):
                        kt = jq - 4 + idx
                        S.pe(lambda e, kt=kt, idx=idx, ii=ii, last=(idx == 4): e.matmul(
                            banks[obk][:, jq4 * 128:(jq4 + 1) * 128], lhsT=va[:, kt, hl, :], rhs=PT5[par][:, idx * 128:(idx + 1) * 128],
                            start=(ii == 0), stop=last),
                            r=[R_v2b[kt // 4], R_PT5[par]], w=[R_bank[obk]])
                    if jq4 == 3:
                        nq = jq // 4
                        po = slice(hl * 64, (hl + 1) * 64)
                        pd = slice((1 - hl) * 64, (2 - hl) * 64)
                        S.act(lambda e: e.activation(out=ftmp[hl][po, :], in_=banks[obk][pd, :], func=AF.Ln), r=[R_bank[obk]], w=[R_ftmp[hl]])
                        S.act(lambda e: e.activation(out=ftmp[hl][po, :], in_=ftmp[hl][po, :], func=AF.Exp, scale=-1.0), r=[R_ftmp[hl]], w=[R_ftmp[hl]])
                        S.dve(lambda e: e.tensor_tensor(
                            out=mixT[po, hp, nq * 512:(nq + 1) * 512], in0=banks[obk][po, :], in1=ftmp[hl][po, :], op=ALU.mult),
                            r=[R_bank[obk], R_ftmp[hl]], w=[R_mix[hp][nq]])

                ca_inproj(hp, 0)
                ca_qk(0)
                for it in range(len(steps)):
                    if it % 8 == 0 and it // 8 + 1 < NB:
                        ca_inproj(hp, it // 8 + 1)
                    if it + 1 < len(steps):
                        ca_qk(it + 1)
                    ca_pv(it)

            car, j = new_phase()
            arena_res.extend(tail_res)
            XH = [car.take([3 + 512], F32) for _ in range(3)]
            XC_ = [car.take([512], F32) for _ in range(2)]
            XCb_ = [car.take([512], BF16) for _ in range(2)]
            RR_ = [car.take([512], F32) for _ in range(2)]
            II_ = [car.take([512], F32) for _ in range(2)]
            AA_ = [car.take([512], F32) for _ in range(2)]
            MM_ = [car.take([512], F32) for _ in range(2)]
            HH = [car.take([512], F32) for _ in range(2)]
            R_XC_, R_XCb_, R_RR_, R_II_, R_AA_, R_MM_ = ares(2), ares(2), ares(2), ares(2), ares(2), ares(2)
            R_XH = ares(3); R_HH = ares(2)
            seed(R_XC_ + R_XCb_ + R_RR_ + R_II_ + R_AA_ + R_MM_ + R_XH + R_HH, j)
            assert car.off <= ARENA_BYTES - TAIL, car.off
            slot_g = wload(w_in1_d[:, 1536:2048], 8, 512)
            slot_x = wload(w_in1_d[:, 2048:2560], 8, 512)
            lsteps = [(c, n) for c in range(4) for n in range(NB)]

            def lru_sets(it):
                s = it % 2
                return (s, XC_[s], XCb_[s], RR_[s], II_[s], AA_[s], MM_[s],
                        R_XC_[s], R_XCb_[s], R_RR_[s], R_II_[s], R_AA_[s], R_MM_[s],
                        (0, 1, 2, 3) if s == 0 else (4, 5, 6, 7))

            def lru_front_a(it):
                c, n = lsteps[it]
                blk = slice(n * 512, (n + 1) * 512)
                s3 = it % 3
                p3 = (it - 1) % 3
                b0, b1 = (0, 1) if it % 2 == 0 else (4, 5)
                for kc in range(8):
                    S.pe(lambda e, kc=kc: e.matmul(banks[b0][:, :], lhsT=wbuf[slot_g][:, kc, c * 128:(c + 1) * 128], rhs=hT[:, kc, blk],
                                                   start=(kc == 0), stop=(kc == 7)),
                         r=[R_w[slot_g], R_hT[kc][n]], w=[R_bank[b0]])
                for kc in range(8):
                    S.pe(lambda e, kc=kc: e.matmul(banks[b1][:, :], lhsT=wbuf[slot_x][:, kc, c * 128:(c + 1) * 128], rhs=hT[:, kc, blk],
                                                   start=(kc == 0), stop=(kc == 7)),
                         r=[R_w[slot_x], R_hT[kc][n]], w=[R_bank[b1]])
                S.act(lambda e: e.activation(out=mixT[:, 4 + c, blk], in_=banks[b0][:, :], func=AF.Gelu_apprx_tanh),
                      r=[R_bank[b0]], w=[R_mix[4 + c][n]])
                if n == 0:
                    S.dve(lambda e: e.memset(XH[s3][:, 0:3], 0.0), w=[R_XH[s3]])
                else:
                    S.dve(lambda e: e.tensor_copy(out=XH[s3][:, 0:3], in_=XH[p3][:, 512:515]), r=[R_XH[p3]], w=[R_XH[s3]])
                S.act(lambda e: e.activation(out=XH[s3][:, 3:515], in_=banks[b1][:, :], func=AF.Copy), r=[R_bank[b1]], w=[R_XH[s3]])

            def lru_front_b(it):
                c, n = lsteps[it]
                s3 = it % 3
                s, XC, XCb, RR, II, AA, MM, R_XC, R_XCb, R_RR, R_II, R_AA, R_MM, (b0, b1, b2, b3) = lru_sets(it)
                S.dve(lambda e: e.tensor_scalar(out=XC, in0=XH[s3][:, 3:515], scalar1=cw[:, c * 4 + 3:c * 4 + 4],
                                                scalar2=cbias[:, c:c + 1], op0=ALU.mult, op1=ALU.add),
                      r=[R_XH[s3], R_cw, R_cbias], w=[R_XC])
                for jt in range(3):
                    S.dve(lambda e, jt=jt: e.scalar_tensor_tensor(out=XC, in0=XH[s3][:, jt:jt + 512], scalar=cw[:, c * 4 + jt:c * 4 + jt + 1],
                                                                  in1=XC, op0=ALU.mult, op1=ALU.add),
                          r=[R_XH[s3], R_cw, R_XC], w=[R_XC])
                S.dve(lambda e: e.tensor_copy(out=XCb, in_=XC), r=[R_XC], w=[R_XCb])
                S.pe(lambda e: e.matmul(banks[b2][:, :], lhsT=bd[:, 0, c, :], rhs=XCb, start=True, stop=True), r=[R_bd, R_XCb], w=[R_bank[b2]])
                S.pe(lambda e: e.matmul(banks[b3][:, :], lhsT=bd[:, 1, c, :], rhs=XCb, start=True, stop=True), r=[R_bd, R_XCb], w=[R_bank[b3]])

            def lru_back(it):
                c, n = lsteps[it]
                blk = slice(n * 512, (n + 1) * 512)
                s3 = it % 3
                s, XC, XCb, RR, II, AA, MM, R_XC, R_XCb, R_RR, R_II, R_AA, R_MM, (b0, b1, b2, b3) = lru_sets(it)
                S.act(lambda e: e.activation(out=RR, in_=banks[b2][:, :], func=AF.Tanh, scale=0.5, bias=lbah[:, c:c + 1]), r=[R_bank[b2], R_lba], w=[R_RR])
                S.act(lambda e: e.activation(out=II, in_=banks[b3][:, :], func=AF.Tanh, scale=0.5, bias=lbxh[:, c:c + 1]), r=[R_bank[b3], R_lbx], w=[R_II])
                S.act(lambda e: e.activation(out=AA, in_=RR, func=AF.Exp, scale=c8h[:, c:c + 1], bias=c8h[:, c:c + 1]), r=[R_RR, R_c8], w=[R_AA])
                S.act(lambda e: e.activation(out=MM, in_=RR, func=AF.Exp, scale=c8[:, c:c + 1], bias=c8[:, c:c + 1]), r=[R_RR, R_c8], w=[R_MM])
                S.act(lambda e: e.activation(out=MM, in_=MM, func=AF.Ln, scale=-1.0, bias=1.0), r=[R_MM], w=[R_MM])
                S.act(lambda e: e.activation(out=MM, in_=MM, func=AF.Exp, scale=0.5, bias=LN_HALF[:, :]), r=[R_MM, R_c8], w=[R_MM])
                S.dve(lambda e: e.scalar_tensor_tensor(out=II, in0=II, scalar=1.0, in1=XC, op0=ALU.add, op1=ALU.mult), r=[R_II, R_XC], w=[R_II])
                S.dve(lambda e: e.tensor_tensor(out=MM, in0=MM, in1=II, op=ALU.mult), r=[R_II, R_MM], w=[R_MM])
                init = 0.0 if n == 0 else HH[1 - s][:, 511:512]
                rinit = [] if n == 0 else [R_HH[1 - s]]
                S.dve(lambda e: e.tensor_tensor_scan(out=HH[s], data0=AA, data1=MM, initial=init, op0=ALU.mult, op1=ALU.add),
                      r=[R_AA, R_MM] + rinit, w=[R_HH[s]])
                S.dve(lambda e: e.tensor_tensor(out=mixT[:, 4 + c, blk], in0=HH[s], in1=mixT[:, 4 + c, blk], op=ALU.mult),
                      r=[R_HH[s], R_mix[4 + c][n]], w=[R_mix[4 + c][n]])

            NL = len(lsteps)
            lru_front_a(0)
            lru_front_a(1)
            lru_front_b(0)
            for it in range(NL):
                if it + 2 < NL:
                    lru_front_a(it + 2)
                if it + 1 < NL:
                    lru_front_b(it + 1)
                lru_back(it)

            out_proj_residual(1, w_out1_d)
        if stop_after >= 4:
            mlp(1)

        if abs(stop_after - 0.9) < 1e-6 or abs(stop_after - 2.9) < 1e-6:
            for c in range(8):
                for n in range(NB):
                    blk = slice(n * 512, (n + 1) * 512)
                    S.dve(lambda e, c=c, blk=blk: e.tensor_copy(out=xT[:, c, blk], in_=mixT[:, c, blk]), r=[R_mix[c][n]], w=[R_xT[c][n]])
        car, j = new_phase()
        NYS = 8
        ys = [car.take([D], F32) for _ in range(NYS)]
        R_ys = ares(NYS)
        seed(R_ys, j)
        for tt in range(NT):
            s = tt % NYS
            n = tt // 4
            for half in range(2):
                bk = 2 * (tt % 2) + half
                for c4 in range(4):
                    c = half * 4 + c4
                    S.pe(lambda e, bk=bk, c4=c4, c=c, tt=tt: e.transpose(
                        out=banks[bk][:, c4 * 128:(c4 + 1) * 128], in_=xT[:, c, tt * 128:(tt + 1) * 128], identity=ident_f),
                        r=[R_xT[c][n], R_cf], w=[R_bank[bk]])
                if half == 0:
                    S.act(lambda e, bk=bk, s=s: e.activation(out=ys[s][:, 0:512], in_=banks[bk][:, :], func=AF.Copy),
                          r=[R_bank[bk]], w=[R_ys[s]])
                else:
                    S.dve(lambda e, bk=bk, s=s: e.tensor_copy(out=ys[s][:, 512:1024], in_=banks[bk][:, :]),
                          r=[R_bank[bk]], w=[R_ys[s]])
            S.dma(lambda e, s=s, tt=tt: e.dma_start(out=out_d[tt * 128:(tt + 1) * 128, :], in_=ys[s]), r=[R_ys[s]])

        for op in conv_ops[n_early_conv:]:
            op.deps.extend(early_sync)
        npre = 1
        S.q["pool"] = S.q["pool"][:npre] + conv_ops + S.q["pool"][npre:]
        S.emit()
    return nc


def _consts():
    c = np.zeros((128, 520), np.float32)
    c[:, 0:128] = np.eye(128, dtype=np.float32)
    c[:, 128:256] = 1.0
    s = np.arange(128)[:, None]
    t = np.arange(128)[None, :]
    c[:, 256:384] = np.where((s // 64 == t // 64) & (s > t), -1.0 / 16.0, 0.0)
    c[:, 384:512] = np.where(s > t, NEG, 0.0)
    c[:, 512:514] = np.where(s // 64 == np.arange(2)[None, :], -1.0 / 16.0, 0.0)
    return c


def _pc(v, nchunk):
    return np.ascontiguousarray(np.asarray(v, np.float32).reshape(nchunk, 128).T)


def _layout_inputs(inp):
    f = lambda a: np.ascontiguousarray(np.asarray(a, np.float32))
    nw = f(inp["norm_w"]).reshape(2, 4, 8, 128).transpose(3, 0, 1, 2).reshape(128, 64)
    k = np.arange(128)[:, None, None]
    idx = np.arange(5)[None, :, None]
    q = np.arange(128)[None, None, :]
    d = 512 - 128 * idx + q - k
    gidx = np.clip(d, -128, 128) + 128
    rb = f(inp["rel_bias"])[0]
    relT = rb[:, gidx]
    rel34 = np.ascontiguousarray(relT[:, :, 3:5, :].transpose(1, 0, 2, 3)).reshape(128, 8 * 2 * 128)
    cvec = np.ascontiguousarray(np.broadcast_to(rb[:, 256][None, :], (128, 8)))
    cw = f(inp["conv_w"])[0]
    cwl = np.ascontiguousarray(cw.reshape(4, 4, 128).transpose(2, 1, 0)).reshape(128, 16)
    shared = {
        "nw": np.ascontiguousarray(nw),
        "consts": _consts(),
        "w_in0": f(inp["w_in_even"])[0],
        "wa_aug": np.ascontiguousarray(np.concatenate([f(inp["gla_w_a_up"])[0], f(inp["gla_b_a"])[0][None, :]], axis=0)),
        "gnw": _pc(f(inp["gla_norm_w"])[0], 4),
        "fbf": f(inp["fox_b_f"])[0].reshape(8, 1),
        "w_out0": f(inp["w_out_even"])[0],
        "w_in1": f(inp["w_in_odd"])[0],
        "rel34": rel34,
        "cvec": cvec,
        "cw": cwl,
        "cb": _pc(f(inp["conv_b"])[0], 4),
        "lwa": f(inp["lru_w_a"])[0],
        "lwx": f(inp["lru_w_x"])[0],
        "lba": _pc(f(inp["lru_b_a"])[0], 4),
        "lbx": _pc(f(inp["lru_b_x"])[0], 4),
        "lam": _pc(f(inp["lru_lambda"])[0], 4),
        "w_out1": f(inp["w_out_odd"])[0],
        "w_up": f(inp["w_mlp_up"]),
        "w_dn": f(inp["w_mlp_down"]),
    }
    x = f(inp["x"])
    return [dict(shared, x=np.ascontiguousarray(x[b])) for b in range(x.shape[0])]


_NC_CACHE = {}


def kernel(**inputs):
    stop_after = float(inputs.pop("_stop_after", 99))
    ncores = int(inputs.pop("_ncores", 8))
    if stop_after not in _NC_CACHE:
        _NC_CACHE[stop_after] = build_program(stop_after)
    nc = _NC_CACHE[stop_after]
    in_maps = _layout_inputs(inputs)[:ncores]
    res = run_bass_kernel_spmd(nc, in_maps, core_ids=list(range(ncores)))
    return np.stack([np.asarray(r["out"], np.float32) for r in res.results], axis=0)
```

```python
from contextlib import ExitStack
import numpy as np
import concourse.bass as bass
import concourse.mybir as mybir
from concourse.bass_utils import run_bass_kernel_spmd

F32 = mybir.dt.float32
BF16 = mybir.dt.bfloat16
AF = mybir.ActivationFunctionType
ALU = mybir.AluOpType

T = 2048
D = 1024
NB = 4
NT = 16
EPS = 1e-6
NEG = -30000.0


import types


def _freeze(fn):
    if fn.__closure__ is None:
        return fn
    cells = []
    for c in fn.__closure__:
        try:
            cells.append(types.CellType(c.cell_contents))
        except ValueError:
            cells.append(c)
    return types.FunctionType(fn.__code__, fn.__globals__, fn.__name__, fn.__defaults__, tuple(cells))


class Res:
    __slots__ = ("name", "lw", "rs")

    def __init__(self, name="r"):
        self.name = name
        self.lw = None
        self.rs = []


class Op:
    __slots__ = ("eng", "fn", "deps", "signal", "tick", "dma", "dsem", "dval", "prev_dval")

    def __init__(self, eng, fn, dma):
        self.eng = eng
        self.fn = fn
        self.dma = dma
        self.deps = []
        self.signal = False
        self.tick = 0
        self.dsem = None
        self.dval = 0
        self.prev_dval = 0


class Sched:
    ENGS = ("pe", "act", "dve", "pool", "sp")
    NDSEM = {"sp": 16, "pool": 6}

    def __init__(self, nc):
        self.nc = nc
        self.q = {e: [] for e in self.ENGS}
        self.dcount = {e: 0 for e in self.NDSEM}

    def add(self, eng, fn, reads=(), writes=(), dma=False):
        op = Op(eng, _freeze(fn), dma)
        deps = []
        for r in reads:
            if r.lw is not None:
                deps.append(r.lw)
        for w in writes:
            if w.lw is not None:
                deps.append(w.lw)
            deps.extend(w.rs)
        seen = set()
        for d in deps:
            if d is op or id(d) in seen:
                continue
            seen.add(id(d))
            if (not d.dma) and (not dma) and d.eng == "pe" and eng == "pe":
                continue
            op.deps.append(d)
        for r in reads:
            r.rs.append(op)
        for w in writes:
            w.lw = op
            w.rs = []
        if dma:
            i = self.dcount[eng]
            self.dcount[eng] += 1
            op.dsem = (eng, i % self.NDSEM[eng])
        self.q[eng].append(op)
        return op

    def pe(self, fn, r=(), w=()):
        return self.add("pe", fn, r, w)

    def act(self, fn, r=(), w=()):
        return self.add("act", fn, r, w)

    def dve(self, fn, r=(), w=()):
        return self.add("dve", fn, r, w)

    def pool(self, fn, r=(), w=()):
        return self.add("dve", fn, r, w)

    def dma(self, fn, r=(), w=(), q="sp"):
        return self.add(q, fn, r, w, dma=True)

    def emit(self):
        nc = self.nc
        for e in self.ENGS:
            for op in self.q[e]:
                for d in op.deps:
                    if not d.dma:
                        d.signal = True
        for e in self.ENGS:
            t = 0
            for op in self.q[e]:
                if op.dma:
                    continue
                if op.signal:
                    t += 1
                    op.tick = t
        dvals = {}
        for e in self.NDSEM:
            k = 0
            for op in self.q[e]:
                if op.dma:
                    op.dsem = (e, k % self.NDSEM[e])
                    k += 1
                    v = dvals.get(op.dsem, 0)
                    op.prev_dval = v
                    op.dval = v + 16
                    dvals[op.dsem] = op.dval
        with ExitStack() as st:
            esem = {e: st.enter_context(nc.semaphore("s_" + e)) for e in ("pe", "act", "dve", "pool")}
            dsem = {}
            for e, n in self.NDSEM.items():
                for i in range(min(n, self.dcount[e])):
                    dsem[(e, i)] = st.enter_context(nc.semaphore("d_%s%d" % (e, i)))
            block = st.enter_context(nc.Block())
            q = self.q

            def run(ename, eng):
                waited = {}

                def wait(key, sem, val):
                    if waited.get(key, 0) >= val:
                        return
                    waited[key] = val
                    eng.wait_ge(sem, val)

                for op in q[ename]:
                    need = {}
                    for d in op.deps:
                        if d.dma:
                            k = ("d",) + d.dsem
                            need[k] = max(need.get(k, 0), d.dval)
                        else:
                            k = ("e", d.eng)
                            need[k] = max(need.get(k, 0), d.tick)
                    for k, v in need.items():
                        if k[0] == "d":
                            wait(k, dsem[(k[1], k[2])], v)
                        else:
                            wait(k, esem[k[1]], v)
                    if op.dma:
                        if op.prev_dval:
                            wait(("d",) + op.dsem, dsem[op.dsem], op.prev_dval)
                        op.fn(eng).then_inc(dsem[op.dsem], 16)
                    else:
                        ins = op.fn(eng)
                        if op.signal:
                            ins.then_inc(esem[ename], 1)
                if ename in self.NDSEM:
                    last = {}
                    for op in q[ename]:
                        if op.dma:
                            last[op.dsem] = op.dval
                    for k, v in last.items():
                        wait(("d",) + k, dsem[k], v)

            if q["pe"]:
                block.tensor(lambda e: run("pe", e))
            if q["act"]:
                block.scalar(lambda e: run("act", e))
            if q["dve"]:
                block.vector(lambda e: run("dve", e))
            if q["pool"]:
                block.gpsimd(lambda e: run("pool", e))
            if q["sp"]:
                block.sync(lambda e: run("sp", e))


ARENA_BYTES = 42 * 1024


def build_program(stop_after=99):
    nc = bass.Bass("TRN2", target_bir_lowering=False)
    dt_in = {}

    def din(name, shape):
        dt_in[name] = nc.dram_tensor(name, list(shape), F32, kind="ExternalInput").ap()
        return dt_in[name]

    x_d = din("x", [T, D])
    nw_d = din("nw", [128, 64])
    consts_d = din("consts", [128, 520])
    w_in0_d = din("w_in0", [D, 3096])
    wa_aug_d = din("wa_aug", [17, 256])
    gnw_d = din("gnw", [128, 4])
    fbf_d = din("fbf", [8, 1])
    w_out0_d = din("w_out0", [D, D])
    w_in1_d = din("w_in1", [D, 2560])
    rel34_d = din("rel34", [128, 8 * 2 * 128])
    cvec_d = din("cvec", [128, 8])
    cw_d = din("cw", [128, 16])
    cb_d = din("cb", [128, 4])
    lwa_d = din("lwa", [8, 64, 64])
    lwx_d = din("lwx", [8, 64, 64])
    lba_d = din("lba", [128, 4])
    lbx_d = din("lbx", [128, 4])
    lam_d = din("lam", [128, 4])
    w_out1_d = din("w_out1", [D, D])
    w_up_d = din("w_up", [2, D, 4 * D])
    w_dn_d = din("w_dn", [2, 4 * D, D])
    out_d = nc.dram_tensor("out", [T, D], F32, kind="ExternalOutput").ap()
    wsb = nc.dram_tensor("wsb", [72, 128, 4096], BF16, kind="Internal").ap()

    S = Sched(nc)
    with ExitStack() as st:
        def sb(name, shape, dt):
            return st.enter_context(nc.sbuf_tensor(name, list(shape), dt))

        xT = sb("xT", [128, 8, T], F32)
        hT = sb("hT", [128, 8, T], BF16)
        mixT = sb("mixT", [128, 8, T], BF16)
        wbuf = [sb("wbuf%d" % i, [128, 8, 512], BF16) for i in range(3)]
        nw = sb("nw_sb", [128, 64], F32)
        cf = sb("cf", [128, 128], F32)
        cb16 = sb("cb16", [128, 520], BF16)
        ftmp = [sb("ftmp%d" % i, [128, 512], F32) for i in range(2)]
        dummy = sb("dummyt", [128, 8], F32)
        rstd = [sb("rstd%d" % i, [128, 512], F32) for i in range(2)]
        sqt = [sb("sqt%d" % i, [128, 512], BF16) for i in range(2)]
        ncum_p = sb("ncum_p", [128, NT * 8], F32)
        fbf_p = sb("fbf_p", [8, 1], F32)
        arena = sb("arena", [128, ARENA_BYTES // 2], BF16)
        banks = [st.enter_context(nc.psum_tensor("bank%d" % i, [128, 512], F32)) for i in range(8)]

        ident_f = cf[:, 0:128]
        R_ftmp = [Res(), Res()]
        ident_b = cb16[:, 0:128]
        ones_b = cb16[:, 128:256]
        tri_b = cb16[:, 256:384]
        trimask_b = cb16[:, 384:512]
        ind_b = cb16[:, 512:514]

        R_xT = [[Res() for _ in range(NB)] for _ in range(8)]
        R_hT = [[Res() for _ in range(NB)] for _ in range(8)]
        R_mix = [[Res() for _ in range(NB)] for _ in range(8)]
        R_w = [Res() for _ in range(3)]
        R_nw, R_cf, R_cb = Res(), Res(), Res()
        R_rstd = [Res(), Res()]
        R_sq = [Res(), Res()]
        R_bank = [Res() for _ in range(8)]
        R_dummy = Res()
        wcount = [0]

        arena_res = []

        class Carver:
            def __init__(self):
                self.off = 0

            def take(self, shape_free, dt, parts=128):
                n = 1
                for s in shape_free:
                    n *= s
                nbytes = n * (4 if dt == F32 else 2)
                nbytes = (nbytes + 31) // 32 * 32
                assert self.off + nbytes <= ARENA_BYTES, (self.off, nbytes)
                v = arena[0:parts, self.off // 2:(self.off + nbytes) // 2]
                if dt == F32:
                    v = v.bitcast(F32)
                v = v[:, 0:n]
                if len(shape_free) == 2:
                    v = v.rearrange("p (a b) -> p a b", a=shape_free[0])
                elif len(shape_free) == 3:
                    v = v.rearrange("p (a b c) -> p a b c", a=shape_free[0], b=shape_free[1])
                elif len(shape_free) == 4:
                    v = v.rearrange("p (a b c d) -> p a b c d", a=shape_free[0], b=shape_free[1], c=shape_free[2])
                self.off += nbytes
                return v

        def ares(n=1):
            rs = [Res() for _ in range(n)]
            arena_res.extend(rs)
            return rs if n > 1 else rs[0]

        def phase_switch():
            old = list(arena_res)
            del arena_res[:]
            j = S.pool(lambda e: e.memset(dummy[:, :], 0.0), w=old + [R_dummy])
            return j

        def new_phase():
            j = phase_switch()
            return Carver(), j

        def seed(rs, j):
            for r in (rs if isinstance(rs, (list, tuple)) else [rs]):
                r.lw = j

        scr = {}
        conv_ops = []

        def wconv(src_ap, kc, ncols):
            key = repr(src_ap)
            if key in scr:
                return scr[key]
            idx = len(scr)
            R = Res()
            src = src_ap.rearrange("(k p) n -> p k n", p=128)
            dst = wsb[idx][:, 0:kc * ncols].rearrange("p (k n) -> p k n", k=kc)
            op = S.dma(lambda e: e.dma_start(out=dst, in_=src), w=[R], q="pool")
            S.q["pool"].remove(op)
            conv_ops.append(op)
            scr[key] = (idx, R)
            return scr[key]

        def wload(src_ap, kc, ncols, dst=None, R_dst=None):
            idx, R = wconv(src_ap, kc, ncols)
            slot = None
            if dst is None:
                slot = wcount[0] % 3
                wcount[0] += 1
                dst = wbuf[slot][:, 0:kc, 0:ncols]
                R_dst = R_w[slot]
            src = wsb[idx][:, 0:kc * ncols].rearrange("p (k n) -> p k n", k=kc)
            S.dma(lambda e: e.dma_start(out=dst, in_=src), r=[R], w=[R_dst], q="sp")
            return slot

        def mlp_tile_src(l, idx):
            if idx < 8:
                return w_up_d[l, :, idx * 512:(idx + 1) * 512]
            mg, fg = (idx - 8) // 4, (idx - 8) % 4
            return w_dn_d[l, fg * 1024:(fg + 1) * 1024, mg * 512:(mg + 1) * 512]

        def wload_bf(l, idx):
            return wload(mlp_tile_src(l, idx), 8, 512)

        S.dma(lambda e: e.dma_start(out=nw[:], in_=nw_d[:, :]), w=[R_nw])
        S.dma(lambda e: e.dma_start(out=cf[:], in_=consts_d[:, 0:128]), w=[R_cf])
        S.dma(lambda e: e.dma_start(out=cb16[:], in_=consts_d[:, :]), w=[R_cb], q="pool")

        def nwcol(l, j, c):
            i = (l * 4 + j) * 8 + c
            return nw[:, i:i + 1]

        car, j0 = new_phase()
        NXS = 8
        xs = [car.take([D], F32) for _ in range(NXS)]
        R_xs = ares(NXS)
        seed(R_xs, j0)
        for tt in range(NT):
            s = tt % NXS
            n = tt // 4
            S.dma(lambda e, s=s, tt=tt: e.dma_start(out=xs[s], in_=x_d[tt * 128:(tt + 1) * 128, :]), w=[R_xs[s]])
            for half in range(2):
                bk = 2 * (tt % 2) + half
                for c4 in range(4):
                    c = half * 4 + c4
                    S.pe(lambda e, bk=bk, c4=c4, c=c, s=s: e.transpose(
                        out=banks[bk][:, c4 * 128:(c4 + 1) * 128], in_=xs[s][:, c * 128:(c + 1) * 128], identity=ident_f),
                        r=[R_xs[s], R_cf], w=[R_bank[bk]])
                dst = xT[:, half * 4:half * 4 + 4, tt * 128:(tt + 1) * 128]
                src = banks[bk][:, :].rearrange("p (a b) -> p a b", a=4)
                wr = [R_xT[half * 4 + c4][n] for c4 in range(4)]
                if half == 0:
                    S.act(lambda e, dst=dst, src=src: e.activation(out=dst, in_=src, func=AF.Copy), r=[R_bank[bk]], w=wr)
                else:
                    S.dve(lambda e, dst=dst, src=src: e.tensor_copy(out=dst, in_=src), r=[R_bank[bk]], w=wr)

        def rs_of(slot):
            if isinstance(slot, int):
                return rstd[slot], R_rstd[slot]
            return slot

        def stats_rstd(src_fn, src_res_fn, n, slot, bank, nchunks, scale, use_pool=False):
            rt, R_rt = rs_of(slot)

            def st_mm(c):
                sq = c % 2
                S.pe(lambda e, sq=sq, c=c: e.matmul(banks[bank][:, :], lhsT=ones_b, rhs=sqt[sq][:],
                                                   start=(c == 0), stop=(c == nchunks - 1)),
                     r=[R_sq[sq], R_cb], w=[R_bank[bank]])

            for c in range(nchunks):
                sq = c % 2
                src = src_fn(c)
                if use_pool and c % 2 == 1:
                    S.add("pool", lambda e, src=src, sq=sq: e.tensor_tensor(out=sqt[sq][:], in0=src, in1=src, op=ALU.mult),
                          [src_res_fn(c)], [R_sq[sq]])
                else:
                    S.act(lambda e, src=src, sq=sq: e.activation(out=sqt[sq][:], in_=src, func=AF.Square),
                          r=[src_res_fn(c)], w=[R_sq[sq]])
                if c >= 1:
                    st_mm(c - 1)
            st_mm(nchunks - 1)
            S.act(lambda e: e.activation(out=rt[:, :], in_=banks[bank][:, :], func=AF.Ln, scale=scale, bias=EPS),
                  r=[R_bank[bank]], w=[R_rt])
            S.act(lambda e: e.activation(out=rt[:, :], in_=rt[:, :], func=AF.Exp, scale=-0.5), r=[R_rt], w=[R_rt])

        def prenorm_block(l, j, n, bank, use_pool=False):
            slot = n % 2
            blk = slice(n * 512, (n + 1) * 512)
            stats_rstd(lambda c: xT[:, c, blk], lambda c: R_xT[c][n], n, slot, bank, 8, 1.0 / D, use_pool=use_pool)
            for c in range(8):
                S.dve(lambda e, c=c: e.scalar_tensor_tensor(out=hT[:, c, blk], in0=xT[:, c, blk], scalar=nwcol(l, j, c),
                                                            in1=rstd[slot][:], op0=ALU.mult, op1=ALU.mult),
                      r=[R_xT[c][n], R_rstd[slot], R_nw], w=[R_hT[c][n]])

        def evac_y(bk, y32, R_y, m, l, j, stats_bank):
            sq = m % 2
            S.act(lambda e, bk=bk, sq=sq: e.activation(out=sqt[sq][:], in_=banks[bk][:, :], func=AF.Square),
                  r=[R_bank[bk]], w=[R_sq[sq]])
            S.act(lambda e, bk=bk, m=m: e.activation(out=y32[:, m, :], in_=banks[bk][:, :], func=AF.Identity, scale=nwcol(l, j, m)),
                  r=[R_bank[bk], R_nw], w=[R_y[m]])

            def stats_mm():
                S.pe(lambda e, sq=sq, m=m: e.matmul(banks[stats_bank][:, :], lhsT=ones_b, rhs=sqt[sq][:],
                                                   start=(m == 0), stop=(m == 7)),
                     r=[R_sq[sq], R_cb], w=[R_bank[stats_bank]])
            return stats_mm

        def post_rstd(bank, slot):
            rt, R_rt = rs_of(slot)
            S.act(lambda e: e.activation(out=rt[:, :], in_=banks[bank][:, :], func=AF.Ln, scale=1.0 / D, bias=EPS),
                  r=[R_bank[bank]], w=[R_rt])
            S.act(lambda e: e.activation(out=rt[:, :], in_=rt[:, :], func=AF.Exp, scale=-0.5), r=[R_rt], w=[R_rt])

        def postnorm_residual(l, j, n, y32, R_y, bank, slot):
            rt, R_rt = rs_of(slot)
            blk = slice(n * 512, (n + 1) * 512)
            for hf in range(2):
                cs = slice(hf * 4, hf * 4 + 4)
                S.dve(lambda e, cs=cs: e.tensor_tensor(out=y32[:, cs, :], in0=y32[:, cs, :],
                                                      in1=rt[:, :].unsqueeze(1).broadcast_to([128, 4, 512]), op=ALU.mult),
                      r=list(R_y[cs]) + [R_rt], w=list(R_y[cs]))
                S.dve(lambda e, cs=cs: e.tensor_tensor(out=xT[:, cs, blk], in0=xT[:, cs, blk], in1=y32[:, cs, :], op=ALU.add),
                      r=list(R_y[cs]) + [R_xT[c][n] for c in range(hf * 4, hf * 4 + 4)], w=[R_xT[c][n] for c in range(hf * 4, hf * 4 + 4)])

        def out_proj_residual(l, w_out_d):
            car, j = new_phase()
            y32s = [car.take([8, 512], F32) for _ in range(2)]
            prs = [car.take([512], F32) for _ in range(2)]
            R_ys = [ares(8), ares(8)]
            R_prs = ares(2)
            seed(R_ys[0] + R_ys[1] + R_prs, j)
            pend = [None]
            for n in range(NB):
                blk = slice(n * 512, (n + 1) * 512)
                y32, R_y = y32s[n % 2], R_ys[n % 2]
                for half in range(2):
                    slot = wload(w_out_d[:, half * 512:(half + 1) * 512], 8, 512)
                    for m4 in range(4):
                        m = half * 4 + m4
                        bk = (0, 1, 4, 5)[m % 4]
                        for c in range(8):
                            S.pe(lambda e, bk=bk, slot=slot, c=c, m4=m4: e.matmul(
                                banks[bk][:, :], lhsT=wbuf[slot][:, c, m4 * 128:(m4 + 1) * 128], rhs=mixT[:, c, blk],
                                start=(c == 0), stop=(c == 7)),
                                r=[R_w[slot], R_mix[c][n]], w=[R_bank[bk]])
                        if pend[0] is not None:
                            pend[0]()
                        pend[0] = evac_y(bk, y32, R_y, m, l, 1, 2)
                pend[0]()
                pend[0] = None
                post_rstd(2, (prs[n % 2], R_prs[n % 2]))
                if n > 0:
                    postnorm_residual(l, 1, n - 1, y32s[(n - 1) % 2], R_ys[(n - 1) % 2], 2, slot=(prs[(n - 1) % 2], R_prs[(n - 1) % 2]))
                if n == 2:
                    prenorm_block(l, 2, 0, 3)
            postnorm_residual(l, 1, NB - 1, y32s[(NB - 1) % 2], R_ys[(NB - 1) % 2], 2, slot=(prs[(NB - 1) % 2], R_prs[(NB - 1) % 2]))

        TL = {}
        TAIL = 7040

        def alloc_tail(j):
            tailc = Carver()
            tailc.off = ARENA_BYTES - TAIL
            bd = tailc.take([2, 4, 128], BF16)
            cw = tailc.take([16], F32)
            cbias = tailc.take([4], F32)
            lba = tailc.take([4], F32)
            lbx = tailc.take([4], F32)
            lam = tailc.take([4], F32)
            c8 = tailc.take([4], F32)
            c16 = tailc.take([4], F32)
            c8h = tailc.take([4], F32)
            lbah = tailc.take([4], F32)
            lbxh = tailc.take([4], F32)
            LN_HALF = tailc.take([1], F32)
            rel34 = tailc.take([8, 2, 128], BF16)
            cvec = tailc.take([8], F32)
            tail_res = [Res() for _ in range(9)]
            R_bd, R_cw, R_cbias, R_lba, R_lbx, R_lam, R_c8, R_rel34, R_cvec = tail_res
            seed(tail_res, j)
            S.dve(lambda e: e.memset(bd[:, :, :, :], 0.0), w=[R_bd])
            for wi, wd in enumerate((lwa_d, lwx_d)):
                for blk8 in range(8):
                    c = blk8 // 2
                    o = (blk8 % 2) * 64
                    S.dma(lambda e, wi=wi, wd=wd, blk8=blk8, c=c, o=o: e.dma_start(out=bd[o:o + 64, wi, c, o:o + 64], in_=wd[blk8, :, :]),
                          r=[], w=[R_bd], q="pool")
            S.dma(lambda e: e.dma_start(out=rel34[:, :, :, :], in_=rel34_d[:, :].rearrange("p (h a b) -> p h a b", h=8, a=2)),
                  w=[R_rel34], q="pool")
            S.dma(lambda e: e.dma_start(out=cvec, in_=cvec_d[:, :]), w=[R_cvec])
            S.dma(lambda e: e.dma_start(out=cw, in_=cw_d[:, :]), w=[R_cw])
            S.dma(lambda e: e.dma_start(out=cbias, in_=cb_d[:, :]), w=[R_cbias])
            S.dma(lambda e: e.dma_start(out=lba, in_=lba_d[:, :]), w=[R_lba])
            S.dma(lambda e: e.dma_start(out=lbx, in_=lbx_d[:, :]), w=[R_lbx])
            S.dma(lambda e: e.dma_start(out=lam, in_=lam_d[:, :]), w=[R_lam])
            TL.update(dict(bd=bd, cw=cw, cbias=cbias, lba=lba, lbx=lbx, lam=lam, c8=c8, c16=c16, c8h=c8h, lbah=lbah, lbxh=lbxh,
                           LN_HALF=LN_HALF, rel34=rel34, cvec=cvec, tail_res=tail_res))

        def mlp(l):
            car, j = new_phase()
            uT = mixT[:, :, :].rearrange("p c (a b) -> p (c a) b", b=512)
            y32s = [car.take([8, 512], F32) for _ in range(2)]
            prs1 = car.take([512], F32)
            prs = [prs1, prs1]
            rl = [ftmp[0][:, :], ftmp[1][:, :]]
            R_u = [R_mix[f // 4][f % 4] for f in range(32)]
            R_ys = [ares(8), ares(8)]
            R_prs1 = ares()
            R_prs = [R_prs1, R_prs1]
            R_rl = R_ftmp
            seed(R_ys[0] + R_ys[1] + [R_prs1], j)
            assert car.off <= ARENA_BYTES - TAIL, car.off

            def up(n):
                blk = slice(n * 512, (n + 1) * 512)
                for fg in range(8):
                    slot = wload_bf(l, fg)
                    for f4 in range(4):
                        f = fg * 4 + f4
                        bk = f % 2
                        for c in range(8):
                            S.pe(lambda e, bk=bk, slot=slot, c=c, f4=f4: e.matmul(
                                banks[bk][:, :], lhsT=wbuf[slot][:, c, f4 * 128:(f4 + 1) * 128], rhs=hT[:, c, blk],
                                start=(c == 0), stop=(c == 7)),
                                r=[R_w[slot], R_hT[c][n]], w=[R_bank[bk]])
                        S.act(lambda e, bk=bk: e.activation(out=rl[bk], in_=banks[bk][:, :], func=AF.Relu),
                              r=[R_bank[bk]], w=[R_rl[bk]])
                        S.dve(lambda e, bk=bk, f=f: e.tensor_tensor(out=uT[:, f, :], in0=rl[bk], in1=rl[bk], op=ALU.mult),
                              r=[R_rl[bk]], w=[R_u[f]])

            def down(n):
                y32, R_y = y32s[n % 2], R_ys[n % 2]
                pend_dn = None
                for mg in range(2):
                    for fg in range(4):
                        slot = wload_bf(l, 8 + mg * 4 + fg)
                        for m4 in range(4):
                            if mg == 1 and fg == 0 and m4 == 1 and pend_dn is not None:
                                pend_dn()
                                pend_dn = None
                            bk = 4 + m4
                            for f8 in range(8):
                                f = fg * 8 + f8
                                S.pe(lambda e, bk=bk, slot=slot, f8=f8, f=f, m4=m4, fg=fg: e.matmul(
                                    banks[bk][:, :], lhsT=wbuf[slot][:, f8, m4 * 128:(m4 + 1) * 128], rhs=uT[:, f, :],
                                    start=(fg == 0 and f8 == 0), stop=(fg == 3 and f8 == 7)),
                                    r=[R_w[slot], R_u[f]], w=[R_bank[bk]])
                    prev = None
                    for m4 in range(4):
                        m = mg * 4 + m4
                        bk = 4 + m4
                        cur = evac_y(bk, y32, R_y, m, l, 3, 3)
                        if prev is not None:
                            prev()
                        prev = cur
                    if mg == 0:
                        pend_dn = prev
                    else:
                        prev()
                post_rstd(3, (prs[n % 2], R_prs[n % 2]))

            def post(n):
                postnorm_residual(l, 3, n, y32s[n % 2], R_ys[n % 2], 3, slot=(prs[n % 2], R_prs[n % 2]))

            def next_prenorm(n):
                pass

            for n in range(NB):
                up(n)
                if n == 0 and l == 0:
                    alloc_tail(j)
                if n > 0:
                    post(n - 1)
                if n > 1:
                    next_prenorm(n - 2)
                if n + 1 < NB:
                    prenorm_block(l, 2, n + 1, 2)
                down(n)
            post(NB - 1)
            next_prenorm(NB - 2)
            next_prenorm(NB - 1)

        if stop_after >= 0.25:

            car, j = new_phase()
            wk = car.take([8, 256], BF16)
            qg = car.take([2, T], BF16)
            a_aug = car.take([T], BF16, parts=32)
            wa_f = ftmp[0][0:32, 0:256]
            wa_b = car.take([256], BF16, parts=32)
            gnw = car.take([4], F32)
            tmpE = [car.take([256], F32) for _ in range(2)]
            L_bf = [car.take([256], BF16) for _ in range(2)]
            kdec = [car.take([2, 2, 128], BF16) for _ in range(2)]
            v_bf = [car.take([512], BF16) for _ in range(2)]
            dec = car.take([2, 32], F32)
            S32 = car.take([2, 128], F32)
            S_bf = [car.take([2, 128], BF16) for _ in range(2)]
            o32s = [car.take([4, 512], F32) for _ in range(2)]
            gate = ftmp[0][:, :]
            t1 = ftmp[1][:, :]
            R_gate, R_t1 = R_ftmp
            R_wk, R_qg, R_aaug, R_wab, R_gnw, R_S32 = ares(6)
            R_waf = R_ftmp[0]
            R_decs = ares(NT)
            R_tmpE = ares(2); R_L = ares(2); R_kdec = ares(2); R_vbf = ares(2); R_Sbf = ares(2)
            R_o32s = [ares(4), ares(4)]
            seed([R_wk, R_qg, R_aaug, R_wab, R_gnw, R_S32], j)
            seed(R_decs + R_tmpE + R_L + R_kdec + R_vbf + R_Sbf + R_o32s[0] + R_o32s[1], j)

            wload(w_in0_d[:, 256:512], 8, 256, dst=wk[:, :, :], R_dst=R_wk)
            S.dma(lambda e: e.dma_start(out=wa_f[0:17, :], in_=wa_aug_d[:, :]), w=[R_waf])
            S.dma(lambda e: e.dma_start(out=gnw, in_=gnw_d[:, :]), w=[R_gnw])
            S.dve(lambda e: e.tensor_copy(out=wa_b[0:17, :], in_=wa_f[0:17, :]), r=[R_waf], w=[R_wab])
            S.pool(lambda e: e.memset(a_aug[0:32, :], 1.0), w=[R_aaug])
            S.pool(lambda e: e.memset(S32[:, :, :], 0.0), w=[R_S32])
            for _s in range(2):
                S.pool(lambda e, _s=_s: e.memset(kdec[_s][:, :, :, :], 0.0), w=[R_kdec[_s]])
            slot_q = wload(w_in0_d[:, 0:256], 8, 256)
            slot_a = wload(w_in0_d[:, 1536:1552], 8, 16)
            slot_r = wload(w_in0_d[:, 1024:1536], 8, 512)
            fw = car.take([8, 8], BF16)
            R_fw = ares()
            seed(R_fw, j)
            wload(w_in0_d[:, 3088:3096], 8, 8, dst=fw[:, :, :], R_dst=R_fw)
            early_sync = [op for op in S.q["sp"] if op.dma]
            n_early_conv = len(conv_ops)
            R_fbf = Res()
            R_ncum = Res()
            S.dma(lambda e: e.dma_start(out=fbf_p[:, :], in_=fbf_d[:, :]), w=[R_fbf])
            S.dve(lambda e: e.tensor_scalar(out=fbf_p[:, :], in0=fbf_p[:, :], scalar1=-1.0, scalar2=None, op0=ALU.mult), r=[R_fbf], w=[R_fbf])
            ones8 = cb16[0:8, 128:129].broadcast_to([8, 512])
            lf = rstd[1][0:8, :]
            R_lf = R_rstd[1]
            for _n in range(NB):
                S.dve(lambda e, _n=_n: e.memset(mixT[:, 7, _n * 512:(_n + 1) * 512], 0.0), w=[R_mix[7][_n]])
            prenorm_block(0, 0, 0, 5)
            pend_tr = []
            for n in range(NB):
                blk = slice(n * 512, (n + 1) * 512)
                for pr in range(2):
                    bk = pr
                    for c in range(8):
                        S.pe(lambda e, bk=bk, c=c, pr=pr: e.matmul(banks[bk][:, :], lhsT=wbuf[slot_q][:, c, pr * 128:(pr + 1) * 128],
                                                                   rhs=hT[:, c, blk], start=(c == 0), stop=(c == 7)),
                             r=[R_w[slot_q], R_hT[c][n]], w=[R_bank[bk]])
                for c in range(8):
                    S.pe(lambda e, c=c: e.matmul(banks[2][0:16, :], lhsT=wbuf[slot_a][:, c, 0:16], rhs=hT[:, c, blk],
                                                 start=(c == 0), stop=(c == 7)),
                         r=[R_w[slot_a], R_hT[c][n]], w=[R_bank[2]])
                while pend_tr:
                    pend_tr.pop(0)()
                if n + 1 < NB:
                    prenorm_block(0, 0, n + 1, 5 + ((n + 1) % 2))
                for pr in range(2):
                    bk = pr
                    S.act(lambda e, bk=bk, pr=pr: e.activation(out=qg[:, pr, blk], in_=banks[bk][:, :], func=AF.Copy, scale=0.125),
                          r=[R_bank[bk]], w=[R_qg])
                S.dve(lambda e: e.tensor_copy(out=a_aug[0:16, blk], in_=banks[2][0:16, :]), r=[R_bank[2]], w=[R_aaug])
                for hd in range(4):
                    bk = 3 + hd % 2
                    for c in range(8):
                        S.pe(lambda e, c=c, hd=hd, bk=bk: e.matmul(banks[bk][:, :], lhsT=wbuf[slot_r][:, c, hd * 128:(hd + 1) * 128],
                                                                   rhs=hT[:, c, blk], start=(c == 0), stop=(c == 7)),
                             r=[R_w[slot_r], R_hT[c][n]], w=[R_bank[bk]])
                    S.act(lambda e, hd=hd, bk=bk: e.activation(out=mixT[:, hd, blk], in_=banks[bk][:, :], func=AF.Silu),
                          r=[R_bank[bk]], w=[R_mix[hd][n]])
                cumb = ftmp[n % 2][0:8, :]
                for c in range(8):
                    S.pe(lambda e, c=c: e.matmul(banks[2][0:8, :], lhsT=fw[:, c, :], rhs=hT[:, c, blk], start=(c == 0), stop=(c == 7)),
                         r=[R_fw, R_hT[c][n]], w=[R_bank[2]])
                S.act(lambda e: e.activation(out=lf, in_=banks[2][0:8, :], func=AF.Exp, scale=-1.0, bias=fbf_p[:, :]),
                      r=[R_bank[2], R_fbf], w=[R_lf])
                S.act(lambda e: e.activation(out=lf, in_=lf, func=AF.Ln, bias=1.0), r=[R_lf], w=[R_lf])
                S.dve(lambda e: e.tensor_scalar(out=lf, in0=lf, scalar1=-1.0, scalar2=None, op0=ALU.mult), r=[R_lf], w=[R_lf])
                init = 0.0 if n == 0 else ftmp[(n - 1) % 2][0:8, 511:512]
                rinit = [] if n == 0 else [R_ftmp[(n - 1) % 2]]
                S.dve(lambda e: e.tensor_tensor_scan(out=cumb, data0=ones8, data1=lf, initial=init, op0=ALU.mult, op1=ALU.add),
                      r=[R_lf, R_cb] + rinit, w=[R_ftmp[n % 2]])
                S.act(lambda e: e.activation(out=mixT[64:72, 7, blk], in_=cumb, func=AF.Copy, scale=8.0),
                      r=[R_ftmp[n % 2]], w=[R_mix[7][n]])
                def cum_transposes(n=n, cumb=cumb):
                    for t4 in range(4):
                        tt = 4 * n + t4
                        S.pe(lambda e, tt=tt, t4=t4: e.transpose(out=banks[7][:, tt * 8:(tt + 1) * 8], in_=cumb[:, t4 * 128:(t4 + 1) * 128],
                                                                 identity=ident_f[0:8, 0:8]),
                             r=[R_ftmp[n % 2], R_cf], w=[R_bank[7]])
                pend_tr.append(cum_transposes)
            while pend_tr:
                pend_tr.pop(0)()
            S.act(lambda e: e.activation(out=ncum_p[:, :], in_=banks[7][:, 0:128], func=AF.Copy, scale=-1.0),
                  r=[R_bank[7]], w=[R_ncum])
            slot_v = wload(w_in0_d[:, 512:1024], 8, 512)

            def b1_k(tt):
                n = tt // 4
                tok = slice(tt * 128, (tt + 1) * 128)
                for c in range(8):
                    S.pe(lambda e, c=c: e.matmul(banks[0][:, 0:256], lhsT=hT[:, c, tok], rhs=wk[:, c, :],
                                                 start=(c == 0), stop=(c == 7)),
                         r=[R_hT[c][n], R_wk], w=[R_bank[0]])

            def b1_v(tt):
                n = tt // 4
                tok = slice(tt * 128, (tt + 1) * 128)
                for c in range(8):
                    S.pe(lambda e, c=c: e.matmul(banks[1][:, :], lhsT=hT[:, c, tok], rhs=wbuf[slot_v][:, c, :],
                                                 start=(c == 0), stop=(c == 7)),
                         r=[R_hT[c][n], R_w[slot_v]], w=[R_bank[1]])

            def b1_pre(tt):
                s = tt % 2
                tok = slice(tt * 128, (tt + 1) * 128)
                S.pe(lambda e: e.matmul(banks[0][:, 256:512], lhsT=a_aug[0:17, tok], rhs=wa_b[0:17, :], start=True, stop=True),
                     r=[R_aaug, R_wab], w=[R_bank[0]])
                S.act(lambda e: e.activation(out=tmpE[s], in_=banks[0][:, 256:512], func=AF.Exp, scale=-1.0),
                      r=[R_bank[0]], w=[R_tmpE[s]])
                S.act(lambda e: e.activation(out=L_bf[s], in_=tmpE[s], func=AF.Ln, bias=1.0),
                      r=[R_tmpE[s]], w=[R_L[s]])

            def b1_tri(tt):
                s = tt % 2
                S.pe(lambda e: e.matmul(banks[2][:, 0:256], lhsT=tri_b, rhs=L_bf[s], start=True, stop=True),
                     r=[R_L[s], R_cb], w=[R_bank[2]])
                for pr in range(2):
                    S.pe(lambda e, pr=pr: e.matmul(banks[2][:, 256 + 2 * pr:258 + 2 * pr], lhsT=L_bf[s][:, pr * 128:(pr + 1) * 128],
                                                   rhs=ind_b, start=True, stop=True),
                         r=[R_L[s], R_cb], w=[R_bank[2]])
                S.act(lambda e: e.activation(out=tmpE[s], in_=banks[2][:, 0:256], func=AF.Exp),
                      r=[R_bank[2]], w=[R_tmpE[s]])
                S.act(lambda e: e.activation(out=dec[:, :, 2 * tt:2 * tt + 2],
                                             in_=banks[2][:, 256:260].rearrange("p (a b) -> p a b", a=2), func=AF.Exp),
                      r=[R_bank[2]], w=[R_decs[tt]])
                for hh in range(2):
                    S.dve(lambda e, hh=hh: e.tensor_tensor(
                        out=kdec[s][:, :, hh, hh * 64:(hh + 1) * 64],
                        in0=banks[0][:, 0:256].rearrange("p (a b c) -> p a b c", a=2, b=2)[:, :, hh, :],
                        in1=tmpE[s].rearrange("p (a b c) -> p a b c", a=2, b=2)[:, :, hh, :], op=ALU.mult),
                        r=[R_bank[0], R_tmpE[s]], w=[R_kdec[s]])
                S.act(lambda e: e.activation(out=v_bf[s], in_=banks[1][:, :], func=AF.Copy),
                      r=[R_bank[1]], w=[R_vbf[s]])

            def b2_inc(tt, jc):
                s = tt % 2
                rows = slice(jc * 64, (jc + 1) * 64)
                bki = 3 + jc
                for pr in range(2):
                    for hh in range(2):
                        hd = 2 * pr + hh
                        S.pe(lambda e, pr=pr, hh=hh, hd=hd: e.matmul(
                            banks[bki][:, pr * 128:(pr + 1) * 128], lhsT=kdec[s][rows, pr, hh, :],
                            rhs=v_bf[s][rows, hd * 128:(hd + 1) * 128], start=(hh == 0), stop=(hh == 1)),
                            r=[R_kdec[s], R_vbf[s]], w=[R_bank[bki]])

            def b2_chain(tt, jc):
                cg = 2 * tt + jc
                ss = cg % 2
                bki = 3 + jc
                for pr in range(2):
                    S.dve(lambda e, pr=pr: e.scalar_tensor_tensor(
                        out=S32[:, pr, :], in0=S32[:, pr, :], scalar=dec[:, pr, cg:cg + 1],
                        in1=banks[bki][:, pr * 128:(pr + 1) * 128], op0=ALU.mult, op1=ALU.add),
                        r=[R_S32, R_decs[tt], R_bank[bki]], w=[R_S32])
                S.dve(lambda e: e.tensor_copy(out=S_bf[ss], in_=S32), r=[R_S32], w=[R_Sbf[ss]])

            def b2_o(tt, jc):
                cg = 2 * tt + jc
                ss = cg % 2
                for pr in range(2):
                    for hh in range(2):
                        pp = slice(hh * 64, (hh + 1) * 64)
                        S.pe(lambda e, pr=pr, hh=hh, pp=pp: e.matmul(
                            banks[5 + hh][:, pr * 128 + jc * 64:pr * 128 + (jc + 1) * 64], lhsT=S_bf[ss][pp, pr, :],
                            rhs=qg[pp, pr, cg * 64:(cg + 1) * 64], start=True, stop=True),
                            r=[R_Sbf[ss], R_qg], w=[R_bank[5 + hh]])

            def b2_out(tt):
                n = tt // 4
                t4 = tt % 4
                ob = n % 2
                for hh in range(2):
                    for pr in range(2):
                        hd = 2 * pr + hh
                        S.act(lambda e, hh=hh, pr=pr, hd=hd: e.activation(out=o32s[ob][:, hd, t4 * 128:(t4 + 1) * 128],
                                                                          in_=banks[5 + hh][:, pr * 128:(pr + 1) * 128], func=AF.Copy),
                              r=[R_bank[5 + hh]], w=[R_o32s[ob][hd]])

            def gla_fin(n, parts_only=False):
                blk = slice(n * 512, (n + 1) * 512)
                ob = n % 2
                o32, R_o32 = o32s[ob], R_o32s[ob]

                def sq(hd):
                    S.act(lambda e: e.activation(out=sqt[hd % 2][:], in_=o32[:, hd, :], func=AF.Square),
                          r=[R_o32[hd]], w=[R_sq[hd % 2]])

                def mm_rstd(hd):
                    sl = hd % 2
                    S.pe(lambda e: e.matmul(banks[2][:, :], lhsT=ones_b, rhs=sqt[hd % 2][:], start=True, stop=True),
                         r=[R_sq[hd % 2], R_cb], w=[R_bank[2]])
                    S.act(lambda e: e.activation(out=rstd[sl][:, :], in_=banks[2][:, :], func=AF.Ln, scale=1.0 / 128, bias=EPS),
                          r=[R_bank[2]], w=[R_rstd[sl]])
                    S.act(lambda e: e.activation(out=rstd[sl][:, :], in_=rstd[sl][:, :], func=AF.Exp, scale=-0.5),
                          r=[R_rstd[sl]], w=[R_rstd[sl]])

                def apply(hd):
                    sl = hd % 2
                    tq = ftmp[hd % 2]
                    S.dve(lambda e: e.scalar_tensor_tensor(out=tq[:, :], in0=o32[:, hd, :], scalar=gnw[:, hd:hd + 1],
                                                           in1=rstd[sl][:], op0=ALU.mult, op1=ALU.mult),
                          r=[R_o32[hd], R_gnw, R_rstd[sl]], w=[R_ftmp[hd % 2]])
                    S.dve(lambda e: e.tensor_tensor(out=mixT[:, hd, blk], in0=tq[:, :], in1=mixT[:, hd, blk], op=ALU.mult),
                          r=[R_ftmp[hd % 2], R_mix[hd][n]], w=[R_mix[hd][n]])

                if parts_only:
                    return sq, mm_rstd, apply
                sq(0); sq(1)
                mm_rstd(0)
                sq(2)
                mm_rstd(1)
                apply(0)
                sq(3)
                mm_rstd(2)
                apply(1)
                mm_rstd(3)
                apply(2)
                apply(3)

            b1_k(0); b1_v(0); b1_pre(0); b1_tri(0)
            for tt in range(NT):
                nxt = tt + 1 < NT
                fin = gla_fin(tt // 4 - 1, parts_only=True) if tt >= 4 else None
                fh = tt % 4
                b2_inc(tt, 0)
                b2_inc(tt, 1)
                if fin:
                    fin[0](fh)
                if nxt:
                    b1_k(tt + 1)
                    b1_pre(tt + 1)
                if fin:
                    fin[1](fh)
                b2_chain(tt, 0)
                b2_chain(tt, 1)
                if fin:
                    fin[2](fh)
                if nxt:
                    b1_v(tt + 1)
                    b1_tri(tt + 1)
                b2_o(tt, 0)
                b2_o(tt, 1)
                b2_out(tt)
            gla_fin(NB - 1)

        if stop_after >= 0.8:
            car, j = new_phase()
            qa = car.take([2, T], BF16, parts=65)
            ka = car.take([2, T], BF16, parts=65)
            va = car.take([NT, 2, 128], BF16)
            PT = [car.take([512], BF16) for _ in range(3)]
            rden = ftmp[0][0:64, :]
            R_rden = R_ftmp[0]
            ncum = ncum_p[:, :].rearrange("p (a b) -> p a b", a=NT)
            R_PT = ares(3)
            seed(R_PT, j)

            R_qab, R_kab, R_vab = ares(NB), ares(NB), ares(NB)
            seed(R_qab + R_kab + R_vab, j)
            S.pool(lambda e: e.memset(va[:, :, 0, 64:128], 1.0), w=R_vab)
            S.pool(lambda e: e.memset(va[:, :, 1, 0:64], 1.0), w=R_vab)
            S.pool(lambda e: e.memset(ka[64:65, :, :], 1.0), w=R_kab)
            slot_qk = wload(w_in0_d[:, 1552:2064], 8, 512)
            slot_k = wload(w_in0_d[:, 2064:2576], 8, 512)
            slot_v = wload(w_in0_d[:, 2576:3088], 8, 512)

            def fox_inproj(hp, n):
                wc = slice(hp * 128, (hp + 1) * 128)
                blk = slice(n * 512, (n + 1) * 512)
                for (slot, dstt, R_dst, bk) in ((slot_qk, qa, R_qab[n], 0), (slot_k, ka, R_kab[n], 1)):
                    for c in range(8):
                        S.pe(lambda e, c=c, slot=slot, bk=bk: e.matmul(banks[bk][:, :], lhsT=wbuf[slot][:, c, wc], rhs=hT[:, c, blk],
                                                                      start=(c == 0), stop=(c == 7)),
                             r=[R_w[slot], R_hT[c][n]], w=[R_bank[bk]])
                    S.dve(lambda e, dstt=dstt, bk=bk: e.tensor_copy(out=dstt[0:64, 0, blk], in_=banks[bk][0:64, :]),
                          r=[R_bank[bk]], w=[R_dst])
                    S.dve(lambda e, dstt=dstt, bk=bk: e.tensor_copy(out=dstt[0:64, 1, blk], in_=banks[bk][64:128, :]),
                          r=[R_bank[bk]], w=[R_dst])
                for t4 in range(4):
                    tt = 4 * n + t4
                    tok = slice(tt * 128, (tt + 1) * 128)
                    for c in range(8):
                        S.pe(lambda e, c=c, t4=t4: e.matmul(banks[2][:, t4 * 128:(t4 + 1) * 128], lhsT=hT[:, c, tok], rhs=wbuf[slot_v][:, c, wc],
                                                            start=(c == 0), stop=(c == 7)),
                             r=[R_w[slot_v], R_hT[c][n]], w=[R_bank[2]])
                S.dve(lambda e: e.tensor_copy(
                    out=va[:, 4 * n:4 * n + 4, 0, 0:64],
                    in_=banks[2][:, :].rearrange("p (a b c) -> p a b c", a=4, b=2)[:, :, 0, :]),
                    r=[R_bank[2]], w=[R_vab[n]])
                S.dve(lambda e: e.tensor_copy(
                    out=va[:, 4 * n:4 * n + 4, 1, 64:128],
                    in_=banks[2][:, :].rearrange("p (a b c) -> p a b c", a=4, b=2)[:, :, 1, :]),
                    r=[R_bank[2]], w=[R_vab[n]])
                for hl in range(2):
                    hd = 2 * hp + hl
                    S.pe(lambda e, hd=hd, hl=hl: e.matmul(banks[hl][0:65, :], lhsT=ident_b[:, hd:hd + 65], rhs=mixT[:, 7, blk], start=True, stop=True),
                         r=[R_cb, R_mix[7][n]], w=[R_bank[hl]])
                    S.dve(lambda e, hl=hl: e.tensor_copy(out=qa[64:65, hl, blk], in_=banks[hl][64:65, :]),
                          r=[R_bank[hl]], w=[R_qab[n]])

            def fox_attn(hp, hl, qb):
                hd = 2 * hp + hl
                qs = qb * 512
                obk = 6 + hl
                nkt = 4 * (qb + 1)

                def qk_step(kt):
                    jd = kt - 4 * qb
                    c0 = 128 * jd if jd > 0 else 0
                    sb_i = kt % 3
                    sbk = 3 + sb_i
                    diag = jd >= 0
                    S.pe(lambda e: e.matmul(
                        banks[sbk][:, c0:512], lhsT=ka[0:65, hl, kt * 128:(kt + 1) * 128],
                        rhs=qa[0:65, hl, qs + c0:qs + 512], start=True, stop=(not diag)),
                        r=[R_kab[kt // 4], R_qab[qb]], w=[R_bank[sbk]])
                    if diag:
                        S.pe(lambda e: e.matmul(banks[sbk][:, c0:c0 + 128], lhsT=ident_b, rhs=trimask_b, start=False, stop=True),
                             r=[R_cb], w=[R_bank[sbk]])
                    S.act(lambda e: e.activation(
                        out=PT[sb_i][:, c0:512], in_=banks[sbk][:, c0:512], func=AF.Exp, scale=0.125,
                        bias=ncum[:, kt, hd:hd + 1]),
                        r=[R_bank[sbk], R_ncum], w=[R_PT[sb_i]])

                def pv_step(kt):
                    jd = kt - 4 * qb
                    c0 = 128 * jd if jd > 0 else 0
                    sb_i = kt % 3
                    S.pe(lambda e: e.matmul(
                        banks[obk][:, c0:512], lhsT=va[:, kt, hl, :], rhs=PT[sb_i][:, c0:512],
                        start=(kt == 0), stop=(kt == nkt - 1)),
                        r=[R_vab[kt // 4], R_PT[sb_i]], w=[R_bank[obk]])

                LA = 2
                for i in range(nkt + LA):
                    if i < nkt:
                        qk_step(i)
                    if i >= LA:
                        pv_step(i - LA)
                po = slice(hl * 64, (hl + 1) * 64)
                pd = slice((1 - hl) * 64, (2 - hl) * 64)
                S.act(lambda e: e.activation(out=ftmp[hl][po, :], in_=banks[obk][pd, :], func=AF.Ln), r=[R_bank[obk]], w=[R_ftmp[hl]])
                S.act(lambda e: e.activation(out=ftmp[hl][po, :], in_=ftmp[hl][po, :], func=AF.Exp, scale=-1.0), r=[R_ftmp[hl]], w=[R_ftmp[hl]])
                S.dve(lambda e: e.tensor_tensor(
                    out=mixT[po, 4 + hp, qs:qs + 512], in0=banks[obk][po, :], in1=ftmp[hl][po, :], op=ALU.mult),
                    r=[R_bank[obk], R_ftmp[hl]], w=[R_mix[4 + hp][qb]])

            for hp in range(4):
                fox_inproj(hp, 0)
                for qb in range(NB):
                    if qb + 1 < NB:
                        fox_inproj(hp, qb + 1)
                    for hl in range(2):
                        fox_attn(hp, hl, qb)

        if stop_after >= 1:
            out_proj_residual(0, w_out0_d)
        if stop_after >= 2:
            mlp(0)

        if stop_after >= 3:
            car, j = new_phase()
            qT2 = car.take([2, T], BF16)
            kT2 = car.take([T], BF16)
            va = car.take([NT, 2, 128], BF16)
            cstA = car.take([8, 128], BF16)
            cst0 = car.take([8, 128], BF16)
            PT5 = [car.take([640], BF16) for _ in range(2)]
            R_cst = ares()
            R_PT5 = ares(2)
            seed([R_cst] + R_PT5, j)
            assert car.off <= ARENA_BYTES - TAIL, car.off
            for n in range(NB):
                prenorm_block(1, 0, n, 4 + (n % 2), use_pool=True)
            bd, cw, cbias, lba, lbx, lam, c8, c16, c8h, lbah, lbxh, LN_HALF, rel34, cvec, tail_res = [TL[k] for k in (
                "bd", "cw", "cbias", "lba", "lbx", "lam", "c8", "c16", "c8h", "lbah", "lbxh", "LN_HALF", "rel34", "cvec", "tail_res")]
            R_bd, R_cw, R_cbias, R_lba, R_lbx, R_lam, R_c8, R_rel34, R_cvec = tail_res
            S.dve(lambda e: e.memset(rel34[64:128, :, 1, 0:64], NEG), r=[R_rel34], w=[R_rel34])
            S.act(lambda e: e.activation(out=c8, in_=lam, func=AF.Exp, scale=-1.0), r=[R_lam], w=[R_c8])
            S.act(lambda e: e.activation(out=c8, in_=c8, func=AF.Ln, bias=1.0), r=[R_c8], w=[R_c8])
            S.dve(lambda e: e.tensor_scalar(out=c16, in0=c8, scalar1=-16.0, scalar2=None, op0=ALU.mult), r=[R_c8], w=[R_c8])
            S.dve(lambda e: e.tensor_scalar(out=c8h, in0=c8, scalar1=-4.0, scalar2=None, op0=ALU.mult), r=[R_c8], w=[R_c8])
            S.dve(lambda e: e.tensor_scalar(out=c8, in0=c8, scalar1=-8.0, scalar2=None, op0=ALU.mult), r=[R_c8], w=[R_c8])
            S.dve(lambda e: e.memset(LN_HALF, -0.6931471805599453), w=[R_c8])
            S.dve(lambda e: e.tensor_scalar(out=lbah, in0=lba, scalar1=0.5, scalar2=None, op0=ALU.mult), r=[R_lba], w=[R_lba])
            S.dve(lambda e: e.tensor_scalar(out=lbxh, in0=lbx, scalar1=0.5, scalar2=None, op0=ALU.mult), r=[R_lbx], w=[R_lbx])
            S.dve(lambda e: e.memset(cst0[:, 0, :], 0.0), w=[R_cst])
            S.dve(lambda e: e.memset(cst0[0:64, 0, 64:128], NEG), r=[R_cst], w=[R_cst])
            for hd in range(8):
                S.dve(lambda e, hd=hd: e.tensor_scalar(out=rel34[:, hd, 0, :], in0=rel34[:, hd, 0, :], scalar1=cvec[:, hd:hd + 1],
                                                       scalar2=None, op0=ALU.subtract),
                      r=[R_rel34, R_cvec], w=[R_rel34])
            R_q2b, R_k2b, R_v2b = ares(NB), ares(NB), ares(NB)
            seed(R_q2b + R_k2b + R_v2b, j)
            S.add("pool", lambda e: e.memset(va[:, :, 0, 64:128], 1.0), (), R_v2b)
            S.add("pool", lambda e: e.memset(va[:, :, 1, 0:64], 1.0), (), R_v2b)
            S.add("pool", lambda e: e.memset(qT2[:, :, :], 0.0), (), R_q2b)
            slot_q = wload(w_in1_d[:, 0:512], 8, 512)
            slot_k = wload(w_in1_d[:, 512:1024], 8, 512)
            slot_v = wload(w_in1_d[:, 1024:1536], 8, 512)

            def ca_inproj(hp, n):
                wc = slice(hp * 128, (hp + 1) * 128)
                blk = slice(n * 512, (n + 1) * 512)
                for c in range(8):
                    S.pe(lambda e, c=c: e.matmul(banks[0][:, :], lhsT=wbuf[slot_q][:, c, wc], rhs=hT[:, c, blk], start=(c == 0), stop=(c == 7)),
                         r=[R_w[slot_q], R_hT[c][n]], w=[R_bank[0]])
                S.dve(lambda e: e.tensor_scalar(out=qT2[0:64, 0, blk], in0=banks[0][0:64, :], scalar1=0.125, scalar2=None, op0=ALU.mult),
                      r=[R_bank[0]], w=[R_q2b[n]])
                S.dve(lambda e: e.tensor_scalar(out=qT2[64:128, 1, blk], in0=banks[0][64:128, :], scalar1=0.125, scalar2=None, op0=ALU.mult),
                      r=[R_bank[0]], w=[R_q2b[n]])
                for c in range(8):
                    S.pe(lambda e, c=c: e.matmul(banks[1][:, :], lhsT=wbuf[slot_k][:, c, wc], rhs=hT[:, c, blk], start=(c == 0), stop=(c == 7)),
                         r=[R_w[slot_k], R_hT[c][n]], w=[R_bank[1]])
                S.dve(lambda e: e.tensor_copy(out=kT2[:, blk], in_=banks[1][:, :]), r=[R_bank[1]], w=[R_k2b[n]])
                for t4 in range(4):
                    tt = 4 * n + t4
                    tok = slice(tt * 128, (tt + 1) * 128)
                    for c in range(8):
                        S.pe(lambda e, c=c, t4=t4: e.matmul(banks[0][:, t4 * 128:(t4 + 1) * 128], lhsT=hT[:, c, tok], rhs=wbuf[slot_v][:, c, wc],
                                                            start=(c == 0), stop=(c == 7)),
                             r=[R_w[slot_v], R_hT[c][n]], w=[R_bank[0]])
                S.dve(lambda e: e.tensor_copy(
                    out=va[:, 4 * n:4 * n + 4, 0, 0:64],
                    in_=banks[0][:, :].rearrange("p (a b c) -> p a b c", a=4, b=2)[:, :, 0, :]),
                    r=[R_bank[0]], w=[R_v2b[n]])
                S.dve(lambda e: e.tensor_copy(
                    out=va[:, 4 * n:4 * n + 4, 1, 64:128],
                    in_=banks[0][:, :].rearrange("p (a b c) -> p a b c", a=4, b=2)[:, :, 1, :]),
                    r=[R_bank[0]], w=[R_v2b[n]])

            for hp in range(4):
                steps = [(hl, jq) for jq in range(NT) for hl in range(2)]

                def ca_qk(it):
                    hl, jq = steps[it]
                    hd = 2 * hp + hl
                    qsl = slice(jq * 128, (jq + 1) * 128)
                    par = it % 2
                    bA = 3 if par == 0 else 5
                    bB = 4 if par == 0 else 6
                    idxs = [i for i in range(5) if jq - 4 + i >= 0]
                    for idx in idxs:
                        kt = jq - 4 + idx
                        bk, col = (bA, idx * 128) if idx < 4 else (bB, 0)
                        nob = idx in (1, 2)
                        S.pe(lambda e, bk=bk, col=col, kt=kt, nob=nob: e.matmul(
                            banks[bk][:, col:col + 128], lhsT=kT2[:, kt * 128:(kt + 1) * 128], rhs=qT2[:, hl, qsl],
                            start=True, stop=nob),
                            r=[R_k2b[kt // 4], R_q2b[jq // 4]], w=[R_bank[bk]])
                        if not nob:
                            brhs = (cst0[:, 0, :], None, None, rel34[:, hd, 0, :], rel34[:, hd, 1, :])[idx]
                            S.pe(lambda e, bk=bk, col=col, brhs=brhs: e.matmul(
                                banks[bk][:, col:col + 128], lhsT=ident_b, rhs=brhs, start=False, stop=True),
                                r=[R_cb, R_cst, R_rel34], w=[R_bank[bk]])
                    i0 = idxs[0]
                    if i0 < 4:
                        S.act(lambda e: e.activation(out=PT5[par][:, i0 * 128:512], in_=banks[bA][:, i0 * 128:512], func=AF.Exp,
                                                     bias=cvec[:, hd:hd + 1]),
                              r=[R_bank[bA], R_cvec], w=[R_PT5[par]])
                    S.act(lambda e: e.activation(out=PT5[par][:, 512:640], in_=banks[bB][:, 0:128], func=AF.Exp),
                          r=[R_bank[bB]], w=[R_PT5[par]])

                def ca_pv(it):
                    hl, jq = steps[it]
                    par = it % 2
                    jq4 = jq % 4
                    obk = 7 if hl == 0 else 2
                    idxs = [i for i in range(5) if jq - 4 + i >= 0]
                    for ii, idx in enumerate(idxs):
                        kt = jq - 4 + idx
                        S.pe(lambda e, kt=kt, idx=idx, ii=ii, last=(idx == 4): e.matmul(
                            banks[obk][:, jq4 * 128:(jq4 + 1) * 128], lhsT=va[:, kt, hl, :], rhs=PT5[par][:, idx * 128:(idx + 1) * 128],
                            start=(ii == 0), stop=last),
                            r=[R_v2b[kt // 4], R_PT5[par]], w=[R_bank[obk]])
                    if jq4 == 3:
                        nq = jq // 4
                        po = slice(hl * 64, (hl + 1) * 64)
                        pd = slice((1 - hl) * 64, (2 - hl) * 64)
                        S.act(lambda e: e.activation(out=ftmp[hl][po, :], in_=banks[obk][pd, :], func=AF.Ln), r=[R_bank[obk]], w=[R_ftmp[hl]])
                        S.act(lambda e: e.activation(out=ftmp[hl][po, :], in_=ftmp[hl][po, :], func=AF.Exp, scale=-1.0), r=[R_ftmp[hl]], w=[R_ftmp[hl]])
                        S.dve(lambda e: e.tensor_tensor(
                            out=mixT[po, hp, nq * 512:(nq + 1) * 512], in0=banks[obk][po, :], in1=ftmp[hl][po, :], op=ALU.mult),
                            r=[R_bank[obk], R_ftmp[hl]], w=[R_mix[hp][nq]])

                ca_inproj(hp, 0)
                ca_qk(0)
                for it in range(len(steps)):
                    if it % 8 == 0 and it // 8 + 1 < NB:
                        ca_inproj(hp, it // 8 + 1)
                    if it + 1 < len(steps):
                        ca_qk(it + 1)
                    ca_pv(it)

            car, j = new_phase()
            arena_res.extend(tail_res)
            XH = [car.take([3 + 512], F32) for _ in range(3)]
            XC_ = [car.take([512], F32) for _ in range(2)]
            XCb_ = [car.take([512], BF16) for _ in range(2)]
            RR_ = [car.take([512], F32) for _ in range(2)]
            II_ = [car.take([512], F32) for _ in range(2)]
            AA_ = [car.take([512], F32) for _ in range(2)]
            MM_ = [car.take([512], F32) for _ in range(2)]
            HH = [car.take([512], F32) for _ in range(2)]
            R_XC_, R_XCb_, R_RR_, R_II_, R_AA_, R_MM_ = ares(2), ares(2), ares(2), ares(2), ares(2), ares(2)
            R_XH = ares(3); R_HH = ares(2)
            seed(R_XC_ + R_XCb_ + R_RR_ + R_II_ + R_AA_ + R_MM_ + R_XH + R_HH, j)
            assert car.off <= ARENA_BYTES - TAIL, car.off
            slot_g = wload(w_in1_d[:, 1536:2048], 8, 512)
            slot_x = wload(w_in1_d[:, 2048:2560], 8, 512)
            lsteps = [(c, n) for c in range(4) for n in range(NB)]

            def lru_sets(it):
                s = it % 2
                return (s, XC_[s], XCb_[s], RR_[s], II_[s], AA_[s], MM_[s],
                        R_XC_[s], R_XCb_[s], R_RR_[s], R_II_[s], R_AA_[s], R_MM_[s],
                        (0, 1, 2, 3) if s == 0 else (4, 5, 6, 7))

            def lru_front_a(it):
                c, n = lsteps[it]
                blk = slice(n * 512, (n + 1) * 512)
                s3 = it % 3
                p3 = (it - 1) % 3
                b0, b1 = (0, 1) if it % 2 == 0 else (4, 5)
                for kc in range(8):
                    S.pe(lambda e, kc=kc: e.matmul(banks[b0][:, :], lhsT=wbuf[slot_g][:, kc, c * 128:(c + 1) * 128], rhs=hT[:, kc, blk],
                                                   start=(kc == 0), stop=(kc == 7)),
                         r=[R_w[slot_g], R_hT[kc][n]], w=[R_bank[b0]])
                for kc in range(8):
                    S.pe(lambda e, kc=kc: e.matmul(banks[b1][:, :], lhsT=wbuf[slot_x][:, kc, c * 128:(c + 1) * 128], rhs=hT[:, kc, blk],
                                                   start=(kc == 0), stop=(kc == 7)),
                         r=[R_w[slot_x], R_hT[kc][n]], w=[R_bank[b1]])
                S.act(lambda e: e.activation(out=mixT[:, 4 + c, blk], in_=banks[b0][:, :], func=AF.Gelu_apprx_tanh),
                      r=[R_bank[b0]], w=[R_mix[4 + c][n]])
                if n == 0:
                    S.dve(lambda e: e.memset(XH[s3][:, 0:3], 0.0), w=[R_XH[s3]])
                else:
                    S.dve(lambda e: e.tensor_copy(out=XH[s3][:, 0:3], in_=XH[p3][:, 512:515]), r=[R_XH[p3]], w=[R_XH[s3]])
                S.act(lambda e: e.activation(out=XH[s3][:, 3:515], in_=banks[b1][:, :], func=AF.Copy), r=[R_bank[b1]], w=[R_XH[s3]])

            def lru_front_b(it):
                c, n = lsteps[it]
                s3 = it % 3
                s, XC, XCb, RR, II, AA, MM, R_XC, R_XCb, R_RR, R_II, R_AA, R_MM, (b0, b1, b2, b3) = lru_sets(it)
                S.dve(lambda e: e.tensor_scalar(out=XC, in0=XH[s3][:, 3:515], scalar1=cw[:, c * 4 + 3:c * 4 + 4],
                                                scalar2=cbias[:, c:c + 1], op0=ALU.mult, op1=ALU.add),
                      r=[R_XH[s3], R_cw, R_cbias], w=[R_XC])
                for jt in range(3):
                    S.dve(lambda e, jt=jt: e.scalar_tensor_tensor(out=XC, in0=XH[s3][:, jt:jt + 512], scalar=cw[:, c * 4 + jt:c * 4 + jt + 1],
                                                                  in1=XC, op0=ALU.mult, op1=ALU.add),
                          r=[R_XH[s3], R_cw, R_XC], w=[R_XC])
                S.dve(lambda e: e.tensor_copy(out=XCb, in_=XC), r=[R_XC], w=[R_XCb])
                S.pe(lambda e: e.matmul(banks[b2][:, :], lhsT=bd[:, 0, c, :], rhs=XCb, start=True, stop=True), r=[R_bd, R_XCb], w=[R_bank[b2]])
                S.pe(lambda e: e.matmul(banks[b3][:, :], lhsT=bd[:, 1, c, :], rhs=XCb, start=True, stop=True), r=[R_bd, R_XCb], w=[R_bank[b3]])

            def lru_back(it):
                c, n = lsteps[it]
                blk = slice(n * 512, (n + 1) * 512)
                s3 = it % 3
                s, XC, XCb, RR, II, AA, MM, R_XC, R_XCb, R_RR, R_II, R_AA, R_MM, (b0, b1, b2, b3) = lru_sets(it)
                S.act(lambda e: e.activation(out=RR, in_=banks[b2][:, :], func=AF.Tanh, scale=0.5, bias=lbah[:, c:c + 1]), r=[R_bank[b2], R_lba], w=[R_RR])
                S.act(lambda e: e.activation(out=II, in_=banks[b3][:, :], func=AF.Tanh, scale=0.5, bias=lbxh[:, c:c + 1]), r=[R_bank[b3], R_lbx], w=[R_II])
                S.act(lambda e: e.activation(out=AA, in_=RR, func=AF.Exp, scale=c8h[:, c:c + 1], bias=c8h[:, c:c + 1]), r=[R_RR, R_c8], w=[R_AA])
                S.act(lambda e: e.activation(out=MM, in_=RR, func=AF.Exp, scale=c8[:, c:c + 1], bias=c8[:, c:c + 1]), r=[R_RR, R_c8], w=[R_MM])
                S.act(lambda e: e.activation(out=MM, in_=MM, func=AF.Ln, scale=-1.0, bias=1.0), r=[R_MM], w=[R_MM])
                S.act(lambda e: e.activation(out=MM, in_=MM, func=AF.Exp, scale=0.5, bias=LN_HALF[:, :]), r=[R_MM, R_c8], w=[R_MM])
                S.dve(lambda e: e.scalar_tensor_tensor(out=II, in0=II, scalar=1.0, in1=XC, op0=ALU.add, op1=ALU.mult), r=[R_II, R_XC], w=[R_II])
                S.dve(lambda e: e.tensor_tensor(out=MM, in0=MM, in1=II, op=ALU.mult), r=[R_II, R_MM], w=[R_MM])
                init = 0.0 if n == 0 else HH[1 - s][:, 511:512]
                rinit = [] if n == 0 else [R_HH[1 - s]]
                S.dve(lambda e: e.tensor_tensor_scan(out=HH[s], data0=AA, data1=MM, initial=init, op0=ALU.mult, op1=ALU.add),
                      r=[R_AA, R_MM] + rinit, w=[R_HH[s]])
                S.dve(lambda e: e.tensor_tensor(out=mixT[:, 4 + c, blk], in0=HH[s], in1=mixT[:, 4 + c, blk], op=ALU.mult),
                      r=[R_HH[s], R_mix[4 + c][n]], w=[R_mix[4 + c][n]])

            NL = len(lsteps)
            lru_front_a(0)
            lru_front_a(1)
            lru_front_b(0)
            for it in range(NL):
                if it + 2 < NL:
                    lru_front_a(it + 2)
                if it + 1 < NL:
                    lru_front_b(it + 1)
                lru_back(it)

            out_proj_residual(1, w_out1_d)
        if stop_after >= 4:
            mlp(1)

        if abs(stop_after - 0.9) < 1e-6 or abs(stop_after - 2.9) < 1e-6:
            for c in range(8):
                for n in range(NB):
                    blk = slice(n * 512, (n + 1) * 512)
                    S.dve(lambda e, c=c, blk=blk: e.tensor_copy(out=xT[:, c, blk], in_=mixT[:, c, blk]), r=[R_mix[c][n]], w=[R_xT[c][n]])
        car, j = new_phase()
        NYS = 8
        ys = [car.take([D], F32) for _ in range(NYS)]
        R_ys = ares(NYS)
        seed(R_ys, j)
        for tt in range(NT):
            s = tt % NYS
            n = tt // 4
            for half in range(2):
                bk = 2 * (tt % 2) + half
                for c4 in range(4):
                    c = half * 4 + c4
                    S.pe(lambda e, bk=bk, c4=c4, c=c, tt=tt: e.transpose(
                        out=banks[bk][:, c4 * 128:(c4 + 1) * 128], in_=xT[:, c, tt * 128:(tt + 1) * 128], identity=ident_f),
                        r=[R_xT[c][n], R_cf], w=[R_bank[bk]])
                if half == 0:
                    S.act(lambda e, bk=bk, s=s: e.activation(out=ys[s][:, 0:512], in_=banks[bk][:, :], func=AF.Copy),
                          r=[R_bank[bk]], w=[R_ys[s]])
                else:
                    S.dve(lambda e, bk=bk, s=s: e.tensor_copy(out=ys[s][:, 512:1024], in_=banks[bk][:, :]),
                          r=[R_bank[bk]], w=[R_ys[s]])
            S.dma(lambda e, s=s, tt=tt: e.dma_start(out=out_d[tt * 128:(tt + 1) * 128, :], in_=ys[s]), r=[R_ys[s]])

        for op in conv_ops[n_early_conv:]:
            op.deps.extend(early_sync)
        npre = 1
        S.q["pool"] = S.q["pool"][:npre] + conv_ops + S.q["pool"][npre:]
        S.emit()
    return nc


def _consts():
    c = np.zeros((128, 520), np.float32)
    c[:, 0:128] = np.eye(128, dtype=np.float32)
    c[:, 128:256] = 1.0
    s = np.arange(128)[:, None]
    t = np.arange(128)[None, :]
    c[:, 256:384] = np.where((s // 64 == t // 64) & (s > t), -1.0 / 16.0, 0.0)
    c[:, 384:512] = np.where(s > t, NEG, 0.0)
    c[:, 512:514] = np.where(s // 64 == np.arange(2)[None, :], -1.0 / 16.0, 0.0)
    return c


def _pc(v, nchunk):
    return np.ascontiguousarray(np.asarray(v, np.float32).reshape(nchunk, 128).T)


def _layout_inputs(inp):
    f = lambda a: np.ascontiguousarray(np.asarray(a, np.float32))
    nw = f(inp["norm_w"]).reshape(2, 4, 8, 128).transpose(3, 0, 1, 2).reshape(128, 64)
    k = np.arange(128)[:, None, None]
    idx = np.arange(5)[None, :, None]
    q = np.arange(128)[None, None, :]
    d = 512 - 128 * idx + q - k
    gidx = np.clip(d, -128, 128) + 128
    rb = f(inp["rel_bias"])[0]
    relT = rb[:, gidx]
    rel34 = np.ascontiguousarray(relT[:, :, 3:5, :].transpose(1, 0, 2, 3)).reshape(128, 8 * 2 * 128)
    cvec = np.ascontiguousarray(np.broadcast_to(rb[:, 256][None, :], (128, 8)))
    cw = f(inp["conv_w"])[0]
    cwl = np.ascontiguousarray(cw.reshape(4, 4, 128).transpose(2, 1, 0)).reshape(128, 16)
    shared = {
        "nw": np.ascontiguousarray(nw),
        "consts": _consts(),
        "w_in0": f(inp["w_in_even"])[0],
        "wa_aug": np.ascontiguousarray(np.concatenate([f(inp["gla_w_a_up"])[0], f(inp["gla_b_a"])[0][None, :]], axis=0)),
        "gnw": _pc(f(inp["gla_norm_w"])[0], 4),
        "fbf": f(inp["fox_b_f"])[0].reshape(8, 1),
        "w_out0": f(inp["w_out_even"])[0],
        "w_in1": f(inp["w_in_odd"])[0],
        "rel34": rel34,
        "cvec": cvec,
        "cw": cwl,
        "cb": _pc(f(inp["conv_b"])[0], 4),
        "lwa": f(inp["lru_w_a"])[0],
        "lwx": f(inp["lru_w_x"])[0],
        "lba": _pc(f(inp["lru_b_a"])[0], 4),
        "lbx": _pc(f(inp["lru_b_x"])[0], 4),
        "lam": _pc(f(inp["lru_lambda"])[0], 4),
        "w_out1": f(inp["w_out_odd"])[0],
        "w_up": f(inp["w_mlp_up"]),
        "w_dn": f(inp["w_mlp_down"]),
    }
    x = f(inp["x"])
    return [dict(shared, x=np.ascontiguousarray(x[b])) for b in range(x.shape[0])]


_NC_CACHE = {}


def kernel(**inputs):
    stop_after = float(inputs.pop("_stop_after", 99))
    ncores = int(inputs.pop("_ncores", 8))
    if stop_after not in _NC_CACHE:
        _NC_CACHE[stop_after] = build_program(stop_after)
    nc = _NC_CACHE[stop_after]
    in_maps = _layout_inputs(inputs)[:ncores]
    res = run_bass_kernel_spmd(nc, in_maps, core_ids=list(range(ncores)))
    return np.stack([np.asarray(r["out"], np.float32) for r in res.results], axis=0)
```
